# Optimizing a Trainium2 kernel written in Bass

```python
import jax, jax.numpy as jnp
from jax import lax
import numpy as np

D_MODEL = 1024
BATCH = 8
SEQ = 4096
DEPTH = 2

CHUNK = 64
N_EVEN = (DEPTH + 1) // 2
N_ODD = DEPTH // 2

A_HEADS = 4
A_DK = 128
A_DV = 128
A_WIDTH = A_HEADS * A_DV
B_HEADS = 8
B_HD = 64
B_WIDTH = B_HEADS * B_HD
IDX_HEADS = 8
IDX_DIM = 64
TOPK_MAX = 256
Q_BLOCK = 128
EVEN_SIZES = (A_HEADS * A_DK,
              A_HEADS * A_DK,
              A_WIDTH,
              A_WIDTH,
              B_WIDTH,
              B_HD,
              B_HD,
              IDX_HEADS * IDX_DIM,
              IDX_DIM,
              IDX_HEADS)
EVEN_IN = sum(EVEN_SIZES)
EVEN_MIX = A_WIDTH + B_WIDTH
LRU_WIDTH = 1280
LRU_BLOCKS = 10
LRU_BW = LRU_WIDTH // LRU_BLOCKS
LRU_CONV = 4
LRU_C = 8.0
D_FF = 3072
FFN_CONV = 3
EPS = 1e-6

kernel_name = "hgrn2_dsa_rglru_convffn_hybrid"


def rms_norm(x, gain):
    xf = x.astype(jnp.float32)
    y = xf * lax.rsqrt(jnp.mean(xf * xf, axis=-1, keepdims=True) + EPS)
    return (y * gain.astype(jnp.float32)).astype(x.dtype)


def causal_dwconv(x, w, b):
    width, ch = w.shape
    y = lax.conv_general_dilated(x, w[:, None, :].astype(x.dtype), window_strides=(1,),
                                 padding=[(width - 1, 0)],
                                 dimension_numbers=('NWC', 'WIO', 'NWC'),
                                 feature_group_count=ch)
    return y + b.astype(x.dtype)


def split_cols(z, sizes):
    idx = [int(v) for v in np.cumsum(sizes)[:-1]]
    return jnp.split(z, idx, axis=-1)


def hgrn_lower_bound(lb_logits, layer):
    p = jax.nn.softmax(lb_logits.astype(jnp.float32), axis=0)
    return jnp.cumsum(p, axis=0)[layer]


def hgrn2(q, f_raw, i, lb):
    bn, s, _ = q.shape
    nc = s // CHUNK

    def to_chunks(t, d):
        return t.astype(jnp.float32).reshape(bn, nc, CHUNK, A_HEADS, d).transpose(1, 0, 3, 2, 4)

    f = lb + (1.0 - lb) * jax.nn.sigmoid(f_raw.astype(jnp.float32))
    qc = to_chunks(jax.nn.silu(q.astype(jnp.float32)), A_DK)
    gc = to_chunks(jnp.log(f), A_DK)
    kc = to_chunks(1.0 - f, A_DK)
    vc = to_chunks(i, A_DV)
    causal = jnp.tril(jnp.ones((CHUNK, CHUNK), dtype=bool))[:, :, None]

    def step(state, inp):
        qb, kb, vb, gb = inp
        b = jnp.cumsum(gb, axis=2)
        inter = jnp.einsum('bhtk,bhkv->bhtv', qb * jnp.exp(b), state)
        rel = b[:, :, :, None, :] - b[:, :, None, :, :]
        decay = jnp.where(causal, jnp.exp(jnp.where(causal, rel, 0.0)), 0.0)
        scores = jnp.einsum('bhtk,bhtsk,bhsk->bhts', qb, decay, kb)
        intra = jnp.einsum('bhts,bhsv->bhtv', scores, vb)
        b_last = b[:, :, -1:, :]
        new_state = (jnp.exp(b_last[:, :, 0, :])[..., None] * state
                     + jnp.einsum('bhsk,bhsv->bhkv', kb * jnp.exp(b_last - b), vb))
        return new_state, inter + intra

    s0 = jnp.zeros((bn, A_HEADS, A_DK, A_DV), jnp.float32)
    _, o = lax.scan(step, s0, (qc, kc, vc, gc))
    return o.transpose(1, 0, 3, 2, 4).reshape(bn, s, A_HEADS, A_DV)


def dsa_attention(q, k, v, qi, ki, wi, q_gain, k_gain):
    bn, s = q.shape[:2]
    topk = min(TOPK_MAX, s // 4)
    q = rms_norm(q, q_gain)
    k = rms_norm(k, k_gain)
    key_chunk = jnp.arange(s) // CHUNK
    nb = s // Q_BLOCK
    scale = B_HD ** -0.5
    idx_scale = IDX_DIM ** -0.5
    kif = ki.astype(jnp.float32)

    def to_blocks(t):
        return jnp.moveaxis(t.reshape(bn, nb, Q_BLOCK, *t.shape[2:]), 1, 0)

    gather = jax.vmap(lambda tb, ib: tb[ib])

    def one_block(args):
        qb, qib, wib, start = args
        q_chunk = (start + jnp.arange(Q_BLOCK)) // CHUNK
        admissible = key_chunk[None, :] <= q_chunk[:, None]
        dots = jnp.einsum('bqhd,bsd->bqhs', qib.astype(jnp.float32), kif) * idx_scale
        iscore = jnp.einsum('bqh,bqhs->bqs', wib.astype(jnp.float32), jax.nn.relu(dots))
        iscore = jnp.where(admissible[None], iscore, -jnp.inf)
        _, sel = lax.top_k(iscore, topk)
        valid = key_chunk[sel] <= q_chunk[None, :, None]
        kg = gather(k, sel).astype(jnp.float32)
        vg = gather(v, sel).astype(jnp.float32)
        logits = jnp.einsum('bqhd,bqkd->bqhk', qb.astype(jnp.float32), kg) * scale
        logits = jnp.where(valid[:, :, None, :], logits, -jnp.inf)
        p = jax.nn.softmax(logits, axis=-1)
        return jnp.einsum('bqhk,bqkd->bqhd', p, vg).astype(v.dtype)

    starts = jnp.arange(nb) * Q_BLOCK
    o = lax.map(one_block, (to_blocks(q), to_blocks(qi), to_blocks(wi), starts))
    return jnp.moveaxis(o, 0, 1).reshape(bn, s, B_HEADS * B_HD)


def hgrn_dsa_mixer(h, w_in, w_out, lb, a_norm, q_gain, k_gain):
    bn, s, _ = h.shape
    (a_q, a_f, a_i, a_g, b_q, b_k, b_v, ix_q, ix_k, ix_w) = split_cols(h @ w_in, EVEN_SIZES)
    a_o = hgrn2(a_q, a_f, a_i, lb)
    a_o = rms_norm(a_o, a_norm.reshape(A_HEADS, A_DV)).reshape(bn, s, A_WIDTH).astype(h.dtype)
    a_o = a_o * jax.nn.silu(a_g)
    b_o = dsa_attention(b_q.reshape(bn, s, B_HEADS, B_HD), b_k, b_v,
                        ix_q.reshape(bn, s, IDX_HEADS, IDX_DIM), ix_k,
                        ix_w * (IDX_HEADS ** -0.5), q_gain, k_gain)
    return jnp.concatenate([a_o, b_o], axis=-1) @ w_out


def rglru_mixer(h, w_in, conv_w, conv_b, wa, ba, wx, bx, lam, w_out):
    bn, s, _ = h.shape
    y_br, x_br = jnp.split(h @ w_in, 2, axis=-1)
    y_br = jax.nn.gelu(y_br, approximate=True)
    xc = causal_dwconv(x_br, conv_w, conv_b)
    xb = xc.reshape(bn, s, LRU_BLOCKS, LRU_BW).astype(jnp.float32)
    r = jax.nn.sigmoid(jnp.einsum('bsnc,ncd->bsnd', xb, wa.astype(jnp.float32)).reshape(bn, s, LRU_WIDTH)
                       + ba.astype(jnp.float32))
    gi = jax.nn.sigmoid(jnp.einsum('bsnc,ncd->bsnd', xb, wx.astype(jnp.float32)).reshape(bn, s, LRU_WIDTH)
                        + bx.astype(jnp.float32))
    log_a = LRU_C * r * jax.nn.log_sigmoid(lam.astype(jnp.float32))
    a = jnp.exp(log_a)
    u = jnp.sqrt(-jnp.expm1(2.0 * log_a)) * (gi * xc.astype(jnp.float32))

    def combine(left, right):
        a1, b1 = left
        a2, b2 = right
        return a1 * a2, a2 * b1 + b2

    _, hs = lax.associative_scan(combine, (a, u), axis=1)
    return (hs.astype(h.dtype) * y_br) @ w_out


def conv_ffn(h, w_up, conv_w, conv_b, w_down):
    u = causal_dwconv(h @ w_up, conv_w, conv_b)
    gate, val = jnp.split(u, 2, axis=-1)
    return (jax.nn.gelu(gate, approximate=True) * val) @ w_down


def setup_inputs(seed: int = 0) -> dict:
    key = jax.random.key(seed)
    ks = iter(jax.random.split(key, 32))

    def nrm(shape, scale):
        return jax.random.normal(next(ks), shape, jnp.float32) * scale

    def gain(shape):
        return 1.0 + nrm(shape, 0.01)

    a8 = jax.random.uniform(next(ks), (N_ODD, LRU_WIDTH), jnp.float32, 0.81, 0.998)
    p = a8 ** (1.0 / LRU_C)
    lam = jnp.log(p) - jnp.log1p(-p)
    return {
        "x": nrm((BATCH, SEQ, D_MODEL), 1.0),
        "lb_logits": nrm((DEPTH + 1, A_HEADS * A_DK), 0.1),
        "even_norm": gain((N_EVEN, D_MODEL)),
        "even_w_in": nrm((N_EVEN, D_MODEL, EVEN_IN), D_MODEL ** -0.5),
        "even_w_out": nrm((N_EVEN, EVEN_MIX, D_MODEL), EVEN_MIX ** -0.5),
        "a_out_norm": gain((N_EVEN, A_WIDTH)),
        "b_q_norm": gain((N_EVEN, B_HD)),
        "b_k_norm": gain((N_EVEN, B_HD)),
        "odd_norm": gain((N_ODD, D_MODEL)),
        "odd_w_in": nrm((N_ODD, D_MODEL, 2 * LRU_WIDTH), D_MODEL ** -0.5),
        "odd_conv_w": nrm((N_ODD, LRU_CONV, LRU_WIDTH), LRU_CONV ** -0.5),
        "odd_conv_b": nrm((N_ODD, LRU_WIDTH), 0.01),
        "odd_gate_a_w": nrm((N_ODD, LRU_BLOCKS, LRU_BW, LRU_BW), LRU_BW ** -0.5),
        "odd_gate_a_b": nrm((N_ODD, LRU_WIDTH), 0.01),
        "odd_gate_x_w": nrm((N_ODD, LRU_BLOCKS, LRU_BW, LRU_BW), LRU_BW ** -0.5),
        "odd_gate_x_b": nrm((N_ODD, LRU_WIDTH), 0.01),
        "odd_lambda": lam,
        "odd_w_out": nrm((N_ODD, LRU_WIDTH, D_MODEL), LRU_WIDTH ** -0.5),
        "ffn_norm": gain((DEPTH, D_MODEL)),
        "ffn_w_up": nrm((DEPTH, D_MODEL, 2 * D_FF), D_MODEL ** -0.5),
        "ffn_conv_w": nrm((DEPTH, FFN_CONV, 2 * D_FF), FFN_CONV ** -0.5),
        "ffn_conv_b": nrm((DEPTH, 2 * D_FF), 0.01),
        "ffn_w_down": nrm((DEPTH, D_FF, D_MODEL), D_FF ** -0.5),
    }


def reference(x, lb_logits, even_norm, even_w_in, even_w_out, a_out_norm, b_q_norm, b_k_norm,
              odd_norm, odd_w_in, odd_conv_w, odd_conv_b, odd_gate_a_w, odd_gate_a_b,
              odd_gate_x_w, odd_gate_x_b, odd_lambda, odd_w_out,
              ffn_norm, ffn_w_up, ffn_conv_w, ffn_conv_b, ffn_w_down):
    h = x
    for layer in range(DEPTH):
        j = layer // 2
        if layer % 2 == 0:
            lb = hgrn_lower_bound(lb_logits, layer)
            h = h + hgrn_dsa_mixer(rms_norm(h, even_norm[j]), even_w_in[j], even_w_out[j], lb,
                                   a_out_norm[j], b_q_norm[j], b_k_norm[j])
        else:
            h = h + rglru_mixer(rms_norm(h, odd_norm[j]), odd_w_in[j], odd_conv_w[j], odd_conv_b[j],
                                odd_gate_a_w[j], odd_gate_a_b[j], odd_gate_x_w[j], odd_gate_x_b[j],
                                odd_lambda[j], odd_w_out[j])
        h = h + conv_ffn(rms_norm(h, ffn_norm[layer]), ffn_w_up[layer], ffn_conv_w[layer],
                         ffn_conv_b[layer], ffn_w_down[layer])
    return h
```

```python
import contextlib
import numpy as np
import ml_dtypes
import concourse.bass as bass
import concourse.mybir as mybir
from concourse.bass_utils import run_bass_kernel_spmd

F32 = mybir.dt.float32
BF16 = mybir.dt.bfloat16
ALU = mybir.AluOpType
AF = mybir.ActivationFunctionType
AX = mybir.AxisListType

S = 4096
D = 1024
DFF = 3072
TT = 512
NT = S // TT
EPS = 1e-6
EVEN_IN = 3272
LRU_W = 1280

ENGS = ("pe", "act", "dve", "pool", "sp")


class Buf:
    __slots__ = ("name", "w", "r")

    def __init__(self, name=""):
        self.name = name
        self.w = None
        self.r = []


class Prog:
    def __init__(self, nc):
        self.nc = nc
        self.q = {e: [] for e in ENGS}
        self.cnt = {e: 0 for e in ENGS}
        self.known = {e: {} for e in ENGS}
        self.dma_sems = []
        self.sem_handles = {}

    def _deps(self, eng, reads, writes):
        best = {}
        for b in reads:
            if b.w is not None:
                k, v = b.w
                if v > best.get(k, 0):
                    best[k] = v
        for b in writes:
            if b.w is not None:
                k, v = b.w
                if v > best.get(k, 0):
                    best[k] = v
            for (k, v) in b.r:
                if v > best.get(k, 0):
                    best[k] = v
        waits = []
        kn = self.known[eng]
        for k, v in best.items():
            if k == "pe" and eng == "pe":
                continue
            if kn.get(k, 0) >= v:
                continue
            kn[k] = v
            waits.append((k, v))
        return waits

    def _commit(self, tok, reads, writes):
        for b in reads:
            if len(b.r) > 24:
                m = {}
                for (k, v) in b.r:
                    if v > m.get(k, 0):
                        m[k] = v
                b.r = list(m.items())
            b.r.append(tok)
        for b in writes:
            b.w = tok
            b.r = []

    def op(self, eng, fn, reads=(), writes=()):
        waits = self._deps(eng, reads, writes)
        self.cnt[eng] += 1
        tok = (eng, self.cnt[eng])
        self.q[eng].append((fn, waits, (eng, 1)))
        self._commit(tok, reads, writes)
        return tok

    def new_dma_sem(self):
        key = "d%d" % (len(self.dma_sems) + 1)
        self.dma_sems.append(key)
        self.cnt[key] = 0
        return key

    def dma(self, eng, fn, sem, reads=(), writes=()):
        waits = self._deps(eng, reads, writes)
        self.cnt[sem] += 16
        tok = (sem, self.cnt[sem])
        self.q[eng].append((fn, waits, (sem, 16)))
        self._commit(tok, reads, writes)
        return tok

    def barrier(self, skip=()):
        for e in ENGS:
            waits = []
            kn = self.known[e]
            for k in list(ENGS) + self.dma_sems:
                if k in skip:
                    continue
                v = self.cnt[k]
                if v > kn.get(k, 0):
                    kn[k] = v
                    waits.append((k, v))
            self.q[e].append((None, waits, None))

    def emit(self):
        nc = self.nc
        with contextlib.ExitStack() as st:
            for k in list(ENGS) + self.dma_sems:
                self.sem_handles[k] = st.enter_context(nc.semaphore("s_" + k))
            block = st.enter_context(nc.Block())
            sh = self.sem_handles

            def run(e, eobj):
                for (fn, waits, inc) in self.q[e]:
                    for (k, v) in waits:
                        eobj.wait_ge(sh[k], v)
                    if fn is not None:
                        fn(eobj).then_inc(sh[inc[0]], inc[1])

            @block.tensor
            def _(eobj):
                run("pe", eobj)

            @block.scalar
            def _(eobj):
                run("act", eobj)

            @block.vector
            def _(eobj):
                run("dve", eobj)

            @block.gpsimd
            def _(eobj):
                run("pool", eobj)

            @block.sync
            def _(eobj):
                run("sp", eobj)


def _dsize(dt):
    return 4 if dt == F32 else 2


class Arena:
    def __init__(self, nc, nbytes):
        self.t = nc.alloc_sbuf_tensor("arena", [128, nbytes // 4], F32)
        self.nwords = nbytes // 4
        self.off = 0

    def reset(self, off=0):
        self.off = off

    def alloc(self, free_shape, dt):
        n = 1
        for s in free_shape:
            n *= s
        nw = (n * _dsize(dt) + 3) // 4
        nw = (nw + 7) // 8 * 8
        assert self.off + nw <= self.nwords, ("SBUF arena overflow", self.off, nw, self.nwords)
        ap = self.t[:, self.off:self.off + nw]
        self.off += nw
        if dt != F32:
            ap = ap.bitcast(dt)
        ap = ap[:, 0:n]
        if len(free_shape) == 2:
            ap = ap.rearrange("p (a b) -> p a b", a=free_shape[0])
        elif len(free_shape) == 3:
            ap = ap.rearrange("p (a b c) -> p a b c", a=free_shape[0], b=free_shape[1])
        return ap


class Ctx:
    pass


def vec_rows(C):
    rows = []

    def add(key, ap1d, n):
        rows.append((key, ap1d.rearrange("(c p) -> c p", p=128), n // 128))

    t = C.t
    for l in range(2):
        add(("ffn_norm", l), t["ffn_norm"][l, :], D)
        for j in range(3):
            add(("ffn_cw", l, j), t["ffn_conv_w"][l, j, :], 2 * DFF)
        add(("ffn_cb", l), t["ffn_conv_b"][l, :], 2 * DFF)
    add(("even_norm",), t["even_norm"][0, :], D)
    add(("odd_norm",), t["odd_norm"][0, :], D)
    add(("a_out_norm",), t["a_out_norm"][0, :], 512)
    for j in range(3):
        add(("lb", j), t["lb_logits"][j, :], 512)
    for j in range(4):
        add(("odd_cw", j), t["odd_conv_w"][0, j, :], LRU_W)
    add(("odd_cb",), t["odd_conv_b"][0, :], LRU_W)
    add(("odd_ba",), t["odd_gate_a_b"][0, :], LRU_W)
    add(("odd_bx",), t["odd_gate_x_b"][0, :], LRU_W)
    add(("odd_lam",), t["odd_lambda"][0, :], LRU_W)
    return rows


def load_vectors(C):
    P, nc = C.P, C.nc
    rows = vec_rows(C)
    total = sum(r[2] for r in rows)
    C.vt = C.persist.alloc([total], F32)
    C.b_vt = Buf("vt")
    C.vcol = {}
    stg = C.arena.alloc([128], F32)
    sem = P.new_dma_sem()
    st = {"col": 0, "cur": 0, "bufs": []}
    guard = Buf("stg_guard")

    def flush():
        n = st["cur"]
        if n == 0:
            return
        c0 = st["col"]
        ps = C.psum[0]
        P.op("pe", lambda e: e.transpose(out=ps[:, 0:n], in_=stg[0:n, :], identity=C.ident_f[0:n, 0:n]),
             reads=st["bufs"] + [C.b_const], writes=[C.b_psum[0], guard])
        P.op("dve", lambda e: e.tensor_copy(out=C.vt[:, c0:c0 + n], in_=ps[:, 0:n]),
             reads=[C.b_psum[0]], writes=[C.b_vt])
        st["col"] += n
        st["cur"] = 0

    slot_bufs = {}
    for key, ap, n in rows:
        r0 = 0
        C.vcol[key] = st["col"] + st["cur"]
        while r0 < n:
            cur = st["cur"]
            if cur == 0:
                st["bufs"] = []
            m = min(n - r0, 128 - cur)
            bb = slot_bufs.setdefault((cur, m), Buf("stg"))
            st["bufs"].append(bb)
            P.dma("sp", lambda e, ap=ap, a0=r0, m=m, c0=cur: e.dma_start(out=stg[c0:c0 + m, :], in_=ap[a0:a0 + m, :]),
                  sem, reads=[guard], writes=[bb])
            st["cur"] += m
            r0 += m
            if st["cur"] == 128:
                flush()
    flush()


def V(C, key, j=0):
    c = C.vcol[key] + j
    return C.vt[:, c:c + 1]


def ensure_prep(C):
    if not getattr(C, "prep_done", False):
        C.prep_done = True
        prep_weights(C)


def prep_some(C, n):
    ensure_prep(C)
    for _ in range(min(n, len(C.prep_queue))):
        C.prep_queue.pop(0)()


def prep_weights(C):
    P, nc, t = C.P, C.nc, C.t
    C.wb = {}
    C.b_wb = {}
    C.prep_queue = []
    sem_i = [0]
    sems = [P.new_dma_sem() for _ in range(4)]
    C.prep_sems = tuple(sems)

    def cast(name, dst, src):
        b = Buf(name)
        s = sems[sem_i[0] % 4]
        sem_i[0] += 1
        C.prep_queue.append(lambda: P.dma("pool", lambda e: e.dma_start(out=dst, in_=src), s, reads=[], writes=[b]))
        return b

    for l in range(2):
        wu = nc.dram_tensor("wup_b%d" % l, [24, 128, 8, 256], BF16, kind="Internal")
        src = t["ffn_w_up"][l].rearrange("(kc p) f -> p kc f", p=128)
        bl = []
        for g in range(12):
            for gv in range(2):
                base = gv * DFF + g * 256
                bl.append(cast("wup%d_%d_%d" % (l, g, gv), wu[g * 2 + gv], src[:, :, base:base + 256]))
        C.wb[("wup", l)] = wu
        C.b_wb[("wup", l)] = bl
        wd = nc.dram_tensor("wdn_b%d" % l, [128, 24, 1024], BF16, kind="Internal")
        srcd = t["ffn_w_down"][l].rearrange("(fc p) d -> p fc d", p=128)
        bl = []
        for q in range(4):
            bl.append(cast("wdn%d_%d" % (l, q), wd[:, q * 6:(q + 1) * 6, :], srcd[:, q * 6:(q + 1) * 6, :]))
        C.wb[("wdn", l)] = wd
        C.b_wb[("wdn", l)] = bl


def norm_transpose(C, xt, b_xt, hT, b_hT, gain_key, nblk, scr):
    P = C.P
    ss, rs, xn, junk = scr["ss"], scr["rs"], scr["xn"], scr["junk"]
    b_ss, b_xn, b_junk = scr["b_ss"], scr["b_xn"], scr["b_junk"]
    for b in range(nblk):
        P.op("act", lambda e, b=b: e.activation(out=junk[:, :], in_=xt[:, b, :], func=AF.Square,
                                                 accum_out=ss[:, b:b + 1]),
             reads=[b_xt], writes=[b_junk, b_ss])
    P.op("pool", lambda e: e.tensor_scalar(out=rs[:, 0:nblk], in0=ss[:, 0:nblk], scalar1=1.0 / D, scalar2=EPS, op0=ALU.mult, op1=ALU.add),
         reads=[b_ss], writes=[b_ss])
    P.op("pool", lambda e: e.tensor_tensor(out=rs[:, 0:nblk], in0=rs[:, 0:nblk], in1=C.neghalf[:, 0:nblk], op=ALU.pow),
         reads=[b_ss, C.b_const], writes=[b_ss])
    for b in range(nblk):
        P.op("dve", lambda e, b=b: e.tensor_scalar(out=xn[:, b, :], in0=xt[:, b, :], scalar1=rs[:, b:b + 1],
                                                    scalar2=None, op0=ALU.mult),
             reads=[b_xt, b_ss], writes=[b_xn[b]])
    for kc in range(8):
        pi = C.tp_rr % 2
        C.tp_rr += 1
        ps = C.psum_tp[pi]
        for b in range(nblk):
            P.op("pe", lambda e, b=b, kc=kc, ps=ps: e.transpose(out=ps[:, b * 128:(b + 1) * 128],
                                                                  in_=xn[:, b, kc * 128:(kc + 1) * 128],
                                                                  identity=C.ident_b[:, :]),
                 reads=[b_xn[b], C.b_const], writes=[C.b_psum_tp[pi]])
        P.op("act", lambda e, kc=kc, ps=ps: e.activation(out=hT[:, kc, 0:nblk * 128], in_=ps[:, 0:nblk * 128],
                                                          func=AF.Copy, scale=V(C, gain_key, kc)),
             reads=[C.b_psum_tp[pi], C.b_vt], writes=[b_hT])


def ffn_phase(C, l, src, b_src, dst, b_dst):
    P, nc, A = C.P, C.nc, C.arena
    prep_some(C, 1000)
    P.barrier()
    A.reset()
    NB = TT // 128
    wd = A.alloc([24, 1024], BF16)
    b_wd = [Buf("wd%d" % q) for q in range(4)]
    wu = [A.alloc([2, 8, 256], BF16) for _ in range(2)]
    b_wu = [[Buf("wu"), Buf("wu")] for _ in range(2)]
    xt = [A.alloc([NB, 1024], F32) for _ in range(2)]
    b_xt = [Buf("xt0"), Buf("xt1")]
    hT = A.alloc([8, TT], BF16)
    b_hT = Buf("hT")
    gT = A.alloc([24, TT], BF16)
    b_gT = [Buf("gT%d" % i) for i in range(24)]
    u_sb = [A.alloc([TT + 4], F32) for _ in range(4)]
    b_u = [Buf("u%d" % i) for i in range(4)]
    cb = [A.alloc([TT], F32) for _ in range(4)]
    b_c = [Buf("c%d" % i) for i in range(4)]
    halo = A.alloc([48, 2], F32)
    b_halo = [Buf("halo%d" % i) for i in range(48)]
    scr = dict(ss=A.alloc([8], F32), rs=A.alloc([8], F32), xn=A.alloc([NB, 1024], BF16), junk=A.alloc([1024], BF16),
               b_ss=Buf("ss"), b_xn=[Buf("xn%d" % i) for i in range(NB)], b_junk=Buf("junk"))
    s_wd = [P.new_dma_sem() for _ in range(4)]
    s_wu = [[P.new_dma_sem(), P.new_dma_sem()] for _ in range(2)]
    s_x = [P.new_dma_sem(), P.new_dma_sem()]
    s_o = [P.new_dma_sem(), P.new_dma_sem()]

    wdd = C.wb[("wdn", l)]
    for q in range(4):
        P.dma("pool", lambda e, q=q: e.dma_start(out=wd[:, q * 6:(q + 1) * 6, :], in_=wdd[:, q * 6:(q + 1) * 6, :]),
              s_wd[q], reads=[C.b_wb[("wdn", l)][q]], writes=[b_wd[q]])
    P.op("pool", lambda e: e.memset(halo[:, :, :], 0.0), writes=b_halo)

    wud = C.wb[("wup", l)]
    srcv = src.rearrange("(t b p) d -> t p b d", b=NB, p=128)
    dstv = dst.rearrange("(t b p) d -> t p b d", b=NB, p=128)

    def load_x(ti):
        sl = ti % 2
        P.dma("sp", lambda e: e.dma_start(out=xt[sl][:, :, :], in_=srcv[ti]), s_x[sl], reads=[b_src[ti]], writes=[b_xt[sl]])

    wu_seq = [(ti, g) for ti in range(NT) for g in range(12)]

    def load_wu(i):
        ti, g = wu_seq[i]
        sl = i % 2
        for gv in range(2):
            P.dma("pool", lambda e, gv=gv: e.dma_start(out=wu[sl][:, gv, :, :], in_=wud[g * 2 + gv]),
                  s_wu[sl][gv], reads=[C.b_wb[("wup", l)][g * 2 + gv]], writes=[b_wu[sl][gv]])

    load_x(0)
    load_wu(0)
    cwk, cbk = ("ffn_cw", l), ("ffn_cb", l)
    norm_transpose(C, xt[0], b_xt[0], hT, b_hT, ("ffn_norm", l), NB, scr)
    for ti in range(NT):
        sl = ti % 2
        if ti + 1 < NT:
            load_x(ti + 1)
        for g in range(12):
            i = ti * 12 + g
            if i + 1 < len(wu_seq):
                load_wu(i + 1)
            wsl = i % 2
            for j in range(2):
                c = g * 2 + j
                res = []
                for gv in range(2):
                    fc = c + 24 * gv
                    pi = C.mm_rr % 4
                    C.mm_rr += 1
                    ps = C.psum[pi]
                    for kc in range(8):
                        P.op("pe", lambda e, kc=kc, ps=ps, gv=gv, j=j, wsl=wsl: e.matmul(
                            out=ps[:, 0:TT], lhsT=wu[wsl][:, gv, kc, j * 128:(j + 1) * 128], rhs=hT[:, kc, :],
                            start=(kc == 0), stop=(kc == 7)),
                            reads=[b_wu[wsl][gv], b_hT], writes=[C.b_psum[pi]])
                    ui = C.u_rr % 4
                    C.u_rr += 1
                    u, bu, cc, bc = u_sb[ui], b_u[ui], cb[ui], b_c[ui]
                    P.op("pool", lambda e, u=u, fc=fc: e.tensor_copy(out=u[:, 0:2], in_=halo[:, fc, :]),
                         reads=[b_halo[fc]], writes=[bu])
                    P.op("act", lambda e, u=u, ps=ps: e.activation(out=u[:, 2:TT + 2], in_=ps[:, 0:TT], func=AF.Copy),
                         reads=[C.b_psum[pi]], writes=[bu])
                    P.op("pool", lambda e, u=u, fc=fc: e.tensor_copy(out=halo[:, fc, :], in_=u[:, TT:TT + 2]),
                         reads=[bu], writes=[b_halo[fc]])
                    P.op("dve", lambda e, u=u, cc=cc, fc=fc: e.tensor_scalar(
                        out=cc[:, :], in0=u[:, 2:TT + 2], scalar1=V(C, cwk + (2,), fc), scalar2=V(C, cbk, fc),
                        op0=ALU.mult, op1=ALU.add), reads=[bu, C.b_vt], writes=[bc])
                    P.op("dve", lambda e, u=u, cc=cc, fc=fc: e.scalar_tensor_tensor(
                        out=cc[:, :], in0=u[:, 1:TT + 1], scalar=V(C, cwk + (1,), fc), in1=cc[:, :],
                        op0=ALU.mult, op1=ALU.add), reads=[bu, bc, C.b_vt], writes=[bc])
                    P.op("dve", lambda e, u=u, cc=cc, fc=fc: e.scalar_tensor_tensor(
                        out=cc[:, :], in0=u[:, 0:TT], scalar=V(C, cwk + (0,), fc), in1=cc[:, :],
                        op0=ALU.mult, op1=ALU.add), reads=[bu, bc, C.b_vt], writes=[bc])
                    res.append((cc, bc))
                (cg, bcg), (cv, bcv) = res
                P.op("act", lambda e, cg=cg: e.activation(out=cg[:, :], in_=cg[:, :], func=AF.Gelu_apprx_tanh),
                     reads=[bcg], writes=[bcg])
                P.op("dve", lambda e, cg=cg, cv=cv, c=c: e.tensor_tensor(out=gT[:, c, :], in0=cg[:, :], in1=cv[:, :], op=ALU.mult),
                     reads=[bcg, bcv], writes=[b_gT[c]])
        if ti + 1 < NT:
            norm_transpose(C, xt[1 - sl], b_xt[1 - sl], hT, b_hT, ("ffn_norm", l), NB, scr)
        for b in range(NB):
            for hf in range(2):
                pi = 4 + (C.dn_rr % 2)
                C.dn_rr += 1
                ps = C.psum[pi]
                for fc in range(24):
                    P.op("pe", lambda e, fc=fc, ps=ps, b=b, hf=hf, sl=sl: e.matmul(
                        out=ps[:, 0:512], lhsT=gT[:, fc, b * 128:(b + 1) * 128], rhs=wd[:, fc, hf * 512:(hf + 1) * 512],
                        start=(fc == 0), stop=(fc == 23)),
                        reads=[b_gT[fc], b_wd[fc // 6]], writes=[C.b_psum[pi]])
                P.op("dve", lambda e, ps=ps, b=b, hf=hf, sl=sl: e.tensor_tensor(
                    out=xt[sl][:, b, hf * 512:(hf + 1) * 512], in0=ps[:, 0:512], in1=xt[sl][:, b, hf * 512:(hf + 1) * 512],
                    op=ALU.add), reads=[C.b_psum[pi], b_xt[sl]], writes=[b_xt[sl]])
        P.dma("sp", lambda e, ti=ti, sl=sl: e.dma_start(out=dstv[ti], in_=xt[sl][:, :, :]), s_o[sl], reads=[b_xt[sl]], writes=[b_dst[ti]])


class Rot:
    def __init__(self, A, n, free_shape, dt, name="rot"):
        self.aps = [A.alloc(free_shape, dt) for _ in range(n)]
        self.bufs = [Buf("%s%d" % (name, i)) for i in range(n)]
        self.i = 0

    def get(self):
        k = self.i % len(self.aps)
        self.i += 1
        return self.aps[k], self.bufs[k]


def mm_psum(C):
    pi = C.mm_rr % 4
    C.mm_rr += 1
    return C.psum[pi], C.b_psum[pi]


def odd_phase(C, src, b_src, dst, b_dst):
    P, nc, A, t = C.P, C.nc, C.arena, C.t
    P.barrier()
    A.reset()
    NB = TT // 128
    NBK = 10
    w_in = A.alloc([8, 2 * LRU_W], BF16)
    b_win = [Buf("win%d" % i) for i in range(8)]
    w_out = A.alloc([NBK, D], BF16)
    b_wout = [Buf("wout0"), Buf("wout1")]
    wa = A.alloc([NBK, 128], BF16)
    wx = A.alloc([NBK, 128], BF16)
    b_wa, b_wx = Buf("wa"), Buf("wx")
    xt = [A.alloc([NB, 1024], F32) for _ in range(2)]
    b_xt = [Buf("xt0"), Buf("xt1")]
    hT = A.alloc([8, TT], BF16)
    b_hT = Buf("hT")
    mixT = A.alloc([NBK, TT], BF16)
    b_mix = [Buf("mix%d" % i) for i in range(NBK)]
    halo = A.alloc([NBK, 4], F32)
    b_halo = [Buf("halo%d" % i) for i in range(NBK)]
    hprev = A.alloc([NBK, 2], F32)
    b_hprev = [Buf("hprev%d" % i) for i in range(NBK)]
    Lt = A.alloc([NBK], F32)
    hb = A.alloc([2, NBK], F32)
    b_L = Buf("L")
    halfc = A.alloc([TT], F32)
    b_half = Buf("half")
    scr = dict(ss=A.alloc([8], F32), rs=A.alloc([8], F32), xn=A.alloc([NB, 1024], BF16), junk=A.alloc([1024], BF16),
               b_ss=Buf("ss"), b_xn=[Buf("xn%d" % i) for i in range(NB)], b_junk=Buf("junk"))
    r_y = Rot(A, 3, [TT], F32, "y")
    r_u = Rot(A, 2, [TT + 4], F32, "u")
    r_xc = Rot(A, 3, [TT], F32, "xc")
    r_xcb = Rot(A, 3, [TT], BF16, "xcb")
    r_r = Rot(A, 2, [TT], F32, "r")
    r_gi = Rot(A, 2, [TT], F32, "gi")
    r_a = Rot(A, 2, [TT], F32, "a")
    r_sq = Rot(A, 2, [TT], F32, "sq")
    r_hs = Rot(A, 2, [TT], F32, "hs")
    sems = [P.new_dma_sem() for _ in range(14)]
    s_x = [P.new_dma_sem(), P.new_dma_sem()]
    s_o = [P.new_dma_sem(), P.new_dma_sem()]

    src_in = t["odd_w_in"][0].rearrange("(kc p) f -> p kc f", p=128)
    for kc in range(8):
        P.dma("pool", lambda e, kc=kc: e.dma_start(out=w_in[:, kc, :], in_=src_in[:, kc, :]), sems[kc], writes=[b_win[kc]])
    P.dma("pool", lambda e: e.dma_start(out=wa[:, :, :], in_=t["odd_gate_a_w"][0].rearrange("n c d -> c n d")),
          sems[8], writes=[b_wa])
    P.dma("pool", lambda e: e.dma_start(out=wx[:, :, :], in_=t["odd_gate_x_w"][0].rearrange("n c d -> c n d")),
          sems[9], writes=[b_wx])
    src_out = t["odd_w_out"][0].rearrange("(n p) d -> p n d", p=128)
    for q in range(2):
        P.dma("pool", lambda e, q=q: e.dma_start(out=w_out[:, q * 5:(q + 1) * 5, :], in_=src_out[:, q * 5:(q + 1) * 5, :]),
              sems[10 + q], writes=[b_wout[q]])
    P.op("pool", lambda e: e.memset(halo[:, :, :], 0.0), writes=b_halo)
    P.op("pool", lambda e: e.memset(hprev[:, :, :], 0.0), writes=b_hprev)
    P.op("pool", lambda e: e.memset(halfc[:, :], 0.5), writes=[b_half])
    lamc0 = C.vcol[("odd_lam",)]
    P.op("act", lambda e, lamc0=lamc0: e.activation(out=Lt[:, :], in_=C.vt[:, lamc0:lamc0 + NBK], func=AF.Exp, scale=-1.0),
         reads=[C.b_vt], writes=[b_L])
    P.op("act", lambda e: e.activation(out=Lt[:, :], in_=Lt[:, :], func=AF.Ln, bias=C.one_t[:, 0:1]),
         reads=[b_L, C.b_const], writes=[b_L])
    P.op("act", lambda e: e.activation(out=Lt[:, :], in_=Lt[:, :], func=AF.Exp, scale=-8.0), reads=[b_L], writes=[b_L])
    ca, cx = C.vcol[("odd_ba",)], C.vcol[("odd_bx",)]
    P.op("dve", lambda e, ca=ca: e.tensor_scalar(out=hb[:, 0, :], in0=C.vt[:, ca:ca + NBK], scalar1=0.5, scalar2=None, op0=ALU.mult),
         reads=[C.b_vt], writes=[b_L])
    P.op("dve", lambda e, cx=cx: e.tensor_scalar(out=hb[:, 1, :], in0=C.vt[:, cx:cx + NBK], scalar1=0.5, scalar2=None, op0=ALU.mult),
         reads=[C.b_vt], writes=[b_L])

    srcv = src.rearrange("(t b p) d -> t p b d", b=NB, p=128)
    dstv = dst.rearrange("(t b p) d -> t p b d", b=NB, p=128)

    def load_x(ti):
        sl = ti % 2
        P.dma("sp", lambda e: e.dma_start(out=xt[sl][:, :, :], in_=srcv[ti]), s_x[sl], reads=[b_src[ti]], writes=[b_xt[sl]])

    def S1(n):
        ps, bps = mm_psum(C)
        for kc in range(8):
            P.op("pe", lambda e, kc=kc: e.matmul(out=ps[:, 0:TT], lhsT=w_in[:, kc, n * 128:(n + 1) * 128],
                                                 rhs=hT[:, kc, :], start=(kc == 0), stop=(kc == 7)),
                 reads=[b_win[kc], b_hT], writes=[bps])
        y, by = r_y.get()
        P.op("act", lambda e: e.activation(out=y[:, :], in_=ps[:, 0:TT], func=AF.Gelu_apprx_tanh), reads=[bps], writes=[by])
        ps2, bps2 = mm_psum(C)
        for kc in range(8):
            P.op("pe", lambda e, kc=kc: e.matmul(out=ps2[:, 0:TT], lhsT=w_in[:, kc, LRU_W + n * 128:LRU_W + (n + 1) * 128],
                                                 rhs=hT[:, kc, :], start=(kc == 0), stop=(kc == 7)),
                 reads=[b_win[kc], b_hT], writes=[bps2])
        u, bu = r_u.get()
        P.op("pool", lambda e: e.tensor_copy(out=u[:, 0:3], in_=halo[:, n, 0:3]), reads=[b_halo[n]], writes=[bu])
        P.op("act", lambda e: e.activation(out=u[:, 3:TT + 3], in_=ps2[:, 0:TT], func=AF.Copy), reads=[bps2], writes=[bu])
        P.op("pool", lambda e: e.tensor_copy(out=halo[:, n, 0:3], in_=u[:, TT:TT + 3]), reads=[bu], writes=[b_halo[n]])
        xc, bxc = r_xc.get()
        P.op("dve", lambda e: e.tensor_scalar(out=xc[:, :], in0=u[:, 3:TT + 3], scalar1=V(C, ("odd_cw", 3), n),
                                              scalar2=V(C, ("odd_cb",), n), op0=ALU.mult, op1=ALU.add),
             reads=[bu, C.b_vt], writes=[bxc])
        for j in range(3):
            P.op("dve", lambda e, j=j: e.scalar_tensor_tensor(out=xc[:, :], in0=u[:, j:TT + j], scalar=V(C, ("odd_cw", j), n),
                                                               in1=xc[:, :], op0=ALU.mult, op1=ALU.add),
                 reads=[bu, bxc, C.b_vt], writes=[bxc])
        xcb, bxcb = r_xcb.get()
        P.op("act", lambda e: e.activation(out=xcb[:, :], in_=xc[:, :], func=AF.Copy), reads=[bxc], writes=[bxcb])
        return dict(y=y, by=by, xc=xc, bxc=bxc, xcb=xcb, bxcb=bxcb)

    def S2(n, st):
        y, by, xc, bxc, xcb, bxcb = st["y"], st["by"], st["xc"], st["bxc"], st["xcb"], st["bxcb"]
        psa, bpsa = mm_psum(C)
        P.op("pe", lambda e: e.matmul(out=psa[:, 0:TT], lhsT=wa[:, n, :], rhs=xcb[:, :], start=True, stop=True),
             reads=[b_wa, bxcb], writes=[bpsa])
        psg, bpsg = mm_psum(C)
        P.op("pe", lambda e: e.matmul(out=psg[:, 0:TT], lhsT=wx[:, n, :], rhs=xcb[:, :], start=True, stop=True),
             reads=[b_wx, bxcb], writes=[bpsg])
        r, br = r_r.get()
        gi, bgi = r_gi.get()
        P.op("act", lambda e: e.activation(out=r[:, :], in_=psa[:, 0:TT], func=AF.Tanh, scale=0.5, bias=hb[:, 0, n:n + 1]),
             reads=[bpsa, b_L], writes=[br])
        P.op("act", lambda e: e.activation(out=gi[:, :], in_=psg[:, 0:TT], func=AF.Tanh, scale=0.5, bias=hb[:, 1, n:n + 1]),
             reads=[bpsg, b_L], writes=[bgi])
        a, ba = r_a.get()
        sq, bsq = r_sq.get()
        P.op("pool", lambda e: e.tensor_scalar(out=r[:, :], in0=r[:, :], scalar1=0.5, scalar2=0.5, op0=ALU.mult, op1=ALU.add),
             reads=[br], writes=[br])
        P.op("pool", lambda e: e.tensor_tensor(out=a[:, :], in0=Lt[:, n:n + 1].to_broadcast([128, TT]), in1=r[:, :], op=ALU.pow),
             reads=[br, b_L], writes=[ba])
        P.op("pool", lambda e: e.tensor_tensor(out=sq[:, :], in0=a[:, :], in1=a[:, :], op=ALU.mult), reads=[ba], writes=[bsq])
        P.op("pool", lambda e: e.tensor_scalar(out=sq[:, :], in0=sq[:, :], scalar1=-0.25, scalar2=0.25, op0=ALU.mult, op1=ALU.add),
             reads=[bsq], writes=[bsq])
        P.op("pool", lambda e: e.tensor_tensor(out=sq[:, :], in0=sq[:, :], in1=halfc[:, :], op=ALU.pow), reads=[bsq, b_half], writes=[bsq])
        P.op("dve", lambda e: e.scalar_tensor_tensor(out=gi[:, :], in0=gi[:, :], scalar=1.0, in1=xc[:, :], op0=ALU.add, op1=ALU.mult),
             reads=[bgi, bxc], writes=[bgi])
        P.op("dve", lambda e: e.tensor_tensor(out=gi[:, :], in0=gi[:, :], in1=sq[:, :], op=ALU.mult), reads=[bgi, bsq], writes=[bgi])
        hs, bhs = r_hs.get()
        P.op("dve", lambda e: e.tensor_tensor_scan(out=hs[:, :], data0=a[:, :], data1=gi[:, :], initial=hprev[:, n, 0:1],
                                                   op0=ALU.mult, op1=ALU.add), reads=[ba, bgi, b_hprev[n]], writes=[bhs])
        P.op("pool", lambda e: e.tensor_copy(out=hprev[:, n, 0:1], in_=hs[:, TT - 1:TT]), reads=[bhs], writes=[b_hprev[n]])
        P.op("dve", lambda e: e.tensor_tensor(out=mixT[:, n, :], in0=hs[:, :], in1=y[:, :], op=ALU.mult),
             reads=[bhs, by], writes=[b_mix[n]])

    load_x(0)
    norm_transpose(C, xt[0], b_xt[0], hT, b_hT, ("odd_norm",), NB, scr)
    for ti in range(NT):
        sl = ti % 2
        if ti + 1 < NT:
            load_x(ti + 1)
        prev = None
        for n in range(NBK):
            st = S1(n)
            if prev is not None:
                S2(*prev)
            prev = (n, st)
        S2(*prev)
        if ti + 1 < NT:
            norm_transpose(C, xt[1 - sl], b_xt[1 - sl], hT, b_hT, ("odd_norm",), NB, scr)
        for b in range(NB):
            for hf in range(2):
                pi = 4 + (C.dn_rr % 2)
                C.dn_rr += 1
                ps = C.psum[pi]
                for n in range(NBK):
                    P.op("pe", lambda e, n=n, ps=ps, b=b, hf=hf: e.matmul(
                        out=ps[:, 0:512], lhsT=mixT[:, n, b * 128:(b + 1) * 128], rhs=w_out[:, n, hf * 512:(hf + 1) * 512],
                        start=(n == 0), stop=(n == NBK - 1)),
                        reads=[b_mix[n], b_wout[n // 5]], writes=[C.b_psum[pi]])
                P.op("dve", lambda e, ps=ps, b=b, hf=hf, sl=sl: e.tensor_tensor(
                    out=xt[sl][:, b, hf * 512:(hf + 1) * 512], in0=ps[:, 0:512], in1=xt[sl][:, b, hf * 512:(hf + 1) * 512],
                    op=ALU.add), reads=[C.b_psum[pi], b_xt[sl]], writes=[b_xt[sl]])
        P.dma("sp", lambda e, ti=ti, sl=sl: e.dma_start(out=dstv[ti], in_=xt[sl][:, :, :]), s_o[sl],
              reads=[b_xt[sl]], writes=[b_dst[ti]])


def evenA_phase(C, src, b_src, aoT, b_ao):
    P, nc, A, t = C.P, C.nc, C.arena, C.t
    P.barrier()
    A.reset()
    NB = TT // 128
    NCH = TT // 64
    w = A.alloc([8, 2048], BF16)
    b_w = [Buf("w%d" % i) for i in range(8)]
    xt = [A.alloc([NB, 1024], F32) for _ in range(2)]
    b_xt = [Buf("xt0"), Buf("xt1")]
    hT = A.alloc([8, TT], BF16)
    b_hT = Buf("hT")
    scr = dict(ss=A.alloc([8], F32), rs=A.alloc([8], F32), xn=A.alloc([NB, 1024], BF16), junk=A.alloc([1024], BF16),
               b_ss=Buf("ss"), b_xn=[Buf("xn%d" % i) for i in range(NB)], b_junk=Buf("junk"))
    lbt = A.alloc([3, 4], F32)
    lb = A.alloc([4], F32)
    oml = A.alloc([4], F32)
    b_lb = Buf("lb")
    rmask = A.alloc([TT], F32)
    b_rmask = Buf("rmask")
    ones_b = A.alloc([128], BF16)
    causT = A.alloc([64], F32)
    b_cm = Buf("cm")
    St = A.alloc([4, 128], F32)
    Sb = A.alloc([4, 128], BF16)
    b_S = [Buf("S%d" % h) for h in range(4)]
    b_Sb = [Buf("Sb%d" % h) for h in range(4)]
    vtok = A.alloc([NCH, 512], BF16)
    b_vtok = [Buf("vtok%d" % c) for c in range(NCH)]
    eb = A.alloc([4, TT], F32)
    b_eb = [Buf("eb%d" % h) for h in range(4)]
    kf = A.alloc([4, TT], F32)
    b_kf = [Buf("kf%d" % h) for h in range(4)]
    ktb = A.alloc([4, TT], BF16)
    b_ktb = [Buf("ktb%d" % h) for h in range(4)]
    qtb = A.alloc([4, TT], BF16)
    b_qtb = [Buf("qtb%d" % h) for h in range(4)]
    sg = A.alloc([4, TT], F32)
    b_sg = [Buf("sg%d" % h) for h in range(4)]
    osb = A.alloc([4, TT], F32)
    b_osb = [Buf("osb%d" % h) for h in range(4)]
    ao = [A.alloc([4, TT], BF16) for _ in range(2)]
    b_aot = [Buf("ao0"), Buf("ao1")]
    r_f = Rot(A, 2, [TT], F32, "f")
    r_g = Rot(A, 2, [TT], F32, "g")
    r_b = Rot(A, 2, [TT], F32, "b")
    r_enb = Rot(A, 2, [TT], F32, "enb")
    r_qs = Rot(A, 2, [TT], F32, "qs")
    r_khb = Rot(A, 4, [64], BF16, "khb")
    r_kht = Rot(A, 4, [128], BF16, "kht")
    r_pt = Rot(A, 4, [64], BF16, "pt")
    r_osq = Rot(A, 2, [TT], BF16, "osq")
    r_rt = Rot(A, 2, [TT], F32, "rt")
    sems = [P.new_dma_sem() for _ in range(8)]
    s_x = [P.new_dma_sem(), P.new_dma_sem()]
    s_o = [P.new_dma_sem(), P.new_dma_sem()]
    s_m = P.new_dma_sem()

    src_in = t["even_w_in"][0].rearrange("(kc p) f -> p kc f", p=128)
    for kc in range(8):
        P.dma("pool", lambda e, kc=kc: e.dma_start(out=w[:, kc, :], in_=src_in[:, kc, 0:2048]), sems[kc], writes=[b_w[kc]])
    P.dma("sp", lambda e: e.dma_start(out=causT[0:64, :], in_=t["c_causT"][:, :]), s_m, writes=[b_cm])
    lbc0 = C.vcol[("lb", 0)]
    P.op("act", lambda e, lbc0=lbc0: e.activation(out=lbt[:, :, :].rearrange("p a b -> p (a b)"), in_=C.vt[:, lbc0:lbc0 + 12], func=AF.Exp),
         reads=[C.b_vt], writes=[b_lb])
    P.op("dve", lambda e: e.tensor_tensor(out=oml[:, :], in0=lbt[:, 0, :], in1=lbt[:, 1, :], op=ALU.add), reads=[b_lb], writes=[b_lb])
    P.op("dve", lambda e: e.tensor_tensor(out=oml[:, :], in0=oml[:, :], in1=lbt[:, 2, :], op=ALU.add), reads=[b_lb], writes=[b_lb])
    P.op("dve", lambda e: e.reciprocal(out=oml[:, :], in_=oml[:, :]), reads=[b_lb], writes=[b_lb])
    P.op("dve", lambda e: e.tensor_tensor(out=lb[:, :], in0=lbt[:, 0, :], in1=oml[:, :], op=ALU.mult), reads=[b_lb], writes=[b_lb])
    P.op("dve", lambda e: e.tensor_scalar(out=oml[:, :], in0=lb[:, :], scalar1=-1.0, scalar2=1.0, op0=ALU.mult, op1=ALU.add),
         reads=[b_lb], writes=[b_lb])
    P.op("pool", lambda e: e.memset(rmask[:, :], 1.0), writes=[b_rmask])
    P.op("pool", lambda e: e.memset(rmask[:, :].rearrange("p (c t) -> p c t", t=64)[:, :, 0:1], 0.0), writes=[b_rmask])
    P.op("pool", lambda e: e.memset(ones_b[:, :], 1.0), writes=[b_rmask])
    P.op("pool", lambda e: e.memset(St[:, :, :], 0.0), writes=b_S)
    P.op("pool", lambda e: e.memset(Sb[:, :, :], 0.0), writes=b_Sb)

    srcv = src.rearrange("(t b p) d -> t p b d", b=NB, p=128)
    aov = aoT.rearrange("h p s -> p h s")

    def load_x(ti):
        sl = ti % 2
        P.dma("sp", lambda e: e.dma_start(out=xt[sl][:, :, :], in_=srcv[ti]), s_x[sl], reads=[b_src[ti]], writes=[b_xt[sl]])

    def proj(col0, M=128):
        ps, bps = mm_psum(C)
        for kc in range(8):
            P.op("pe", lambda e, kc=kc, ps=ps: e.matmul(out=ps[0:M, 0:TT], lhsT=w[:, kc, col0:col0 + M], rhs=hT[:, kc, :],
                                                        start=(kc == 0), stop=(kc == 7)),
                 reads=[b_w[kc], b_hT], writes=[bps])
        return ps, bps

    load_x(0)
    for ti in range(NT):
        sl = ti % 2
        if ti + 1 < NT:
            load_x(ti + 1)
        norm_transpose(C, xt[sl], b_xt[sl], hT, b_hT, ("even_norm",), NB, scr)
        for c in range(NCH):
            ps, bps = mm_psum(C)
            for kc in range(8):
                P.op("pe", lambda e, kc=kc, ps=ps, c=c: e.matmul(out=ps[0:64, 0:512], lhsT=hT[:, kc, c * 64:(c + 1) * 64],
                                                                  rhs=w[:, kc, 1024:1536], start=(kc == 0), stop=(kc == 7)),
                     reads=[b_w[kc], b_hT], writes=[bps])
            P.op("act", lambda e, ps=ps, c=c: e.activation(out=vtok[0:64, c, :], in_=ps[0:64, 0:512], func=AF.Copy),
                 reads=[bps], writes=[b_vtok[c]])
        for h in range(4):
            ps, bps = proj(512 + h * 128)
            f, bf = r_f.get()
            P.op("act", lambda e, f=f, ps=ps: e.activation(out=f[:, :], in_=ps[:, 0:TT], func=AF.Sigmoid), reads=[bps], writes=[bf])
            P.op("dve", lambda e, f=f, h=h: e.tensor_scalar(out=f[:, :], in0=f[:, :], scalar1=oml[:, h:h + 1], scalar2=lb[:, h:h + 1],
                                                            op0=ALU.mult, op1=ALU.add), reads=[bf, b_lb], writes=[bf])
            g, bg = r_g.get()
            P.op("act", lambda e, f=f, g=g: e.activation(out=g[:, :], in_=f[:, :], func=AF.Ln), reads=[bf], writes=[bg])
            P.op("dve", lambda e, f=f: e.tensor_scalar(out=f[:, :], in0=f[:, :], scalar1=-1.0, scalar2=1.0, op0=ALU.mult, op1=ALU.add),
                 reads=[bf], writes=[bf])
            bb, bbb = r_b.get()
            P.op("dve", lambda e, bb=bb, g=g: e.tensor_tensor_scan(out=bb[:, :], data0=rmask[:, :], data1=g[:, :], initial=0.0,
                                                                   op0=ALU.mult, op1=ALU.add), reads=[bg, b_rmask], writes=[bbb])
            P.op("act", lambda e, bb=bb, h=h: e.activation(out=eb[:, h, :], in_=bb[:, :], func=AF.Exp), reads=[bbb], writes=[b_eb[h]])
            enb, benb = r_enb.get()
            P.op("act", lambda e, bb=bb, enb=enb: e.activation(out=enb[:, :], in_=bb[:, :], func=AF.Exp, scale=-1.0),
                 reads=[bbb], writes=[benb])
            P.op("dve", lambda e, f=f, enb=enb, h=h: e.tensor_tensor(out=kf[:, h, :], in0=f[:, :], in1=enb[:, :], op=ALU.mult),
                 reads=[bf, benb], writes=[b_kf[h]])
            P.op("pool", lambda e, h=h: e.tensor_copy(out=ktb[:, h, :], in_=kf[:, h, :]), reads=[b_kf[h]], writes=[b_ktb[h]])
            ps, bps = proj(h * 128)
            qs, bqs = r_qs.get()
            P.op("act", lambda e, qs=qs, ps=ps: e.activation(out=qs[:, :], in_=ps[:, 0:TT], func=AF.Silu), reads=[bps], writes=[bqs])
            P.op("dve", lambda e, qs=qs, h=h: e.tensor_tensor(out=qtb[:, h, :], in0=qs[:, :], in1=eb[:, h, :], op=ALU.mult),
                 reads=[bqs, b_eb[h]], writes=[b_qtb[h]])
            ps, bps = proj(1536 + h * 128)
            P.op("act", lambda e, ps=ps, h=h: e.activation(out=sg[:, h, :], in_=ps[:, 0:TT], func=AF.Silu), reads=[bps], writes=[b_sg[h]])
        for c in range(NCH):
            c0, c1, last = c * 64, (c + 1) * 64, c * 64 + 63
            for h in range(4):
                khb, bkhb = r_khb.get()
                P.op("dve", lambda e, khb=khb, h=h, c0=c0, c1=c1, last=last: e.tensor_scalar(
                    out=khb[:, :], in0=kf[:, h, c0:c1], scalar1=eb[:, h, last:last + 1], scalar2=None, op0=ALU.mult),
                    reads=[b_kf[h], b_eb[h]], writes=[bkhb])
                pi = C.tp_rr % 2
                C.tp_rr += 1
                pst = C.psum_tp[pi]
                P.op("pe", lambda e, pst=pst, khb=khb: e.transpose(out=pst[0:64, 0:128], in_=khb[:, :], identity=C.ident_b[:, :]),
                     reads=[bkhb, C.b_const], writes=[C.b_psum_tp[pi]])
                kht, bkht = r_kht.get()
                P.op("act", lambda e, pst=pst, kht=kht: e.activation(out=kht[0:64, :], in_=pst[0:64, 0:128], func=AF.Copy),
                     reads=[C.b_psum_tp[pi]], writes=[bkht])
                ps, bps = mm_psum(C)
                P.op("pe", lambda e, ps=ps, h=h, c0=c0, c1=c1: e.matmul(out=ps[0:64, 0:64], lhsT=ktb[:, h, c0:c1], rhs=qtb[:, h, c0:c1],
                                                                        start=True, stop=True),
                     reads=[b_ktb[h], b_qtb[h]], writes=[bps])
                pt, bpt = r_pt.get()
                P.op("dve", lambda e, ps=ps, pt=pt: e.tensor_tensor(out=pt[0:64, :], in0=ps[0:64, 0:64], in1=causT[0:64, :], op=ALU.mult),
                     reads=[bps, b_cm], writes=[bpt])
                pso, bpso = mm_psum(C)
                P.op("pe", lambda e, pso=pso, h=h, c0=c0, c1=c1: e.matmul(out=pso[:, 0:64], lhsT=Sb[:, h, :], rhs=qtb[:, h, c0:c1],
                                                                          start=True, stop=False),
                     reads=[b_Sb[h], b_qtb[h]], writes=[bpso])
                P.op("pe", lambda e, pso=pso, h=h, c=c, pt=pt: e.matmul(out=pso[:, 0:64], lhsT=vtok[0:64, c, h * 128:(h + 1) * 128],
                                                                        rhs=pt[0:64, :], start=False, stop=True),
                     reads=[b_vtok[c], bpt], writes=[bpso])
                P.op("act", lambda e, pso=pso, h=h, c0=c0, c1=c1: e.activation(out=osb[:, h, c0:c1], in_=pso[:, 0:64], func=AF.Copy),
                     reads=[bpso], writes=[b_osb[h]])
                psu, bpsu = mm_psum(C)
                P.op("pe", lambda e, psu=psu, kht=kht, h=h, c=c: e.matmul(out=psu[:, 0:128], lhsT=kht[0:64, :],
                                                                          rhs=vtok[0:64, c, h * 128:(h + 1) * 128], start=True, stop=True),
                     reads=[bkht, b_vtok[c]], writes=[bpsu])
                P.op("dve", lambda e, psu=psu, h=h, last=last: e.scalar_tensor_tensor(
                    out=St[:, h, :], in0=St[:, h, :], scalar=eb[:, h, last:last + 1], in1=psu[:, 0:128], op0=ALU.mult, op1=ALU.add),
                    reads=[b_S[h], b_eb[h], bpsu], writes=[b_S[h]])
                P.op("pool", lambda e, h=h: e.tensor_copy(out=Sb[:, h, :], in_=St[:, h, :]), reads=[b_S[h]], writes=[b_Sb[h]])
        aot, baot = ao[sl], b_aot[sl]
        for h in range(4):
            osq, bosq = r_osq.get()
            P.op("act", lambda e, osq=osq, h=h: e.activation(out=osq[:, :], in_=osb[:, h, :], func=AF.Square), reads=[b_osb[h]], writes=[bosq])
            ps, bps = mm_psum(C)
            P.op("pe", lambda e, ps=ps, osq=osq: e.matmul(out=ps[:, 0:TT], lhsT=ones_b[:, :], rhs=osq[:, :], start=True, stop=True),
                 reads=[bosq, b_rmask], writes=[bps])
            rt, brt = r_rt.get()
            P.op("act", lambda e, ps=ps, rt=rt: e.activation(out=rt[:, :], in_=ps[:, 0:TT], func=AF.Sqrt, scale=1.0 / 128, bias=C.eps_t[:, 0:1]),
                 reads=[bps, C.b_const], writes=[brt])
            P.op("dve", lambda e, rt=rt: e.reciprocal(out=rt[:, :], in_=rt[:, :]), reads=[brt], writes=[brt])
            P.op("dve", lambda e, rt=rt, h=h: e.tensor_tensor(out=rt[:, :], in0=rt[:, :], in1=osb[:, h, :], op=ALU.mult),
                 reads=[brt, b_osb[h]], writes=[brt])
            P.op("dve", lambda e, rt=rt, h=h, aot=aot: e.scalar_tensor_tensor(
                out=aot[:, h, :], in0=rt[:, :], scalar=V(C, ("a_out_norm",), h), in1=sg[:, h, :], op0=ALU.mult, op1=ALU.mult),
                reads=[brt, b_sg[h], C.b_vt], writes=[baot])
            dsel = getattr(C, "dbg_sel", None)
            if dsel is not None:
                srcs = {"lbd": (eb, b_eb), "eb": (eb, b_eb), "osb": (osb, b_osb), "sg": (sg, b_sg), "qtb": (qtb, b_qtb), "ktb": (ktb, b_ktb)}
                sa, sb = srcs[dsel]
                P.op("dve", lambda e, h=h, aot=aot, sa=sa: e.tensor_copy(out=aot[:, h, :], in_=sa[:, h, :]),
                     reads=[sb[h], baot], writes=[baot])
        if getattr(C, "dbg_sel", None) == "lbd":
            P.op("dve", lambda e, aot=aot: e.tensor_copy(out=aot[:, 0, 0:4], in_=lb[:, :]), reads=[b_lb, baot], writes=[baot])
            P.op("dve", lambda e, aot=aot: e.tensor_copy(out=aot[:, 0, 4:8], in_=oml[:, :]), reads=[b_lb, baot], writes=[baot])
            P.op("dve", lambda e, aot=aot: e.tensor_copy(out=aot[:, 0, 8:20], in_=lbt[:, :, :].rearrange("p a b -> p (a b)")), reads=[b_lb, baot], writes=[baot])
            P.op("dve", lambda e, aot=aot: e.tensor_copy(out=aot[:, 0, 20:32], in_=C.vt[:, lbc0:lbc0 + 12]), reads=[C.b_vt, baot], writes=[baot])
        P.dma("sp", lambda e, ti=ti, aot=aot: e.dma_start(out=aov[:, :, ti * TT:(ti + 1) * TT], in_=aot[:, :, :]), s_o[sl],
              reads=[baot], writes=[b_ao[ti]])


NIT = 12
TOPK = 256
NEG = -1.0e30
MBIG = 30000.0


def _interleave(gens):
    items = []
    for g, n in gens:
        items.append([g, max(n, 1), 0, True])
    total = max(it[1] for it in items) if items else 0
    for step in range(1, total + 1):
        for it in items:
            g, n, done, alive = it
            want = (step * n + total - 1) // total
            while alive and it[2] < want:
                try:
                    next(g)
                    it[2] += 1
                except StopIteration:
                    it[3] = False
                    alive = False
    for it in items:
        if it[3]:
            for _ in it[0]:
                pass


def evenB_phase(C, src, b_src, aoT, b_ao, dst, b_dst, boT=None):
    P, nc, A, t = C.P, C.nc, C.arena, C.t
    P.barrier(skip=getattr(C, "prep_sems", ()))
    A.reset()
    NB = TT // 128
    NKT = S // 128
    wq = A.alloc([8, 512], BF16)
    wiq = A.alloc([8, 512], BF16)
    wk2 = A.alloc([8, 128], BF16)
    wik2 = A.alloc([8, 128], BF16)
    wvw = A.alloc([8, 72], BF16)
    w_out = A.alloc([8, D], BF16)
    b_wparts = [Buf("wp%d" % i) for i in range(10)]
    kiT2 = A.alloc([S], BF16)
    knT2 = A.alloc([S], BF16)
    vaug = A.alloc([NKT, 65], BF16)
    b_ki = [Buf("ki%d" % i) for i in range(NT)]
    b_kn = [Buf("kn%d" % i) for i in range(NT)]
    b_va = [Buf("va%d" % i) for i in range(NKT)]
    b_vones = Buf("vones")
    xt = A.alloc([NB, 1024], F32)
    b_xt = Buf("xt")
    hT = A.alloc([8, TT], BF16)
    b_hT = Buf("hT")
    scr = dict(ss=A.alloc([8], F32), rs=A.alloc([8], F32), xn=A.alloc([NB, 1024], BF16), junk=A.alloc([1024], BF16),
               b_ss=Buf("ss"), b_xn=[Buf("xn%d" % i) for i in range(NB)], b_junk=Buf("junk"))
    qiT = A.alloc([2, 4, TT], BF16)
    qnT = A.alloc([2, 4, TT], BF16)
    b_qi = [Buf("qi%d" % i) for i in range(4)]
    b_qn = [Buf("qn%d" % i) for i in range(4)]
    b_qz = Buf("qz")
    mix = A.alloc([8, TT], BF16)
    b_mixa = Buf("mixa")
    b_mixb = [Buf("mixb%d" % i) for i in range(NB)]
    wabs = A.alloc([NB, 8], F32)
    sgn = A.alloc([NB, 8], F32)
    b_wabs = [Buf("wabs%d" % i) for i in range(NB)]
    isc = [A.alloc([S], F32) for _ in range(2)]
    b_isc = [Buf("isc0"), Buf("isc1")]
    mask = A.alloc([S], BF16)
    b_mask = Buf("mask")
    maskT = [A.alloc([NKT, 128], BF16) for _ in range(2)]
    b_maskT = [[Buf("mT%d_%d" % (k, i)) for i in range(NKT // 4)] for k in range(2)]
    dsg = [A.alloc([8, 128], BF16) for _ in range(2)]
    b_dsg = [Buf("dsg0"), Buf("dsg1")]
    r_R = Rot(A, 3, [512], BF16, "R")
    r_E = Rot(A, 2, [8, 128], BF16, "E")
    r_Pm = Rot(A, 2, [8, 128], BF16, "Pm")
    bis = [A.alloc([8], F32) for _ in range(2)]
    b_bis = [Buf("bis0"), Buf("bis1")]
    wtab = [A.alloc([NIT + 2], F32) for _ in range(2)]
    pow2 = A.alloc([NIT + 2], F32)
    b_pow2 = Buf("pow2")
    botok = A.alloc([512], BF16)
    b_botok = Buf("botok")
    rc = A.alloc([8], F32)
    b_rc = Buf("rc")
    r_ksb = Rot(A, 1, [TT], F32, "ksb")
    r_sq = Rot(A, 1, [TT], BF16, "sq")
    r_rt = Rot(A, 1, [TT], F32, "rt")
    blk1 = A.alloc([128], BF16)
    kg2 = A.alloc([2], F32)
    b_g2 = Buf("g2")
    sems = [P.new_dma_sem() for _ in range(12)]
    s_x, s_o, s_a, s_bo = P.new_dma_sem(), P.new_dma_sem(), P.new_dma_sem(), P.new_dma_sem()

    src_in = t["even_w_in"][0].rearrange("(kc p) f -> p kc f", p=128)
    wo_src = t["even_w_out"][0].rearrange("(m p) d -> p m d", p=128)
    loads = [(wk2[:, :, 0:64], src_in[:, :, 2560:2624]), (wk2[:, :, 64:128], src_in[:, :, 2560:2624]),
             (wik2[:, :, 0:64], src_in[:, :, 3200:3264]), (wik2[:, :, 64:128], src_in[:, :, 3200:3264]),
             (wiq[:, :, :], src_in[:, :, 2688:3200]), (wq[:, :, :], src_in[:, :, 2048:2560]),
             (wvw[:, :, 0:64], src_in[:, :, 2624:2688]), (wvw[:, :, 64:72], src_in[:, :, 3264:3272]),
             (w_out[:, 0:4, :], wo_src[:, 0:4, :]), (w_out[:, 4:8, :], wo_src[:, 4:8, :])]
    for i, (o_, i_) in enumerate(loads):
        P.dma("pool", lambda e, o_=o_, i_=i_: e.dma_start(out=o_, in_=i_), sems[i], writes=[b_wparts[i]])
    b_wk2, b_wik2, b_wiq, b_wq, b_wvw, b_wo = b_wparts[0:2], b_wparts[2:4], [b_wparts[4]], [b_wparts[5]], b_wparts[6:8], b_wparts[8:10]
    ensure_prep(C)
    for i, (nm, col) in enumerate((("b_k_norm", 0), ("b_q_norm", 1))):
        for hf in range(2):
            P.dma("sp", lambda e, nm=nm, col=col, hf=hf: e.dma_start(
                out=kg2[hf * 64:(hf + 1) * 64, col:col + 1], in_=t[nm][0, :].rearrange("(p o) -> p o", o=1)),
                sems[10], writes=[b_g2])
    P.op("dve", lambda e: e.tensor_scalar(out=kg2[:, 1:2], in0=kg2[:, 1:2], scalar1=0.125, scalar2=None, op0=ALU.mult),
         reads=[b_g2], writes=[b_g2])
    P.op("pool", lambda e: e.memset(blk1[:, :], 0.0), writes=[b_pow2])
    P.op("pool", lambda e: e.memset(blk1[0:64, 0:64], 1.0), writes=[b_pow2])
    P.op("pool", lambda e: e.memset(blk1[64:128, 64:128], 1.0), writes=[b_pow2])
    for i in range(NIT + 2):
        P.op("pool", lambda e, i=i: e.memset(pow2[:, i:i + 1], 2.0 ** (-(i + 1))), writes=[b_pow2])
    P.op("pool", lambda e: e.memset(vaug[:, :, 64:65], 1.0), writes=[b_vones])
    P.op("pool", lambda e: e.memset(qiT[:, :, :, :], 0.0), writes=[b_qz])
    P.op("pool", lambda e: e.memset(qnT[:, :, :, :], 0.0), writes=[b_qz])

    srcv = src.rearrange("(t b p) d -> t p b d", b=NB, p=128)
    dstv = dst.rearrange("(t b p) d -> t p b d", b=NB, p=128)
    aov = aoT.rearrange("h p s -> p h s")

    def proj_fm(wt, bw, c0, M=128):
        ps, bps = mm_psum(C)
        for kc in range(8):
            P.op("pe", lambda e, kc=kc: e.matmul(out=ps[0:M, 0:TT], lhsT=wt[:, kc, c0:c0 + M], rhs=hT[:, kc, :],
                                                 start=(kc == 0), stop=(kc == 7)),
                 reads=list(bw) + [b_hT], writes=[bps])
        return ps, bps

    def qk_norm(ps, bps, gcol, outs):
        ksb, bksb = r_ksb.get()
        P.op("act", lambda e: e.activation(out=ksb[:, :], in_=ps[:, 0:TT], func=AF.Copy), reads=[bps], writes=[bksb])
        sq, bsq = r_sq.get()
        P.op("act", lambda e: e.activation(out=sq[:, :], in_=ksb[:, :], func=AF.Square), reads=[bksb], writes=[bsq])
        ps2, bps2 = mm_psum(C)
        P.op("pe", lambda e: e.matmul(out=ps2[:, 0:TT], lhsT=blk1[:, :], rhs=sq[:, :], start=True, stop=True),
             reads=[bsq, b_pow2], writes=[bps2])
        rt, brt = r_rt.get()
        P.op("act", lambda e: e.activation(out=rt[:, :], in_=ps2[:, 0:TT], func=AF.Sqrt, scale=1.0 / 64, bias=C.eps_t[:, 0:1]),
             reads=[bps2, C.b_const], writes=[brt])
        P.op("dve", lambda e: e.reciprocal(out=rt[:, :], in_=rt[:, :]), reads=[brt], writes=[brt])
        for (oap, p0, p1, bo) in outs:
            P.op("dve", lambda e, oap=oap, p0=p0, p1=p1: e.scalar_tensor_tensor(
                out=oap, in0=ksb[p0:p1, :], scalar=kg2[p0:p1, gcol:gcol + 1], in1=rt[p0:p1, :], op0=ALU.mult, op1=ALU.mult),
                reads=[bksb, brt, b_g2, b_qz], writes=[bo])

    def tok_proj(ti, b):
        J = ti * NB + b
        ps, bps = mm_psum(C)
        for kc in range(8):
            P.op("pe", lambda e, kc=kc: e.matmul(out=ps[:, 0:72], lhsT=hT[:, kc, b * 128:(b + 1) * 128], rhs=wvw[:, kc, :],
                                                 start=(kc == 0), stop=(kc == 7)),
                 reads=b_wvw + [b_hT], writes=[bps])
        P.op("act", lambda e: e.activation(out=vaug[:, J, 0:64], in_=ps[:, 0:64], func=AF.Copy), reads=[bps], writes=[b_va[J]])
        P.op("act", lambda e: e.activation(out=wabs[:, b, :], in_=ps[:, 64:72], func=AF.Abs, scale=0.125 * (8 ** -0.5)),
             reads=[bps], writes=[b_wabs[b]])
        P.op("act", lambda e: e.activation(out=sgn[:, b, :], in_=ps[:, 64:72], func=AF.Sign), reads=[bps], writes=[b_wabs[b]])

    def stage_A(ti, b):
        J = ti * NB + b
        nkeys = 128 * (J + 1)
        iscJ, biscJ = isc[J % 2], b_isc[J % 2]
        dg, bdg = dsg[J % 2], b_dsg[J % 2]
        for h in range(8):
            P.op("pool", lambda e, h=h: e.tensor_scalar(out=dg[:, h, :], in0=C.ident_b[:, :], scalar1=sgn[:, b, h:h + 1],
                                                        scalar2=None, op0=ALU.mult),
                 reads=[C.b_const, b_wabs[b]], writes=[bdg])
        ngrp = (nkeys + 511) // 512
        acc, bacc = C.psum[3], C.b_psum[3]
        for G in range(ngrp):
            k0 = G * 512
            wd = min(512, nkeys - k0)
            kbufs = [b_ki[i] for i in range(k0 // TT, (k0 + wd - 1) // TT + 1)]
            pend = None
            for h in range(8):
                hp, jj = h % 2, h // 2
                pi = C.mm_rr % 3
                C.mm_rr += 1
                dps, bdps = C.psum[pi], C.b_psum[pi]
                P.op("pe", lambda e, hp=hp, jj=jj, dps=dps, k0=k0, wd=wd: e.matmul(
                    out=dps[:, 0:wd], lhsT=qiT[:, hp, jj, b * 128:(b + 1) * 128], rhs=kiT2[:, k0:k0 + wd], start=True, stop=True),
                    reads=[b_qi[jj], b_qz] + kbufs, writes=[bdps])
                R, bR = r_R.get()
                P.op("act", lambda e, h=h, dps=dps, R=R, wd=wd: e.activation(out=R[:, 0:wd], in_=dps[:, 0:wd], func=AF.Relu,
                                                                             scale=wabs[:, b, h:h + 1]),
                     reads=[bdps, b_wabs[b]], writes=[bR])
                if pend is not None:
                    pend()
                pend = (lambda h=h, R=R, bR=bR, wd=wd: P.op("pe", lambda e: e.matmul(
                    out=acc[:, 0:wd], lhsT=dg[:, h, :], rhs=R[:, 0:wd], start=(h == 0), stop=(h == 7)),
                    reads=[bdg, bR], writes=[bacc]))
            pend()
            P.op("act", lambda e, k0=k0, wd=wd: e.activation(out=iscJ[:, k0:k0 + wd], in_=acc[:, 0:wd], func=AF.Copy),
                 reads=[bacc], writes=[biscJ])

    def stage_B(ti, b):
        J = ti * NB + b
        nkeys = 128 * (J + 1)
        nk = J + 1
        iscJ, biscJ = isc[J % 2], b_isc[J % 2]
        bs, bbs, wt = bis[J % 2], b_bis[J % 2], wtab[J % 2]
        mT, bmT = maskT[J % 2], b_maskT[J % 2]
        if J < 2:
            P.op("dve", lambda e: e.memset(iscJ[0:64, nkeys - 64:nkeys], NEG), writes=[biscJ])
            P.op("dve", lambda e: e.memset(bs[:, 6:7], -1.0e29), writes=[bbs])
            yield
        else:
            P.op("dve", lambda e: e.tensor_reduce(out=bs[:, 0:1], in_=iscJ[:, 0:nkeys], axis=AX.X, op=ALU.min),
                 reads=[biscJ], writes=[bbs])
            P.op("dve", lambda e: e.memset(iscJ[0:64, nkeys - 64:nkeys], NEG), reads=[], writes=[biscJ])
            yield
            P.op("dve", lambda e: e.tensor_reduce(out=bs[:, 1:2], in_=iscJ[:, 0:nkeys], axis=AX.X, op=ALU.max),
                 reads=[biscJ], writes=[bbs])
            P.op("dve", lambda e: e.tensor_tensor(out=bs[:, 2:3], in0=bs[:, 1:2], in1=bs[:, 0:1], op=ALU.subtract),
                 reads=[bbs], writes=[bbs])
            P.op("dve", lambda e: e.tensor_scalar(out=wt[:, :], in0=pow2[:, :], scalar1=bs[:, 2:3], scalar2=None, op0=ALU.mult),
                 reads=[bbs, b_pow2], writes=[bbs])
            P.op("dve", lambda e: e.tensor_tensor(out=bs[:, 3:4], in0=bs[:, 0:1], in1=wt[:, 0:1], op=ALU.add),
                 reads=[bbs], writes=[bbs])
            yield
            for i in range(NIT):
                P.op("dve", lambda e: e.tensor_scalar(out=mask[:, 0:nkeys], in0=iscJ[:, 0:nkeys], scalar1=bs[:, 3:4], scalar2=None,
                                                      op0=ALU.is_ge, op1=ALU.add, accum_out=bs[:, 4:5]),
                     reads=[biscJ, bbs], writes=[b_mask, bbs])
                P.op("dve", lambda e: e.tensor_scalar(out=bs[:, 5:6], in0=bs[:, 4:5], scalar1=TOPK - 0.5, scalar2=0.5,
                                                      op0=ALU.is_ge, op1=ALU.subtract), reads=[bbs], writes=[bbs])
                P.op("dve", lambda e, i=i: e.scalar_tensor_tensor(out=bs[:, 3:4], in0=bs[:, 5:6], scalar=wt[:, i:i + 1],
                                                                   in1=bs[:, 3:4], op0=ALU.mult, op1=ALU.add),
                     reads=[bbs], writes=[bbs])
                yield
            P.op("dve", lambda e: e.tensor_tensor(out=bs[:, 6:7], in0=bs[:, 3:4], in1=wt[:, NIT:NIT + 1], op=ALU.subtract),
                 reads=[bbs], writes=[bbs])
        P.op("dve", lambda e: e.tensor_scalar(out=mask[:, 0:nkeys], in0=iscJ[:, 0:nkeys], scalar1=bs[:, 6:7], scalar2=None,
                                              op0=ALU.is_ge), reads=[biscJ, bbs], writes=[b_mask])
        for g4 in range((nk + 3) // 4):
            n4 = min(4, nk - g4 * 4)
            pi = C.tp_rr % 2
            C.tp_rr += 1
            pst = C.psum_tp[pi]
            for q4 in range(n4):
                kt = g4 * 4 + q4
                P.op("pe", lambda e, kt=kt, q4=q4, pst=pst: e.transpose(out=pst[:, q4 * 128:(q4 + 1) * 128],
                                                                        in_=mask[:, kt * 128:(kt + 1) * 128], identity=C.ident_b[:, :]),
                     reads=[b_mask, C.b_const], writes=[C.b_psum_tp[pi]])
            P.op("act", lambda e, g4=g4, n4=n4, pst=pst: e.activation(
                out=mT[:, g4 * 4:g4 * 4 + n4, :], in_=pst[:, 0:n4 * 128].rearrange("p (a b) -> p a b", a=n4), func=AF.Copy),
                reads=[C.b_psum_tp[pi]], writes=[bmT[g4]])
            yield

    def stage_C(ti, b):
        J = ti * NB + b
        nk = J + 1
        mT, bmT = maskT[J % 2], b_maskT[J % 2]
        OA, bOA, OB, bOB = C.psum[4], C.b_psum[4], C.psum[5], C.b_psum[5]
        for kt in range(nk):
            kb = b_kn[(kt * 128) // TT]
            pr = C.lp_rr % 2
            C.lp_rr += 1
            L = C.psall[:, pr * 1024:(pr + 1) * 1024]
            bL = [C.b_psum[2 * pr], C.b_psum[2 * pr + 1]]
            for hp in range(2):
                P.op("pe", lambda e, kt=kt, L=L, hp=hp: e.matmul(
                    out=L[:, hp * 512:(hp + 1) * 512], lhsT=knT2[:, kt * 128:(kt + 1) * 128],
                    rhs=qnT[:, hp, :, b * 128:(b + 1) * 128], start=True, stop=True),
                    reads=[kb, b_qz] + b_qn, writes=[bL[hp]])
            E, bE = r_E.get()
            P.op("act", lambda e, E=E, L=L: e.activation(out=E[:, :, :], in_=L.rearrange("p (a b) -> p a b", a=8), func=AF.Exp),
                 reads=bL, writes=[bE])
            Pm, bPm = r_Pm.get()
            P.op("dve", lambda e, E=E, Pm=Pm, kt=kt: e.tensor_tensor(out=Pm[:, :, :], in0=E[:, :, :],
                                                                     in1=mT[:, kt:kt + 1, :].to_broadcast([128, 8, 128]), op=ALU.mult),
                 reads=[bE, bmT[kt // 4]], writes=[bPm])
            for e8 in range(8):
                O, bO = (OA, bOA) if e8 < 4 else (OB, bOB)
                c65 = (e8 % 4) * 65
                P.op("pe", lambda e, e8=e8, Pm=Pm, kt=kt, O=O, c65=c65: e.matmul(
                    out=O[:, c65:c65 + 65], lhsT=Pm[:, e8, :], rhs=vaug[:, kt, :],
                    start=(kt == 0 and e8 % 4 == 0), stop=(kt == nk - 1), skip_group_check=True),
                    reads=[bPm, b_va[kt], b_vones], writes=[bO])
            yield
        for half, (O, bO) in enumerate(((OA, bOA), (OB, bOB))):
            Ov = O[:, 0:260].rearrange("p (a b) -> p a b", a=4)
            P.op("dve", lambda e, Ov=Ov, half=half: e.reciprocal(out=rc[:, half * 4:half * 4 + 4].rearrange("p (a b) -> p a b", b=1),
                                                                  in_=Ov[:, :, 64:65]), reads=[bO], writes=[b_rc])
            bov = botok[:, :].rearrange("p (j two d) -> p j two d", two=2, d=64)[:, :, half, :]
            P.op("dve", lambda e, Ov=Ov, half=half, bov=bov: e.tensor_tensor(
                out=bov, in0=Ov[:, :, 0:64],
                in1=rc[:, half * 4:half * 4 + 4].rearrange("p (a b) -> p a b", b=1).to_broadcast([128, 4, 64]), op=ALU.mult),
                reads=[bO, b_rc], writes=[b_botok])
        pi = C.tp_rr % 2
        C.tp_rr += 1
        pst = C.psum_tp[pi]
        for jj in range(4):
            P.op("pe", lambda e, jj=jj: e.transpose(out=pst[:, jj * 128:(jj + 1) * 128], in_=botok[:, jj * 128:(jj + 1) * 128],
                                                    identity=C.ident_b[:, :]),
                 reads=[b_botok, C.b_const], writes=[C.b_psum_tp[pi]])
        P.op("act", lambda e: e.activation(out=mix[:, 4:8, b * 128:(b + 1) * 128],
                                           in_=pst[:, 0:512].rearrange("p (a b) -> p a b", a=4), func=AF.Copy),
             reads=[C.b_psum_tp[pi]], writes=[b_mixb[b]])
        yield

    C.lp_rr = 0
    for ti in range(NT):
        P.dma("sp", lambda e, ti=ti: e.dma_start(out=xt[:, :, :], in_=srcv[ti]), s_x, reads=[b_src[ti]], writes=[b_xt])
        if boT is None:
            P.dma("sp", lambda e, ti=ti: e.dma_start(out=mix[:, 0:4, :], in_=aov[:, :, ti * TT:(ti + 1) * TT]), s_a,
                  reads=[b_ao[ti]], writes=[b_mixa])
        norm_transpose(C, xt, b_xt, hT, b_hT, ("even_norm",), NB, scr)
        tc = slice(ti * TT, (ti + 1) * TT)
        ps, bps = proj_fm(wk2, b_wk2, 0)
        qk_norm(ps, bps, 0, [(knT2[:, tc], 0, 128, b_kn[ti])])
        ps, bps = proj_fm(wik2, b_wik2, 0)
        P.op("act", lambda e, ps=ps, tc=tc: e.activation(out=kiT2[:, tc], in_=ps[:, 0:TT], func=AF.Copy), reads=[bps], writes=[b_ki[ti]])
        for jj in range(4):
            ps, bps = proj_fm(wiq, b_wiq, jj * 128)
            for hp in range(2):
                P.op("act", lambda e, ps=ps, jj=jj, hp=hp: e.activation(out=qiT[hp * 64:(hp + 1) * 64, hp, jj, :],
                                                                       in_=ps[hp * 64:(hp + 1) * 64, 0:TT], func=AF.Copy),
                     reads=[bps, b_qz], writes=[b_qi[jj]])
        for b in range(NB):
            tok_proj(ti, b)
        for jj in range(4):
            ps, bps = proj_fm(wq, b_wq, jj * 128)
            qk_norm(ps, bps, 1, [(qnT[0:64, 0, jj, :], 0, 64, b_qn[jj]), (qnT[64:128, 1, jj, :], 64, 128, b_qn[jj])])
        prep_some(C, 8)
        stage_A(ti, 0)
        for b in range(NB):
            if b + 1 < NB:
                stage_A(ti, b + 1)
            J = ti * NB + b
            gens = [(stage_B(ti, b), NIT + 4 + (J + 4) // 4)]
            if b > 0:
                gens.append((stage_C(ti, b - 1), J + 1))
            _interleave(gens)
        _interleave([(stage_C(ti, NB - 1), ti * NB + NB)])
        if boT is not None:
            P.dma("sp", lambda e, ti=ti: e.dma_start(out=boT.rearrange("h p s -> p h s")[:, :, ti * TT:(ti + 1) * TT], in_=mix[:, 4:8, :]),
                  s_bo, reads=b_mixb, writes=[b_dst[ti]])
            continue
        for b in range(NB):
            for hf in range(2):
                pi = 4 + (C.dn_rr % 2)
                C.dn_rr += 1
                ps = C.psum[pi]
                for m in range(8):
                    P.op("pe", lambda e, m=m, ps=ps, b=b, hf=hf: e.matmul(
                        out=ps[:, 0:512], lhsT=mix[:, m, b * 128:(b + 1) * 128], rhs=w_out[:, m, hf * 512:(hf + 1) * 512],
                        start=(m == 0), stop=(m == 7)),
                        reads=[b_mixa, b_mixb[b]] + b_wo, writes=[C.b_psum[pi]])
                P.op("dve", lambda e, ps=ps, b=b, hf=hf: e.tensor_tensor(
                    out=xt[:, b, hf * 512:(hf + 1) * 512], in0=ps[:, 0:512], in1=xt[:, b, hf * 512:(hf + 1) * 512], op=ALU.add),
                    reads=[C.b_psum[pi], b_xt], writes=[b_xt])
        P.dma("sp", lambda e, ti=ti: e.dma_start(out=dstv[ti], in_=xt[:, :, :]), s_o, reads=[b_xt], writes=[b_dst[ti]])


def build(phases=("ffn0",), dbg=None):
    nc = bass.Bass("TRN2", target_bir_lowering=False)
    C = Ctx()
    C.nc = nc
    C.P = P = Prog(nc)
    specs = {
        "x": [S, D], "lb_logits": [3, 512], "even_norm": [1, D], "even_w_in": [1, D, EVEN_IN], "even_w_out": [1, D, D],
        "a_out_norm": [1, 512], "b_q_norm": [1, 64], "b_k_norm": [1, 64], "odd_norm": [1, D],
        "odd_w_in": [1, D, 2 * LRU_W], "odd_conv_w": [1, 4, LRU_W], "odd_conv_b": [1, LRU_W],
        "odd_gate_a_w": [1, 10, 128, 128], "odd_gate_a_b": [1, LRU_W], "odd_gate_x_w": [1, 10, 128, 128],
        "odd_gate_x_b": [1, LRU_W], "odd_lambda": [1, LRU_W], "odd_w_out": [1, LRU_W, D],
        "ffn_norm": [2, D], "ffn_w_up": [2, D, 2 * DFF], "ffn_conv_w": [2, 3, 2 * DFF], "ffn_conv_b": [2, 2 * DFF],
        "ffn_w_down": [2, DFF, D],
        "c_ident_f": [128, 128], "c_causT": [64, 64],
    }
    C.t = {k: nc.dram_tensor(k, v, F32, kind="ExternalInput") for k, v in specs.items()}
    C.t["c_ident_b"] = nc.dram_tensor("c_ident_b", [128, 128], BF16, kind="ExternalInput")
    out = nc.dram_tensor("out", [S, D], F32, kind="ExternalOutput")

    C.persist = Arena.__new__(Arena)
    pt = nc.alloc_sbuf_tensor("persist", [128, 2048], F32)
    C.persist.t, C.persist.nwords, C.persist.off = pt, 2048, 0
    C.arena = Arena(nc, 198 * 1024)
    psall = nc.alloc_psum_tensor("psall", [128, 4096], F32)
    C.psall = psall
    C.psum = [psall[:, i * 512:(i + 1) * 512] for i in range(6)]
    C.b_psum = [Buf("ps%d" % i) for i in range(6)]
    tp = [psall[:, i * 512:(i + 1) * 512] for i in range(6, 8)]
    C.psum_tp = [a.bitcast(BF16) for a in tp]
    C.psum_tp_f = tp
    C.b_psum_tp = [Buf("pstp0"), Buf("pstp1")]
    C.tp_rr = C.mm_rr = C.u_rr = C.dn_rr = 0

    C.ident_f = C.persist.alloc([128], F32)
    C.ident_b = C.persist.alloc([128], BF16)
    C.eps_t = C.persist.alloc([1], F32)
    C.one_t = C.persist.alloc([1], F32)
    C.neghalf = C.persist.alloc([8], F32)
    C.b_const = Buf("const")
    s_c = P.new_dma_sem()
    s_c2 = P.new_dma_sem()
    P.op("dve", lambda e: e.memset(C.eps_t[:, :], EPS), writes=[C.b_const])
    P.op("dve", lambda e: e.memset(C.one_t[:, :], 1.0), writes=[C.b_const])
    P.op("dve", lambda e: e.memset(C.neghalf[:, :], -0.5), writes=[C.b_const])
    b_i1, b_i2 = Buf("i1"), Buf("i2")
    C.b_if, C.b_ib = b_i1, b_i2
    P.dma("sp", lambda e: e.dma_start(out=C.ident_f[:, :], in_=C.t["c_ident_f"][:, :]), s_c, writes=[C.b_const])
    P.dma("sp", lambda e: e.dma_start(out=C.ident_b[:, :], in_=C.t["c_ident_b"][:, :]), s_c2, writes=[C.b_const])

    load_vectors(C)

    xin = C.t["x"]
    b_xin = [Buf("xin%d" % i) for i in range(NT)]
    scratch = {}

    def dram_act(name):
        if name not in scratch:
            scratch[name] = (nc.dram_tensor(name, [S, D], F32, kind="Internal"), [Buf(name + str(i)) for i in range(NT)])
        return scratch[name]

    cur, b_cur = xin, b_xin
    plist = list(phases)
    if dbg and dbg.startswith("aoT"):
        if ":" in dbg:
            C.dbg_sel = dbg.split(":")[1]
        aoT = nc.dram_tensor("dbg", [4, 128, S], BF16, kind="ExternalOutput")
    else:
        aoT = nc.dram_tensor("aoT", [4, 128, S], BF16, kind="Internal")
    b_ao = [Buf("ao%d" % k) for k in range(NT)]
    final_bufs = None
    for i, ph in enumerate(plist):
        last = (i == len(plist) - 1)
        if last:
            dst, b_dst = out, [Buf("out%d" % k) for k in range(NT)]
        else:
            dst, b_dst = dram_act("act%d" % i)
        if ph == "ffn0":
            ffn_phase(C, 0, cur, b_cur, dst, b_dst)
        elif ph == "ffn1":
            ffn_phase(C, 1, cur, b_cur, dst, b_dst)
        elif ph == "odd":
            odd_phase(C, cur, b_cur, dst, b_dst)
        elif ph == "evenB":
            if dbg == "boT":
                boT = nc.dram_tensor("dbg", [4, 128, S], BF16, kind="ExternalOutput")
                b_dst = [Buf("bo%d" % k) for k in range(NT)]
                evenB_phase(C, cur, b_cur, aoT, b_ao, dst, b_dst, boT=boT)
            else:
                evenB_phase(C, cur, b_cur, aoT, b_ao, dst, b_dst)
        elif ph == "evenA":
            evenA_phase(C, cur, b_cur, aoT, b_ao)
            final_bufs = b_ao
            continue
        else:
            raise ValueError(ph)
        cur, b_cur = dst, b_dst
        final_bufs = b_dst
    waits = P._deps("sp", final_bufs, ())
    P.q["sp"].append((None, waits, None))
    P.barrier()
    P.emit()
    return nc


_CONSTS = None


def consts():
    global _CONSTS
    if _CONSTS is None:
        eye = np.eye(128, dtype=np.float32)
        _CONSTS = {"c_ident_f": eye, "c_ident_b": eye.astype(ml_dtypes.bfloat16),
                   "c_causT": np.triu(np.ones((64, 64), dtype=np.float32))}
    return _CONSTS


LAST = {}


def run(inputs, phases, core_ids=tuple(range(8)), trace=False, dbg=None):
    nc = build(phases, dbg)
    shared = {k: np.ascontiguousarray(v, dtype=np.float32) for k, v in inputs.items() if k != "x"}
    shared.update(consts())
    x = np.asarray(inputs["x"], dtype=np.float32)
    in_maps = []
    for c in core_ids:
        m = dict(shared)
        m["x"] = np.ascontiguousarray(x[c])
        in_maps.append(m)
    res = run_bass_kernel_spmd(nc, in_maps, core_ids=list(core_ids), **({"trace": True} if trace else {}))
    LAST["res"] = res
    if dbg:
        return np.stack([r["dbg"] for r in res.results], axis=0)
    return np.stack([r["out"] for r in res.results], axis=0)


PHASES = ("evenA", "evenB", "ffn0", "odd", "ffn1")


def kernel(**inputs):
    return run(inputs, PHASES).astype(np.float32)
```

```python
import contextlib
import numpy as np
import ml_dtypes
import concourse.bass as bass
import concourse.mybir as mybir
from concourse.bass_utils import run_bass_kernel_spmd

F32 = mybir.dt.float32
BF16 = mybir.dt.bfloat16
ALU = mybir.AluOpType
AF = mybir.ActivationFunctionType
AX = mybir.AxisListType

S = 4096
D = 1024
DFF = 3072
TT = 512
NT = S // TT
EPS = 1e-6
EVEN_IN = 3272
LRU_W = 1280

ENGS = ("pe", "act", "dve", "pool", "sp")


class Buf:
    __slots__ = ("name", "w", "r")

    def __init__(self, name=""):
        self.name = name
        self.w = None
        self.r = []


class Prog:
    def __init__(self, nc):
        self.nc = nc
        self.q = {e: [] for e in ENGS}
        self.cnt = {e: 0 for e in ENGS}
        self.known = {e: {} for e in ENGS}
        self.dma_sems = []
        self.sem_handles = {}

    def _deps(self, eng, reads, writes):
        best = {}
        for b in reads:
            if b.w is not None:
                k, v = b.w
                if v > best.get(k, 0):
                    best[k] = v
        for b in writes:
            if b.w is not None:
                k, v = b.w
                if v > best.get(k, 0):
                    best[k] = v
            for (k, v) in b.r:
                if v > best.get(k, 0):
                    best[k] = v
        waits = []
        kn = self.known[eng]
        for k, v in best.items():
            if k == "pe" and eng == "pe":
                continue
            if kn.get(k, 0) >= v:
                continue
            kn[k] = v
            waits.append((k, v))
        return waits

    def _commit(self, tok, reads, writes):
        for b in reads:
            if len(b.r) > 24:
                m = {}
                for (k, v) in b.r:
                    if v > m.get(k, 0):
                        m[k] = v
                b.r = list(m.items())
            b.r.append(tok)
        for b in writes:
            b.w = tok
            b.r = []

    def op(self, eng, fn, reads=(), writes=()):
        waits = self._deps(eng, reads, writes)
        self.cnt[eng] += 1
        tok = (eng, self.cnt[eng])
        self.q[eng].append((fn, waits, (eng, 1)))
        self._commit(tok, reads, writes)
        return tok

    def new_dma_sem(self):
        key = "d%d" % (len(self.dma_sems) + 1)
        self.dma_sems.append(key)
        self.cnt[key] = 0
        return key

    def dma(self, eng, fn, sem, reads=(), writes=()):
        waits = self._deps(eng, reads, writes)
        self.cnt[sem] += 16
        tok = (sem, self.cnt[sem])
        self.q[eng].append((fn, waits, (sem, 16)))
        self._commit(tok, reads, writes)
        return tok

    def barrier(self, skip=()):
        for e in ENGS:
            waits = []
            kn = self.known[e]
            for k in list(ENGS) + self.dma_sems:
                if k in skip:
                    continue
                v = self.cnt[k]
                if v > kn.get(k, 0):
                    kn[k] = v
                    waits.append((k, v))
            self.q[e].append((None, waits, None))

    def emit(self):
        nc = self.nc
        with contextlib.ExitStack() as st:
            for k in list(ENGS) + self.dma_sems:
                self.sem_handles[k] = st.enter_context(nc.semaphore("s_" + k))
            block = st.enter_context(nc.Block())
            sh = self.sem_handles

            def run(e, eobj):
                for (fn, waits, inc) in self.q[e]:
                    for (k, v) in waits:
                        eobj.wait_ge(sh[k], v)
                    if fn is not None:
                        fn(eobj).then_inc(sh[inc[0]], inc[1])

            @block.tensor
            def _(eobj):
                run("pe", eobj)

            @block.scalar
            def _(eobj):
                run("act", eobj)

            @block.vector
            def _(eobj):
                run("dve", eobj)

            @block.gpsimd
            def _(eobj):
                run("pool", eobj)

            @block.sync
            def _(eobj):
                run("sp", eobj)


def _dsize(dt):
    return 4 if dt == F32 else 2


class Arena:
    def __init__(self, nc, nbytes):
        self.t = nc.alloc_sbuf_tensor("arena", [128, nbytes // 4], F32)
        self.nwords = nbytes // 4
        self.off = 0

    def reset(self, off=0):
        self.off = off

    def alloc(self, free_shape, dt):
        n = 1
        for s in free_shape:
            n *= s
        nw = (n * _dsize(dt) + 3) // 4
        nw = (nw + 7) // 8 * 8
        assert self.off + nw <= self.nwords, ("SBUF arena overflow", self.off, nw, self.nwords)
        ap = self.t[:, self.off:self.off + nw]
        self.off += nw
        if dt != F32:
            ap = ap.bitcast(dt)
        ap = ap[:, 0:n]
        if len(free_shape) == 2:
            ap = ap.rearrange("p (a b) -> p a b", a=free_shape[0])
        elif len(free_shape) == 3:
            ap = ap.rearrange("p (a b c) -> p a b c", a=free_shape[0], b=free_shape[1])
        return ap


class Ctx:
    pass


def vec_rows(C):
    rows = []

    def add(key, ap1d, n):
        rows.append((key, ap1d.rearrange("(c p) -> c p", p=128), n // 128))

    t = C.t
    for l in range(2):
        add(("ffn_norm", l), t["ffn_norm"][l, :], D)
        for j in range(3):
            add(("ffn_cw", l, j), t["ffn_conv_w"][l, j, :], 2 * DFF)
        add(("ffn_cb", l), t["ffn_conv_b"][l, :], 2 * DFF)
    add(("even_norm",), t["even_norm"][0, :], D)
    add(("odd_norm",), t["odd_norm"][0, :], D)
    add(("a_out_norm",), t["a_out_norm"][0, :], 512)
    for j in range(3):
        add(("lb", j), t["lb_logits"][j, :], 512)
    for j in range(4):
        add(("odd_cw", j), t["odd_conv_w"][0, j, :], LRU_W)
    add(("odd_cb",), t["odd_conv_b"][0, :], LRU_W)
    add(("odd_ba",), t["odd_gate_a_b"][0, :], LRU_W)
    add(("odd_bx",), t["odd_gate_x_b"][0, :], LRU_W)
    add(("odd_lam",), t["odd_lambda"][0, :], LRU_W)
    return rows


def load_vectors(C):
    P, nc = C.P, C.nc
    rows = vec_rows(C)
    total = sum(r[2] for r in rows)
    C.vt = C.persist.alloc([total], F32)
    C.b_vt = Buf("vt")
    C.vcol = {}
    stg = C.arena.alloc([128], F32)
    sem = P.new_dma_sem()
    st = {"col": 0, "cur": 0, "bufs": []}
    guard = Buf("stg_guard")

    def flush():
        n = st["cur"]
        if n == 0:
            return
        c0 = st["col"]
        ps = C.psum[0]
        P.op("pe", lambda e: e.transpose(out=ps[:, 0:n], in_=stg[0:n, :], identity=C.ident_f[0:n, 0:n]),
             reads=st["bufs"] + [C.b_const], writes=[C.b_psum[0], guard])
        P.op("dve", lambda e: e.tensor_copy(out=C.vt[:, c0:c0 + n], in_=ps[:, 0:n]),
             reads=[C.b_psum[0]], writes=[C.b_vt])
        st["col"] += n
        st["cur"] = 0

    slot_bufs = {}
    for key, ap, n in rows:
        r0 = 0
        C.vcol[key] = st["col"] + st["cur"]
        while r0 < n:
            cur = st["cur"]
            if cur == 0:
                st["bufs"] = []
            m = min(n - r0, 128 - cur)
            bb = slot_bufs.setdefault((cur, m), Buf("stg"))
            st["bufs"].append(bb)
            P.dma("sp", lambda e, ap=ap, a0=r0, m=m, c0=cur: e.dma_start(out=stg[c0:c0 + m, :], in_=ap[a0:a0 + m, :]),
                  sem, reads=[guard], writes=[bb])
            st["cur"] += m
            r0 += m
            if st["cur"] == 128:
                flush()
    flush()


def V(C, key, j=0):
    c = C.vcol[key] + j
    return C.vt[:, c:c + 1]


def ensure_prep(C):
    if not getattr(C, "prep_done", False):
        C.prep_done = True
        prep_weights(C)


def prep_some(C, n):
    ensure_prep(C)
    for _ in range(min(n, len(C.prep_queue))):
        C.prep_queue.pop(0)()


def prep_weights(C):
    P, nc, t = C.P, C.nc, C.t
    C.wb = {}
    C.b_wb = {}
    C.prep_queue = []
    sem_i = [0]
    sems = [P.new_dma_sem() for _ in range(4)]
    C.prep_sems = tuple(sems)

    def cast(name, dst, src):
        b = Buf(name)
        s = sems[sem_i[0] % 4]
        sem_i[0] += 1
        C.prep_queue.append(lambda: P.dma("pool", lambda e: e.dma_start(out=dst, in_=src), s, reads=[], writes=[b]))
        return b

    for l in range(2):
        wu = nc.dram_tensor("wup_b%d" % l, [24, 128, 8, 256], BF16, kind="Internal")
        src = t["ffn_w_up"][l].rearrange("(kc p) f -> p kc f", p=128)
        bl = []
        for g in range(12):
            for gv in range(2):
                base = gv * DFF + g * 256
                bl.append(cast("wup%d_%d_%d" % (l, g, gv), wu[g * 2 + gv], src[:, :, base:base + 256]))
        C.wb[("wup", l)] = wu
        C.b_wb[("wup", l)] = bl
        wd = nc.dram_tensor("wdn_b%d" % l, [128, 24, 1024], BF16, kind="Internal")
        srcd = t["ffn_w_down"][l].rearrange("(fc p) d -> p fc d", p=128)
        bl = []
        for q in range(4):
            bl.append(cast("wdn%d_%d" % (l, q), wd[:, q * 6:(q + 1) * 6, :], srcd[:, q * 6:(q + 1) * 6, :]))
        C.wb[("wdn", l)] = wd
        C.b_wb[("wdn", l)] = bl


def norm_transpose(C, xt, b_xt, hT, b_hT, gain_key, nblk, scr):
    P = C.P
    ss, rs, xn, junk = scr["ss"], scr["rs"], scr["xn"], scr["junk"]
    b_ss, b_xn, b_junk = scr["b_ss"], scr["b_xn"], scr["b_junk"]
    for b in range(nblk):
        P.op("act", lambda e, b=b: e.activation(out=junk[:, :], in_=xt[:, b, :], func=AF.Square,
                                                 accum_out=ss[:, b:b + 1]),
             reads=[b_xt], writes=[b_junk, b_ss])
    P.op("pool", lambda e: e.tensor_scalar(out=rs[:, 0:nblk], in0=ss[:, 0:nblk], scalar1=1.0 / D, scalar2=EPS, op0=ALU.mult, op1=ALU.add),
         reads=[b_ss], writes=[b_ss])
    P.op("pool", lambda e: e.tensor_tensor(out=rs[:, 0:nblk], in0=rs[:, 0:nblk], in1=C.neghalf[:, 0:nblk], op=ALU.pow),
         reads=[b_ss, C.b_const], writes=[b_ss])
    for b in range(nblk):
        P.op("dve", lambda e, b=b: e.tensor_scalar(out=xn[:, b, :], in0=xt[:, b, :], scalar1=rs[:, b:b + 1],
                                                    scalar2=None, op0=ALU.mult),
             reads=[b_xt, b_ss], writes=[b_xn[b]])
    for kc in range(8):
        pi = C.tp_rr % 2
        C.tp_rr += 1
        ps = C.psum_tp[pi]
        for b in range(nblk):
            P.op("pe", lambda e, b=b, kc=kc, ps=ps: e.transpose(out=ps[:, b * 128:(b + 1) * 128],
                                                                  in_=xn[:, b, kc * 128:(kc + 1) * 128],
                                                                  identity=C.ident_b[:, :]),
                 reads=[b_xn[b], C.b_const], writes=[C.b_psum_tp[pi]])
        P.op("act", lambda e, kc=kc, ps=ps: e.activation(out=hT[:, kc, 0:nblk * 128], in_=ps[:, 0:nblk * 128],
                                                          func=AF.Copy, scale=V(C, gain_key, kc)),
             reads=[C.b_psum_tp[pi], C.b_vt], writes=[b_hT])


def ffn_phase(C, l, src, b_src, dst, b_dst):
    P, nc, A = C.P, C.nc, C.arena
    prep_some(C, 1000)
    P.barrier()
    A.reset()
    NB = TT // 128
    wd = A.alloc([24, 1024], BF16)
    b_wd = [Buf("wd%d" % q) for q in range(4)]
    wu = [A.alloc([2, 8, 256], BF16) for _ in range(2)]
    b_wu = [[Buf("wu"), Buf("wu")] for _ in range(2)]
    xt = [A.alloc([NB, 1024], F32) for _ in range(2)]
    b_xt = [Buf("xt0"), Buf("xt1")]
    hT = A.alloc([8, TT], BF16)
    b_hT = Buf("hT")
    gT = A.alloc([24, TT], BF16)
    b_gT = [Buf("gT%d" % i) for i in range(24)]
    u_sb = [A.alloc([TT + 4], F32) for _ in range(4)]
    b_u = [Buf("u%d" % i) for i in range(4)]
    cb = [A.alloc([TT], F32) for _ in range(4)]
    b_c = [Buf("c%d" % i) for i in range(4)]
    halo = A.alloc([48, 2], F32)
    b_halo = [Buf("halo%d" % i) for i in range(48)]
    scr = dict(ss=A.alloc([8], F32), rs=A.alloc([8], F32), xn=A.alloc([NB, 1024], BF16), junk=A.alloc([1024], BF16),
               b_ss=Buf("ss"), b_xn=[Buf("xn%d" % i) for i in range(NB)], b_junk=Buf("junk"))
    s_wd = [P.new_dma_sem() for _ in range(4)]
    s_wu = [[P.new_dma_sem(), P.new_dma_sem()] for _ in range(2)]
    s_x = [P.new_dma_sem(), P.new_dma_sem()]
    s_o = [P.new_dma_sem(), P.new_dma_sem()]

    wdd = C.wb[("wdn", l)]
    for q in range(4):
        P.dma("pool", lambda e, q=q: e.dma_start(out=wd[:, q * 6:(q + 1) * 6, :], in_=wdd[:, q * 6:(q + 1) * 6, :]),
              s_wd[q], reads=[C.b_wb[("wdn", l)][q]], writes=[b_wd[q]])
    P.op("pool", lambda e: e.memset(halo[:, :, :], 0.0), writes=b_halo)

    wud = C.wb[("wup", l)]
    srcv = src.rearrange("(t b p) d -> t p b d", b=NB, p=128)
    dstv = dst.rearrange("(t b p) d -> t p b d", b=NB, p=128)

    def load_x(ti):
        sl = ti % 2
        P.dma("sp", lambda e: e.dma_start(out=xt[sl][:, :, :], in_=srcv[ti]), s_x[sl], reads=[b_src[ti]], writes=[b_xt[sl]])

    wu_seq = [(ti, g) for ti in range(NT) for g in range(12)]

    def load_wu(i):
        ti, g = wu_seq[i]
        sl = i % 2
        for gv in range(2):
            P.dma("pool", lambda e, gv=gv: e.dma_start(out=wu[sl][:, gv, :, :], in_=wud[g * 2 + gv]),
                  s_wu[sl][gv], reads=[C.b_wb[("wup", l)][g * 2 + gv]], writes=[b_wu[sl][gv]])

    load_x(0)
    load_wu(0)
    cwk, cbk = ("ffn_cw", l), ("ffn_cb", l)
    norm_transpose(C, xt[0], b_xt[0], hT, b_hT, ("ffn_norm", l), NB, scr)
    for ti in range(NT):
        sl = ti % 2
        if ti + 1 < NT:
            load_x(ti + 1)
        for g in range(12):
            i = ti * 12 + g
            if i + 1 < len(wu_seq):
                load_wu(i + 1)
            wsl = i % 2
            for j in range(2):
                c = g * 2 + j
                res = []
                for gv in range(2):
                    fc = c + 24 * gv
                    pi = C.mm_rr % 4
                    C.mm_rr += 1
                    ps = C.psum[pi]
                    for kc in range(8):
                        P.op("pe", lambda e, kc=kc, ps=ps, gv=gv, j=j, wsl=wsl: e.matmul(
                            out=ps[:, 0:TT], lhsT=wu[wsl][:, gv, kc, j * 128:(j + 1) * 128], rhs=hT[:, kc, :],
                            start=(kc == 0), stop=(kc == 7)),
                            reads=[b_wu[wsl][gv], b_hT], writes=[C.b_psum[pi]])
                    ui = C.u_rr % 4
                    C.u_rr += 1
                    u, bu, cc, bc = u_sb[ui], b_u[ui], cb[ui], b_c[ui]
                    P.op("pool", lambda e, u=u, fc=fc: e.tensor_copy(out=u[:, 0:2], in_=halo[:, fc, :]),
                         reads=[b_halo[fc]], writes=[bu])
                    P.op("act", lambda e, u=u, ps=ps: e.activation(out=u[:, 2:TT + 2], in_=ps[:, 0:TT], func=AF.Copy),
                         reads=[C.b_psum[pi]], writes=[bu])
                    P.op("pool", lambda e, u=u, fc=fc: e.tensor_copy(out=halo[:, fc, :], in_=u[:, TT:TT + 2]),
                         reads=[bu], writes=[b_halo[fc]])
                    P.op("dve", lambda e, u=u, cc=cc, fc=fc: e.tensor_scalar(
                        out=cc[:, :], in0=u[:, 2:TT + 2], scalar1=V(C, cwk + (2,), fc), scalar2=V(C, cbk, fc),
                        op0=ALU.mult, op1=ALU.add), reads=[bu, C.b_vt], writes=[bc])
                    P.op("dve", lambda e, u=u, cc=cc, fc=fc: e.scalar_tensor_tensor(
                        out=cc[:, :], in0=u[:, 1:TT + 1], scalar=V(C, cwk + (1,), fc), in1=cc[:, :],
                        op0=ALU.mult, op1=ALU.add), reads=[bu, bc, C.b_vt], writes=[bc])
                    P.op("dve", lambda e, u=u, cc=cc, fc=fc: e.scalar_tensor_tensor(
                        out=cc[:, :], in0=u[:, 0:TT], scalar=V(C, cwk + (0,), fc), in1=cc[:, :],
                        op0=ALU.mult, op1=ALU.add), reads=[bu, bc, C.b_vt], writes=[bc])
                    res.append((cc, bc))
                (cg, bcg), (cv, bcv) = res
                P.op("act", lambda e, cg=cg: e.activation(out=cg[:, :], in_=cg[:, :], func=AF.Gelu_apprx_tanh),
                     reads=[bcg], writes=[bcg])
                P.op("dve", lambda e, cg=cg, cv=cv, c=c: e.tensor_tensor(out=gT[:, c, :], in0=cg[:, :], in1=cv[:, :], op=ALU.mult),
                     reads=[bcg, bcv], writes=[b_gT[c]])
        if ti + 1 < NT:
            norm_transpose(C, xt[1 - sl], b_xt[1 - sl], hT, b_hT, ("ffn_norm", l), NB, scr)
        for b in range(NB):
            for hf in range(2):
                pi = 4 + (C.dn_rr % 2)
                C.dn_rr += 1
                ps = C.psum[pi]
                for fc in range(24):
                    P.op("pe", lambda e, fc=fc, ps=ps, b=b, hf=hf, sl=sl: e.matmul(
                        out=ps[:, 0:512], lhsT=gT[:, fc, b * 128:(b + 1) * 128], rhs=wd[:, fc, hf * 512:(hf + 1) * 512],
                        start=(fc == 0), stop=(fc == 23)),
                        reads=[b_gT[fc], b_wd[fc // 6]], writes=[C.b_psum[pi]])
                P.op("dve", lambda e, ps=ps, b=b, hf=hf, sl=sl: e.tensor_tensor(
                    out=xt[sl][:, b, hf * 512:(hf + 1) * 512], in0=ps[:, 0:512], in1=xt[sl][:, b, hf * 512:(hf + 1) * 512],
                    op=ALU.add), reads=[C.b_psum[pi], b_xt[sl]], writes=[b_xt[sl]])
        P.dma("sp", lambda e, ti=ti, sl=sl: e.dma_start(out=dstv[ti], in_=xt[sl][:, :, :]), s_o[sl], reads=[b_xt[sl]], writes=[b_dst[ti]])


class Rot:
    def __init__(self, A, n, free_shape, dt, name="rot"):
        self.aps = [A.alloc(free_shape, dt) for _ in range(n)]
        self.bufs = [Buf("%s%d" % (name, i)) for i in range(n)]
        self.i = 0

    def get(self):
        k = self.i % len(self.aps)
        self.i += 1
        return self.aps[k], self.bufs[k]


def mm_psum(C):
    pi = C.mm_rr % 4
    C.mm_rr += 1
    return C.psum[pi], C.b_psum[pi]


def odd_phase(C, src, b_src, dst, b_dst):
    P, nc, A, t = C.P, C.nc, C.arena, C.t
    P.barrier()
    A.reset()
    NB = TT // 128
    NBK = 10
    w_in = A.alloc([8, 2 * LRU_W], BF16)
    b_win = [Buf("win%d" % i) for i in range(8)]
    w_out = A.alloc([NBK, D], BF16)
    b_wout = [Buf("wout0"), Buf("wout1")]
    wa = A.alloc([NBK, 128], BF16)
    wx = A.alloc([NBK, 128], BF16)
    b_wa, b_wx = Buf("wa"), Buf("wx")
    xt = [A.alloc([NB, 1024], F32) for _ in range(2)]
    b_xt = [Buf("xt0"), Buf("xt1")]
    hT = A.alloc([8, TT], BF16)
    b_hT = Buf("hT")
    mixT = A.alloc([NBK, TT], BF16)
    b_mix = [Buf("mix%d" % i) for i in range(NBK)]
    halo = A.alloc([NBK, 4], F32)
    b_halo = [Buf("halo%d" % i) for i in range(NBK)]
    hprev = A.alloc([NBK, 2], F32)
    b_hprev = [Buf("hprev%d" % i) for i in range(NBK)]
    Lt = A.alloc([NBK], F32)
    hb = A.alloc([2, NBK], F32)
    b_L = Buf("L")
    halfc = A.alloc([TT], F32)
    b_half = Buf("half")
    scr = dict(ss=A.alloc([8], F32), rs=A.alloc([8], F32), xn=A.alloc([NB, 1024], BF16), junk=A.alloc([1024], BF16),
               b_ss=Buf("ss"), b_xn=[Buf("xn%d" % i) for i in range(NB)], b_junk=Buf("junk"))
    r_y = Rot(A, 3, [TT], F32, "y")
    r_u = Rot(A, 2, [TT + 4], F32, "u")
    r_xc = Rot(A, 3, [TT], F32, "xc")
    r_xcb = Rot(A, 3, [TT], BF16, "xcb")
    r_r = Rot(A, 2, [TT], F32, "r")
    r_gi = Rot(A, 2, [TT], F32, "gi")
    r_a = Rot(A, 2, [TT], F32, "a")
    r_sq = Rot(A, 2, [TT], F32, "sq")
    r_hs = Rot(A, 2, [TT], F32, "hs")
    sems = [P.new_dma_sem() for _ in range(14)]
    s_x = [P.new_dma_sem(), P.new_dma_sem()]
    s_o = [P.new_dma_sem(), P.new_dma_sem()]

    src_in = t["odd_w_in"][0].rearrange("(kc p) f -> p kc f", p=128)
    for kc in range(8):
        P.dma("pool", lambda e, kc=kc: e.dma_start(out=w_in[:, kc, :], in_=src_in[:, kc, :]), sems[kc], writes=[b_win[kc]])
    P.dma("pool", lambda e: e.dma_start(out=wa[:, :, :], in_=t["odd_gate_a_w"][0].rearrange("n c d -> c n d")),
          sems[8], writes=[b_wa])
    P.dma("pool", lambda e: e.dma_start(out=wx[:, :, :], in_=t["odd_gate_x_w"][0].rearrange("n c d -> c n d")),
          sems[9], writes=[b_wx])
    src_out = t["odd_w_out"][0].rearrange("(n p) d -> p n d", p=128)
    for q in range(2):
        P.dma("pool", lambda e, q=q: e.dma_start(out=w_out[:, q * 5:(q + 1) * 5, :], in_=src_out[:, q * 5:(q + 1) * 5, :]),
              sems[10 + q], writes=[b_wout[q]])
    P.op("pool", lambda e: e.memset(halo[:, :, :], 0.0), writes=b_halo)
    P.op("pool", lambda e: e.memset(hprev[:, :, :], 0.0), writes=b_hprev)
    P.op("pool", lambda e: e.memset(halfc[:, :], 0.25), writes=[b_half])
    lamc0 = C.vcol[("odd_lam",)]
    P.op("act", lambda e, lamc0=lamc0: e.activation(out=Lt[:, :], in_=C.vt[:, lamc0:lamc0 + NBK], func=AF.Exp, scale=-1.0),
         reads=[C.b_vt], writes=[b_L])
    P.op("act", lambda e: e.activation(out=Lt[:, :], in_=Lt[:, :], func=AF.Ln, bias=C.one_t[:, 0:1]),
         reads=[b_L, C.b_const], writes=[b_L])
    P.op("dve", lambda e: e.tensor_scalar(out=Lt[:, :], in0=Lt[:, :], scalar1=-4.0, scalar2=None, op0=ALU.mult), reads=[b_L], writes=[b_L])
    ca, cx = C.vcol[("odd_ba",)], C.vcol[("odd_bx",)]
    P.op("dve", lambda e, ca=ca: e.tensor_scalar(out=hb[:, 0, :], in0=C.vt[:, ca:ca + NBK], scalar1=0.5, scalar2=None, op0=ALU.mult),
         reads=[C.b_vt], writes=[b_L])
    P.op("dve", lambda e, cx=cx: e.tensor_scalar(out=hb[:, 1, :], in0=C.vt[:, cx:cx + NBK], scalar1=0.5, scalar2=None, op0=ALU.mult),
         reads=[C.b_vt], writes=[b_L])

    srcv = src.rearrange("(t b p) d -> t p b d", b=NB, p=128)
    dstv = dst.rearrange("(t b p) d -> t p b d", b=NB, p=128)

    def load_x(ti):
        sl = ti % 2
        P.dma("sp", lambda e: e.dma_start(out=xt[sl][:, :, :], in_=srcv[ti]), s_x[sl], reads=[b_src[ti]], writes=[b_xt[sl]])

    def S1(n):
        ps, bps = mm_psum(C)
        for kc in range(8):
            P.op("pe", lambda e, kc=kc: e.matmul(out=ps[:, 0:TT], lhsT=w_in[:, kc, n * 128:(n + 1) * 128],
                                                 rhs=hT[:, kc, :], start=(kc == 0), stop=(kc == 7)),
                 reads=[b_win[kc], b_hT], writes=[bps])
        y, by = r_y.get()
        P.op("act", lambda e: e.activation(out=y[:, :], in_=ps[:, 0:TT], func=AF.Gelu_apprx_tanh), reads=[bps], writes=[by])
        ps2, bps2 = mm_psum(C)
        for kc in range(8):
            P.op("pe", lambda e, kc=kc: e.matmul(out=ps2[:, 0:TT], lhsT=w_in[:, kc, LRU_W + n * 128:LRU_W + (n + 1) * 128],
                                                 rhs=hT[:, kc, :], start=(kc == 0), stop=(kc == 7)),
                 reads=[b_win[kc], b_hT], writes=[bps2])
        u, bu = r_u.get()
        P.op("pool", lambda e: e.tensor_copy(out=u[:, 0:3], in_=halo[:, n, 0:3]), reads=[b_halo[n]], writes=[bu])
        P.op("act", lambda e: e.activation(out=u[:, 3:TT + 3], in_=ps2[:, 0:TT], func=AF.Copy), reads=[bps2], writes=[bu])
        P.op("pool", lambda e: e.tensor_copy(out=halo[:, n, 0:3], in_=u[:, TT:TT + 3]), reads=[bu], writes=[b_halo[n]])
        xc, bxc = r_xc.get()
        P.op("dve", lambda e: e.tensor_scalar(out=xc[:, :], in0=u[:, 3:TT + 3], scalar1=V(C, ("odd_cw", 3), n),
                                              scalar2=V(C, ("odd_cb",), n), op0=ALU.mult, op1=ALU.add),
             reads=[bu, C.b_vt], writes=[bxc])
        for j in range(3):
            P.op("dve", lambda e, j=j: e.scalar_tensor_tensor(out=xc[:, :], in0=u[:, j:TT + j], scalar=V(C, ("odd_cw", j), n),
                                                               in1=xc[:, :], op0=ALU.mult, op1=ALU.add),
                 reads=[bu, bxc, C.b_vt], writes=[bxc])
        xcb, bxcb = r_xcb.get()
        P.op("act", lambda e: e.activation(out=xcb[:, :], in_=xc[:, :], func=AF.Copy), reads=[bxc], writes=[bxcb])
        return dict(y=y, by=by, xc=xc, bxc=bxc, xcb=xcb, bxcb=bxcb)

    def S2(n, st):
        y, by, xc, bxc, xcb, bxcb = st["y"], st["by"], st["xc"], st["bxc"], st["xcb"], st["bxcb"]
        psa, bpsa = mm_psum(C)
        P.op("pe", lambda e: e.matmul(out=psa[:, 0:TT], lhsT=wa[:, n, :], rhs=xcb[:, :], start=True, stop=True),
             reads=[b_wa, bxcb], writes=[bpsa])
        psg, bpsg = mm_psum(C)
        P.op("pe", lambda e: e.matmul(out=psg[:, 0:TT], lhsT=wx[:, n, :], rhs=xcb[:, :], start=True, stop=True),
             reads=[b_wx, bxcb], writes=[bpsg])
        r, br = r_r.get()
        gi, bgi = r_gi.get()
        P.op("act", lambda e: e.activation(out=r[:, :], in_=psa[:, 0:TT], func=AF.Tanh, scale=0.5, bias=hb[:, 0, n:n + 1]),
             reads=[bpsa, b_L], writes=[br])
        P.op("act", lambda e: e.activation(out=gi[:, :], in_=psg[:, 0:TT], func=AF.Tanh, scale=0.5, bias=hb[:, 1, n:n + 1]),
             reads=[bpsg, b_L], writes=[bgi])
        a, ba = r_a.get()
        sq, bsq = r_sq.get()
        P.op("act", lambda e: e.activation(out=a[:, :], in_=r[:, :], func=AF.Exp, scale=Lt[:, n:n + 1], bias=Lt[:, n:n + 1]),
             reads=[br, b_L], writes=[ba])
        P.op("pool", lambda e: e.tensor_tensor(out=sq[:, :], in0=a[:, :], in1=a[:, :], op=ALU.mult), reads=[ba], writes=[bsq])
        P.op("act", lambda e: e.activation(out=sq[:, :], in_=sq[:, :], func=AF.Sqrt, scale=-0.25, bias=halfc[:, 0:1]),
             reads=[bsq, b_half], writes=[bsq])
        P.op("dve", lambda e: e.scalar_tensor_tensor(out=gi[:, :], in0=gi[:, :], scalar=1.0, in1=xc[:, :], op0=ALU.add, op1=ALU.mult),
             reads=[bgi, bxc], writes=[bgi])
        P.op("dve", lambda e: e.tensor_tensor(out=gi[:, :], in0=gi[:, :], in1=sq[:, :], op=ALU.mult), reads=[bgi, bsq], writes=[bgi])
        hs, bhs = r_hs.get()
        P.op("dve", lambda e: e.tensor_tensor_scan(out=hs[:, :], data0=a[:, :], data1=gi[:, :], initial=hprev[:, n, 0:1],
                                                   op0=ALU.mult, op1=ALU.add), reads=[ba, bgi, b_hprev[n]], writes=[bhs])
        P.op("pool", lambda e: e.tensor_copy(out=hprev[:, n, 0:1], in_=hs[:, TT - 1:TT]), reads=[bhs], writes=[b_hprev[n]])
        P.op("dve", lambda e: e.tensor_tensor(out=mixT[:, n, :], in0=hs[:, :], in1=y[:, :], op=ALU.mult),
             reads=[bhs, by], writes=[b_mix[n]])

    load_x(0)
    norm_transpose(C, xt[0], b_xt[0], hT, b_hT, ("odd_norm",), NB, scr)
    for ti in range(NT):
        sl = ti % 2
        if ti + 1 < NT:
            load_x(ti + 1)
        prev = None
        for n in range(NBK):
            st = S1(n)
            if prev is not None:
                S2(*prev)
            prev = (n, st)
        S2(*prev)
        if ti + 1 < NT:
            norm_transpose(C, xt[1 - sl], b_xt[1 - sl], hT, b_hT, ("odd_norm",), NB, scr)
        for b in range(NB):
            for hf in range(2):
                pi = 4 + (C.dn_rr % 2)
                C.dn_rr += 1
                ps = C.psum[pi]
                for n in range(NBK):
                    P.op("pe", lambda e, n=n, ps=ps, b=b, hf=hf: e.matmul(
                        out=ps[:, 0:512], lhsT=mixT[:, n, b * 128:(b + 1) * 128], rhs=w_out[:, n, hf * 512:(hf + 1) * 512],
                        start=(n == 0), stop=(n == NBK - 1)),
                        reads=[b_mix[n], b_wout[n // 5]], writes=[C.b_psum[pi]])
                P.op("dve", lambda e, ps=ps, b=b, hf=hf, sl=sl: e.tensor_tensor(
                    out=xt[sl][:, b, hf * 512:(hf + 1) * 512], in0=ps[:, 0:512], in1=xt[sl][:, b, hf * 512:(hf + 1) * 512],
                    op=ALU.add), reads=[C.b_psum[pi], b_xt[sl]], writes=[b_xt[sl]])
        P.dma("sp", lambda e, ti=ti, sl=sl: e.dma_start(out=dstv[ti], in_=xt[sl][:, :, :]), s_o[sl],
              reads=[b_xt[sl]], writes=[b_dst[ti]])


def evenA_phase(C, src, b_src, aoT, b_ao):
    P, nc, A, t = C.P, C.nc, C.arena, C.t
    P.barrier()
    A.reset()
    NB = TT // 128
    NCH = TT // 64
    w = A.alloc([8, 2048], BF16)
    b_w = [Buf("w%d" % i) for i in range(8)]
    xt = [A.alloc([NB, 1024], F32) for _ in range(2)]
    b_xt = [Buf("xt0"), Buf("xt1")]
    hT = A.alloc([8, TT], BF16)
    b_hT = Buf("hT")
    scr = dict(ss=A.alloc([8], F32), rs=A.alloc([8], F32), xn=A.alloc([NB, 1024], BF16), junk=A.alloc([1024], BF16),
               b_ss=Buf("ss"), b_xn=[Buf("xn%d" % i) for i in range(NB)], b_junk=Buf("junk"))
    lbt = A.alloc([3, 4], F32)
    lb = A.alloc([4], F32)
    oml = A.alloc([4], F32)
    b_lb = Buf("lb")
    rmask = A.alloc([TT], F32)
    b_rmask = Buf("rmask")
    ones_b = A.alloc([128], BF16)
    causT = A.alloc([64], F32)
    b_cm = Buf("cm")
    St = A.alloc([4, 128], F32)
    Sb = A.alloc([4, 128], BF16)
    b_S = [Buf("S%d" % h) for h in range(4)]
    b_Sb = [Buf("Sb%d" % h) for h in range(4)]
    vtok = A.alloc([NCH, 512], BF16)
    b_vtok = [Buf("vtok%d" % c) for c in range(NCH)]
    eb = A.alloc([4, TT], F32)
    b_eb = [Buf("eb%d" % h) for h in range(4)]
    kf = A.alloc([4, TT], F32)
    b_kf = [Buf("kf%d" % h) for h in range(4)]
    ktb = A.alloc([4, TT], BF16)
    b_ktb = [Buf("ktb%d" % h) for h in range(4)]
    qtb = A.alloc([4, TT], BF16)
    b_qtb = [Buf("qtb%d" % h) for h in range(4)]
    sg = A.alloc([4, TT], F32)
    b_sg = [Buf("sg%d" % h) for h in range(4)]
    osb = A.alloc([4, TT], F32)
    b_osb = [Buf("osb%d" % h) for h in range(4)]
    ao = [A.alloc([4, TT], BF16) for _ in range(2)]
    b_aot = [Buf("ao0"), Buf("ao1")]
    r_f = Rot(A, 2, [TT], F32, "f")
    r_g = Rot(A, 2, [TT], F32, "g")
    r_b = Rot(A, 2, [TT], F32, "b")
    r_enb = Rot(A, 2, [TT], F32, "enb")
    r_qs = Rot(A, 2, [TT], F32, "qs")
    r_khb = Rot(A, 4, [64], BF16, "khb")
    r_kht = Rot(A, 4, [128], BF16, "kht")
    r_pt = Rot(A, 4, [64], BF16, "pt")
    r_osq = Rot(A, 2, [TT], BF16, "osq")
    r_rt = Rot(A, 2, [TT], F32, "rt")
    sems = [P.new_dma_sem() for _ in range(8)]
    s_x = [P.new_dma_sem(), P.new_dma_sem()]
    s_o = [P.new_dma_sem(), P.new_dma_sem()]
    s_m = P.new_dma_sem()

    src_in = t["even_w_in"][0].rearrange("(kc p) f -> p kc f", p=128)
    for kc in range(8):
        P.dma("pool", lambda e, kc=kc: e.dma_start(out=w[:, kc, :], in_=src_in[:, kc, 0:2048]), sems[kc], writes=[b_w[kc]])
    P.dma("sp", lambda e: e.dma_start(out=causT[0:64, :], in_=t["c_causT"][:, :]), s_m, writes=[b_cm])
    lbc0 = C.vcol[("lb", 0)]
    P.op("act", lambda e, lbc0=lbc0: e.activation(out=lbt[:, :, :].rearrange("p a b -> p (a b)"), in_=C.vt[:, lbc0:lbc0 + 12], func=AF.Exp),
         reads=[C.b_vt], writes=[b_lb])
    P.op("dve", lambda e: e.tensor_tensor(out=oml[:, :], in0=lbt[:, 0, :], in1=lbt[:, 1, :], op=ALU.add), reads=[b_lb], writes=[b_lb])
    P.op("dve", lambda e: e.tensor_tensor(out=oml[:, :], in0=oml[:, :], in1=lbt[:, 2, :], op=ALU.add), reads=[b_lb], writes=[b_lb])
    P.op("dve", lambda e: e.reciprocal(out=oml[:, :], in_=oml[:, :]), reads=[b_lb], writes=[b_lb])
    P.op("dve", lambda e: e.tensor_tensor(out=lb[:, :], in0=lbt[:, 0, :], in1=oml[:, :], op=ALU.mult), reads=[b_lb], writes=[b_lb])
    P.op("dve", lambda e: e.tensor_scalar(out=oml[:, :], in0=lb[:, :], scalar1=-1.0, scalar2=1.0, op0=ALU.mult, op1=ALU.add),
         reads=[b_lb], writes=[b_lb])
    P.op("pool", lambda e: e.memset(rmask[:, :], 1.0), writes=[b_rmask])
    P.op("pool", lambda e: e.memset(rmask[:, :].rearrange("p (c t) -> p c t", t=64)[:, :, 0:1], 0.0), writes=[b_rmask])
    P.op("pool", lambda e: e.memset(ones_b[:, :], 1.0), writes=[b_rmask])
    P.op("pool", lambda e: e.memset(St[:, :, :], 0.0), writes=b_S)
    P.op("pool", lambda e: e.memset(Sb[:, :, :], 0.0), writes=b_Sb)

    srcv = src.rearrange("(t b p) d -> t p b d", b=NB, p=128)
    aov = aoT.rearrange("h p s -> p h s")

    def load_x(ti):
        sl = ti % 2
        P.dma("sp", lambda e: e.dma_start(out=xt[sl][:, :, :], in_=srcv[ti]), s_x[sl], reads=[b_src[ti]], writes=[b_xt[sl]])

    def proj(col0, M=128):
        ps, bps = mm_psum(C)
        for kc in range(8):
            P.op("pe", lambda e, kc=kc, ps=ps: e.matmul(out=ps[0:M, 0:TT], lhsT=w[:, kc, col0:col0 + M], rhs=hT[:, kc, :],
                                                        start=(kc == 0), stop=(kc == 7)),
                 reads=[b_w[kc], b_hT], writes=[bps])
        return ps, bps

    load_x(0)
    for ti in range(NT):
        sl = ti % 2
        if ti + 1 < NT:
            load_x(ti + 1)
        norm_transpose(C, xt[sl], b_xt[sl], hT, b_hT, ("even_norm",), NB, scr)
        for c in range(NCH):
            ps, bps = mm_psum(C)
            for kc in range(8):
                P.op("pe", lambda e, kc=kc, ps=ps, c=c: e.matmul(out=ps[0:64, 0:512], lhsT=hT[:, kc, c * 64:(c + 1) * 64],
                                                                  rhs=w[:, kc, 1024:1536], start=(kc == 0), stop=(kc == 7)),
                     reads=[b_w[kc], b_hT], writes=[bps])
            P.op("act", lambda e, ps=ps, c=c: e.activation(out=vtok[0:64, c, :], in_=ps[0:64, 0:512], func=AF.Copy),
                 reads=[bps], writes=[b_vtok[c]])
        for h in range(4):
            ps, bps = proj(512 + h * 128)
            f, bf = r_f.get()
            P.op("act", lambda e, f=f, ps=ps: e.activation(out=f[:, :], in_=ps[:, 0:TT], func=AF.Sigmoid), reads=[bps], writes=[bf])
            P.op("dve", lambda e, f=f, h=h: e.tensor_scalar(out=f[:, :], in0=f[:, :], scalar1=oml[:, h:h + 1], scalar2=lb[:, h:h + 1],
                                                            op0=ALU.mult, op1=ALU.add), reads=[bf, b_lb], writes=[bf])
            g, bg = r_g.get()
            P.op("act", lambda e, f=f, g=g: e.activation(out=g[:, :], in_=f[:, :], func=AF.Ln), reads=[bf], writes=[bg])
            P.op("dve", lambda e, f=f: e.tensor_scalar(out=f[:, :], in0=f[:, :], scalar1=-1.0, scalar2=1.0, op0=ALU.mult, op1=ALU.add),
                 reads=[bf], writes=[bf])
            bb, bbb = r_b.get()
            P.op("dve", lambda e, bb=bb, g=g: e.tensor_tensor_scan(out=bb[:, :], data0=rmask[:, :], data1=g[:, :], initial=0.0,
                                                                   op0=ALU.mult, op1=ALU.add), reads=[bg, b_rmask], writes=[bbb])
            P.op("act", lambda e, bb=bb, h=h: e.activation(out=eb[:, h, :], in_=bb[:, :], func=AF.Exp), reads=[bbb], writes=[b_eb[h]])
            enb, benb = r_enb.get()
            P.op("act", lambda e, bb=bb, enb=enb: e.activation(out=enb[:, :], in_=bb[:, :], func=AF.Exp, scale=-1.0),
                 reads=[bbb], writes=[benb])
            P.op("dve", lambda e, f=f, enb=enb, h=h: e.tensor_tensor(out=kf[:, h, :], in0=f[:, :], in1=enb[:, :], op=ALU.mult),
                 reads=[bf, benb], writes=[b_kf[h]])
            P.op("pool", lambda e, h=h: e.tensor_copy(out=ktb[:, h, :], in_=kf[:, h, :]), reads=[b_kf[h]], writes=[b_ktb[h]])
            ps, bps = proj(h * 128)
            qs, bqs = r_qs.get()
            P.op("act", lambda e, qs=qs, ps=ps: e.activation(out=qs[:, :], in_=ps[:, 0:TT], func=AF.Silu), reads=[bps], writes=[bqs])
            P.op("dve", lambda e, qs=qs, h=h: e.tensor_tensor(out=qtb[:, h, :], in0=qs[:, :], in1=eb[:, h, :], op=ALU.mult),
                 reads=[bqs, b_eb[h]], writes=[b_qtb[h]])
            ps, bps = proj(1536 + h * 128)
            P.op("act", lambda e, ps=ps, h=h: e.activation(out=sg[:, h, :], in_=ps[:, 0:TT], func=AF.Silu), reads=[bps], writes=[b_sg[h]])
        for c in range(NCH):
            c0, c1, last = c * 64, (c + 1) * 64, c * 64 + 63
            for h in range(4):
                khb, bkhb = r_khb.get()
                P.op("dve", lambda e, khb=khb, h=h, c0=c0, c1=c1, last=last: e.tensor_scalar(
                    out=khb[:, :], in0=kf[:, h, c0:c1], scalar1=eb[:, h, last:last + 1], scalar2=None, op0=ALU.mult),
                    reads=[b_kf[h], b_eb[h]], writes=[bkhb])
                pi = C.tp_rr % 2
                C.tp_rr += 1
                pst = C.psum_tp[pi]
                P.op("pe", lambda e, pst=pst, khb=khb: e.transpose(out=pst[0:64, 0:128], in_=khb[:, :], identity=C.ident_b[:, :]),
                     reads=[bkhb, C.b_const], writes=[C.b_psum_tp[pi]])
                kht, bkht = r_kht.get()
                P.op("act", lambda e, pst=pst, kht=kht: e.activation(out=kht[0:64, :], in_=pst[0:64, 0:128], func=AF.Copy),
                     reads=[C.b_psum_tp[pi]], writes=[bkht])
                ps, bps = mm_psum(C)
                P.op("pe", lambda e, ps=ps, h=h, c0=c0, c1=c1: e.matmul(out=ps[0:64, 0:64], lhsT=ktb[:, h, c0:c1], rhs=qtb[:, h, c0:c1],
                                                                        start=True, stop=True),
                     reads=[b_ktb[h], b_qtb[h]], writes=[bps])
                pt, bpt = r_pt.get()
                P.op("dve", lambda e, ps=ps, pt=pt: e.tensor_tensor(out=pt[0:64, :], in0=ps[0:64, 0:64], in1=causT[0:64, :], op=ALU.mult),
                     reads=[bps, b_cm], writes=[bpt])
                pso, bpso = mm_psum(C)
                P.op("pe", lambda e, pso=pso, h=h, c0=c0, c1=c1: e.matmul(out=pso[:, 0:64], lhsT=Sb[:, h, :], rhs=qtb[:, h, c0:c1],
                                                                          start=True, stop=False),
                     reads=[b_Sb[h], b_qtb[h]], writes=[bpso])
                P.op("pe", lambda e, pso=pso, h=h, c=c, pt=pt: e.matmul(out=pso[:, 0:64], lhsT=vtok[0:64, c, h * 128:(h + 1) * 128],
                                                                        rhs=pt[0:64, :], start=False, stop=True),
                     reads=[b_vtok[c], bpt], writes=[bpso])
                P.op("act", lambda e, pso=pso, h=h, c0=c0, c1=c1: e.activation(out=osb[:, h, c0:c1], in_=pso[:, 0:64], func=AF.Copy),
                     reads=[bpso], writes=[b_osb[h]])
                psu, bpsu = mm_psum(C)
                P.op("pe", lambda e, psu=psu, kht=kht, h=h, c=c: e.matmul(out=psu[:, 0:128], lhsT=kht[0:64, :],
                                                                          rhs=vtok[0:64, c, h * 128:(h + 1) * 128], start=True, stop=True),
                     reads=[bkht, b_vtok[c]], writes=[bpsu])
                P.op("dve", lambda e, psu=psu, h=h, last=last: e.scalar_tensor_tensor(
                    out=St[:, h, :], in0=St[:, h, :], scalar=eb[:, h, last:last + 1], in1=psu[:, 0:128], op0=ALU.mult, op1=ALU.add),
                    reads=[b_S[h], b_eb[h], bpsu], writes=[b_S[h]])
                P.op("pool", lambda e, h=h: e.tensor_copy(out=Sb[:, h, :], in_=St[:, h, :]), reads=[b_S[h]], writes=[b_Sb[h]])
        aot, baot = ao[sl], b_aot[sl]
        for h in range(4):
            osq, bosq = r_osq.get()
            P.op("act", lambda e, osq=osq, h=h: e.activation(out=osq[:, :], in_=osb[:, h, :], func=AF.Square), reads=[b_osb[h]], writes=[bosq])
            ps, bps = mm_psum(C)
            P.op("pe", lambda e, ps=ps, osq=osq: e.matmul(out=ps[:, 0:TT], lhsT=ones_b[:, :], rhs=osq[:, :], start=True, stop=True),
                 reads=[bosq, b_rmask], writes=[bps])
            rt, brt = r_rt.get()
            P.op("act", lambda e, ps=ps, rt=rt: e.activation(out=rt[:, :], in_=ps[:, 0:TT], func=AF.Sqrt, scale=1.0 / 128, bias=C.eps_t[:, 0:1]),
                 reads=[bps, C.b_const], writes=[brt])
            P.op("dve", lambda e, rt=rt: e.reciprocal(out=rt[:, :], in_=rt[:, :]), reads=[brt], writes=[brt])
            P.op("dve", lambda e, rt=rt, h=h: e.tensor_tensor(out=rt[:, :], in0=rt[:, :], in1=osb[:, h, :], op=ALU.mult),
                 reads=[brt, b_osb[h]], writes=[brt])
            P.op("dve", lambda e, rt=rt, h=h, aot=aot: e.scalar_tensor_tensor(
                out=aot[:, h, :], in0=rt[:, :], scalar=V(C, ("a_out_norm",), h), in1=sg[:, h, :], op0=ALU.mult, op1=ALU.mult),
                reads=[brt, b_sg[h], C.b_vt], writes=[baot])
            dsel = getattr(C, "dbg_sel", None)
            if dsel is not None:
                srcs = {"lbd": (eb, b_eb), "eb": (eb, b_eb), "osb": (osb, b_osb), "sg": (sg, b_sg), "qtb": (qtb, b_qtb), "ktb": (ktb, b_ktb)}
                sa, sb = srcs[dsel]
                P.op("dve", lambda e, h=h, aot=aot, sa=sa: e.tensor_copy(out=aot[:, h, :], in_=sa[:, h, :]),
                     reads=[sb[h], baot], writes=[baot])
        if getattr(C, "dbg_sel", None) == "lbd":
            P.op("dve", lambda e, aot=aot: e.tensor_copy(out=aot[:, 0, 0:4], in_=lb[:, :]), reads=[b_lb, baot], writes=[baot])
            P.op("dve", lambda e, aot=aot: e.tensor_copy(out=aot[:, 0, 4:8], in_=oml[:, :]), reads=[b_lb, baot], writes=[baot])
            P.op("dve", lambda e, aot=aot: e.tensor_copy(out=aot[:, 0, 8:20], in_=lbt[:, :, :].rearrange("p a b -> p (a b)")), reads=[b_lb, baot], writes=[baot])
            P.op("dve", lambda e, aot=aot: e.tensor_copy(out=aot[:, 0, 20:32], in_=C.vt[:, lbc0:lbc0 + 12]), reads=[C.b_vt, baot], writes=[baot])
        P.dma("sp", lambda e, ti=ti, aot=aot: e.dma_start(out=aov[:, :, ti * TT:(ti + 1) * TT], in_=aot[:, :, :]), s_o[sl],
              reads=[baot], writes=[b_ao[ti]])


NIT = 12
TOPK = 256
NEG = -1.0e30
MBIG = 30000.0


def _interleave(gens):
    items = []
    for g, n in gens:
        items.append([g, max(n, 1), 0, True])
    total = max(it[1] for it in items) if items else 0
    for step in range(1, total + 1):
        for it in items:
            g, n, done, alive = it
            want = (step * n + total - 1) // total
            while alive and it[2] < want:
                try:
                    next(g)
                    it[2] += 1
                except StopIteration:
                    it[3] = False
                    alive = False
    for it in items:
        if it[3]:
            for _ in it[0]:
                pass


def evenB_phase(C, src, b_src, aoT, b_ao, dst, b_dst, boT=None):
    P, nc, A, t = C.P, C.nc, C.arena, C.t
    P.barrier(skip=getattr(C, "prep_sems", ()))
    A.reset()
    NB = TT // 128
    NKT = S // 128
    wq = A.alloc([8, 512], BF16)
    wiq = A.alloc([8, 512], BF16)
    wk2 = A.alloc([8, 128], BF16)
    wik2 = A.alloc([8, 128], BF16)
    wvw = A.alloc([8, 72], BF16)
    w_out = A.alloc([8, D], BF16)
    b_wparts = [Buf("wp%d" % i) for i in range(10)]
    kiT2 = A.alloc([S], BF16)
    knT2 = A.alloc([S], BF16)
    vaug = A.alloc([NKT, 65], BF16)
    b_ki = [Buf("ki%d" % i) for i in range(NT)]
    b_kn = [Buf("kn%d" % i) for i in range(NT)]
    b_va = [Buf("va%d" % i) for i in range(NKT)]
    b_vones = Buf("vones")
    xt = A.alloc([NB, 1024], F32)
    b_xt = Buf("xt")
    hT = A.alloc([8, TT], BF16)
    b_hT = Buf("hT")
    scr = dict(ss=A.alloc([8], F32), rs=A.alloc([8], F32), xn=A.alloc([NB, 1024], BF16), junk=A.alloc([1024], BF16),
               b_ss=Buf("ss"), b_xn=[Buf("xn%d" % i) for i in range(NB)], b_junk=Buf("junk"))
    qiT = A.alloc([2, 4, TT], BF16)
    qnT = A.alloc([2, 4, TT], BF16)
    b_qi = [Buf("qi%d" % i) for i in range(4)]
    b_qn = [Buf("qn%d" % i) for i in range(4)]
    b_qz = Buf("qz")
    mix = A.alloc([8, TT], BF16)
    b_mixa = Buf("mixa")
    b_mixb = [Buf("mixb%d" % i) for i in range(NB)]
    wabs = A.alloc([NB, 8], F32)
    sgn = A.alloc([NB, 8], F32)
    b_wabs = [Buf("wabs%d" % i) for i in range(NB)]
    isc = [A.alloc([S], F32) for _ in range(2)]
    b_isc = [Buf("isc0"), Buf("isc1")]
    mask = A.alloc([S], BF16)
    b_mask = Buf("mask")
    maskT = [A.alloc([NKT, 128], BF16) for _ in range(2)]
    b_maskT = [[Buf("mT%d_%d" % (k, i)) for i in range(NKT // 4)] for k in range(2)]
    dsg = [A.alloc([8, 128], BF16) for _ in range(2)]
    b_dsg = [Buf("dsg0"), Buf("dsg1")]
    r_R = Rot(A, 3, [512], BF16, "R")
    r_E = Rot(A, 2, [8, 128], BF16, "E")
    r_Pm = Rot(A, 2, [8, 128], BF16, "Pm")
    bis = [A.alloc([8], F32) for _ in range(2)]
    b_bis = [Buf("bis0"), Buf("bis1")]
    wtab = [A.alloc([NIT + 2], F32) for _ in range(2)]
    pow2 = A.alloc([NIT + 2], F32)
    b_pow2 = Buf("pow2")
    botok = A.alloc([512], BF16)
    b_botok = Buf("botok")
    rc = A.alloc([8], F32)
    b_rc = Buf("rc")
    r_ksb = Rot(A, 1, [TT], F32, "ksb")
    r_sq = Rot(A, 1, [TT], BF16, "sq")
    r_rt = Rot(A, 1, [TT], F32, "rt")
    blk1 = A.alloc([128], BF16)
    kg2 = A.alloc([2], F32)
    b_g2 = Buf("g2")
    sems = [P.new_dma_sem() for _ in range(12)]
    s_x, s_o, s_a, s_bo = P.new_dma_sem(), P.new_dma_sem(), P.new_dma_sem(), P.new_dma_sem()

    src_in = t["even_w_in"][0].rearrange("(kc p) f -> p kc f", p=128)
    wo_src = t["even_w_out"][0].rearrange("(m p) d -> p m d", p=128)
    loads = [(wk2[:, :, 0:64], src_in[:, :, 2560:2624]), (wk2[:, :, 64:128], src_in[:, :, 2560:2624]),
             (wik2[:, :, 0:64], src_in[:, :, 3200:3264]), (wik2[:, :, 64:128], src_in[:, :, 3200:3264]),
             (wiq[:, :, :], src_in[:, :, 2688:3200]), (wq[:, :, :], src_in[:, :, 2048:2560]),
             (wvw[:, :, 0:64], src_in[:, :, 2624:2688]), (wvw[:, :, 64:72], src_in[:, :, 3264:3272]),
             (w_out[:, 0:4, :], wo_src[:, 0:4, :]), (w_out[:, 4:8, :], wo_src[:, 4:8, :])]
    for i, (o_, i_) in enumerate(loads):
        P.dma("pool", lambda e, o_=o_, i_=i_: e.dma_start(out=o_, in_=i_), sems[i], writes=[b_wparts[i]])
    b_wk2, b_wik2, b_wiq, b_wq, b_wvw, b_wo = b_wparts[0:2], b_wparts[2:4], [b_wparts[4]], [b_wparts[5]], b_wparts[6:8], b_wparts[8:10]
    ensure_prep(C)
    for i, (nm, col) in enumerate((("b_k_norm", 0), ("b_q_norm", 1))):
        for hf in range(2):
            P.dma("sp", lambda e, nm=nm, col=col, hf=hf: e.dma_start(
                out=kg2[hf * 64:(hf + 1) * 64, col:col + 1], in_=t[nm][0, :].rearrange("(p o) -> p o", o=1)),
                sems[10], writes=[b_g2])
    P.op("dve", lambda e: e.tensor_scalar(out=kg2[:, 1:2], in0=kg2[:, 1:2], scalar1=0.125, scalar2=None, op0=ALU.mult),
         reads=[b_g2], writes=[b_g2])
    P.op("pool", lambda e: e.memset(blk1[:, :], 0.0), writes=[b_pow2])
    P.op("pool", lambda e: e.memset(blk1[0:64, 0:64], 1.0), writes=[b_pow2])
    P.op("pool", lambda e: e.memset(blk1[64:128, 64:128], 1.0), writes=[b_pow2])
    for i in range(NIT + 2):
        P.op("pool", lambda e, i=i: e.memset(pow2[:, i:i + 1], 2.0 ** (-(i + 1))), writes=[b_pow2])
    P.op("pool", lambda e: e.memset(vaug[:, :, 64:65], 1.0), writes=[b_vones])
    P.op("pool", lambda e: e.memset(qiT[:, :, :, :], 0.0), writes=[b_qz])
    P.op("pool", lambda e: e.memset(qnT[:, :, :, :], 0.0), writes=[b_qz])

    srcv = src.rearrange("(t b p) d -> t p b d", b=NB, p=128)
    dstv = dst.rearrange("(t b p) d -> t p b d", b=NB, p=128)
    aov = aoT.rearrange("h p s -> p h s")

    def proj_fm(wt, bw, c0, M=128):
        ps, bps = mm_psum(C)
        for kc in range(8):
            P.op("pe", lambda e, kc=kc: e.matmul(out=ps[0:M, 0:TT], lhsT=wt[:, kc, c0:c0 + M], rhs=hT[:, kc, :],
                                                 start=(kc == 0), stop=(kc == 7)),
                 reads=list(bw) + [b_hT], writes=[bps])
        return ps, bps

    def qk_norm(ps, bps, gcol, outs):
        ksb, bksb = r_ksb.get()
        P.op("act", lambda e: e.activation(out=ksb[:, :], in_=ps[:, 0:TT], func=AF.Copy), reads=[bps], writes=[bksb])
        sq, bsq = r_sq.get()
        P.op("act", lambda e: e.activation(out=sq[:, :], in_=ksb[:, :], func=AF.Square), reads=[bksb], writes=[bsq])
        ps2, bps2 = mm_psum(C)
        P.op("pe", lambda e: e.matmul(out=ps2[:, 0:TT], lhsT=blk1[:, :], rhs=sq[:, :], start=True, stop=True),
             reads=[bsq, b_pow2], writes=[bps2])
        rt, brt = r_rt.get()
        P.op("act", lambda e: e.activation(out=rt[:, :], in_=ps2[:, 0:TT], func=AF.Sqrt, scale=1.0 / 64, bias=C.eps_t[:, 0:1]),
             reads=[bps2, C.b_const], writes=[brt])
        P.op("dve", lambda e: e.reciprocal(out=rt[:, :], in_=rt[:, :]), reads=[brt], writes=[brt])
        for (oap, p0, p1, bo) in outs:
            P.op("dve", lambda e, oap=oap, p0=p0, p1=p1: e.scalar_tensor_tensor(
                out=oap, in0=ksb[p0:p1, :], scalar=kg2[p0:p1, gcol:gcol + 1], in1=rt[p0:p1, :], op0=ALU.mult, op1=ALU.mult),
                reads=[bksb, brt, b_g2, b_qz], writes=[bo])

    def tok_proj(ti, b):
        J = ti * NB + b
        ps, bps = mm_psum(C)
        for kc in range(8):
            P.op("pe", lambda e, kc=kc: e.matmul(out=ps[:, 0:72], lhsT=hT[:, kc, b * 128:(b + 1) * 128], rhs=wvw[:, kc, :],
                                                 start=(kc == 0), stop=(kc == 7)),
                 reads=b_wvw + [b_hT], writes=[bps])
        P.op("act", lambda e: e.activation(out=vaug[:, J, 0:64], in_=ps[:, 0:64], func=AF.Copy), reads=[bps], writes=[b_va[J]])
        P.op("act", lambda e: e.activation(out=wabs[:, b, :], in_=ps[:, 64:72], func=AF.Abs, scale=0.125 * (8 ** -0.5)),
             reads=[bps], writes=[b_wabs[b]])
        P.op("act", lambda e: e.activation(out=sgn[:, b, :], in_=ps[:, 64:72], func=AF.Sign), reads=[bps], writes=[b_wabs[b]])

    def stage_A(ti, b):
        J = ti * NB + b
        nkeys = 128 * (J + 1)
        iscJ, biscJ = isc[J % 2], b_isc[J % 2]
        dg, bdg = dsg[J % 2], b_dsg[J % 2]
        for h in range(8):
            P.op("pool", lambda e, h=h: e.tensor_scalar(out=dg[:, h, :], in0=C.ident_b[:, :], scalar1=sgn[:, b, h:h + 1],
                                                        scalar2=None, op0=ALU.mult),
                 reads=[C.b_const, b_wabs[b]], writes=[bdg])
        ngrp = (nkeys + 511) // 512
        acc, bacc = C.psum[3], C.b_psum[3]
        for G in range(ngrp):
            k0 = G * 512
            wd = min(512, nkeys - k0)
            kbufs = [b_ki[i] for i in range(k0 // TT, (k0 + wd - 1) // TT + 1)]
            pend = None
            for h in range(8):
                hp, jj = h % 2, h // 2
                pi = C.mm_rr % 3
                C.mm_rr += 1
                dps, bdps = C.psum[pi], C.b_psum[pi]
                P.op("pe", lambda e, hp=hp, jj=jj, dps=dps, k0=k0, wd=wd: e.matmul(
                    out=dps[:, 0:wd], lhsT=qiT[:, hp, jj, b * 128:(b + 1) * 128], rhs=kiT2[:, k0:k0 + wd], start=True, stop=True),
                    reads=[b_qi[jj], b_qz] + kbufs, writes=[bdps])
                R, bR = r_R.get()
                P.op("act", lambda e, h=h, dps=dps, R=R, wd=wd: e.activation(out=R[:, 0:wd], in_=dps[:, 0:wd], func=AF.Relu,
                                                                             scale=wabs[:, b, h:h + 1]),
                     reads=[bdps, b_wabs[b]], writes=[bR])
                if pend is not None:
                    pend()
                pend = (lambda h=h, R=R, bR=bR, wd=wd: P.op("pe", lambda e: e.matmul(
                    out=acc[:, 0:wd], lhsT=dg[:, h, :], rhs=R[:, 0:wd], start=(h == 0), stop=(h == 7)),
                    reads=[bdg, bR], writes=[bacc]))
            pend()
            P.op("act", lambda e, k0=k0, wd=wd: e.activation(out=iscJ[:, k0:k0 + wd], in_=acc[:, 0:wd], func=AF.Copy),
                 reads=[bacc], writes=[biscJ])

    def stage_B(ti, b):
        J = ti * NB + b
        nkeys = 128 * (J + 1)
        nk = J + 1
        iscJ, biscJ = isc[J % 2], b_isc[J % 2]
        bs, bbs, wt = bis[J % 2], b_bis[J % 2], wtab[J % 2]
        mT, bmT = maskT[J % 2], b_maskT[J % 2]
        if J < 2:
            P.op("dve", lambda e: e.memset(iscJ[0:64, nkeys - 64:nkeys], NEG), writes=[biscJ])
            P.op("dve", lambda e: e.memset(bs[:, 6:7], -1.0e29), writes=[bbs])
            yield
        else:
            P.op("dve", lambda e: e.tensor_reduce(out=bs[:, 0:1], in_=iscJ[:, 0:nkeys], axis=AX.X, op=ALU.min),
                 reads=[biscJ], writes=[bbs])
            P.op("dve", lambda e: e.memset(iscJ[0:64, nkeys - 64:nkeys], NEG), reads=[], writes=[biscJ])
            yield
            P.op("dve", lambda e: e.tensor_reduce(out=bs[:, 1:2], in_=iscJ[:, 0:nkeys], axis=AX.X, op=ALU.max),
                 reads=[biscJ], writes=[bbs])
            P.op("dve", lambda e: e.tensor_tensor(out=bs[:, 2:3], in0=bs[:, 1:2], in1=bs[:, 0:1], op=ALU.subtract),
                 reads=[bbs], writes=[bbs])
            P.op("dve", lambda e: e.tensor_scalar(out=wt[:, :], in0=pow2[:, :], scalar1=bs[:, 2:3], scalar2=None, op0=ALU.mult),
                 reads=[bbs, b_pow2], writes=[bbs])
            P.op("dve", lambda e: e.tensor_tensor(out=bs[:, 3:4], in0=bs[:, 0:1], in1=wt[:, 0:1], op=ALU.add),
                 reads=[bbs], writes=[bbs])
            yield
            for i in range(NIT):
                P.op("dve", lambda e: e.tensor_scalar(out=mask[:, 0:nkeys], in0=iscJ[:, 0:nkeys], scalar1=bs[:, 3:4], scalar2=None,
                                                      op0=ALU.is_ge, op1=ALU.add, accum_out=bs[:, 4:5]),
                     reads=[biscJ, bbs], writes=[b_mask, bbs])
                P.op("dve", lambda e: e.tensor_scalar(out=bs[:, 5:6], in0=bs[:, 4:5], scalar1=TOPK - 0.5, scalar2=0.5,
                                                      op0=ALU.is_ge, op1=ALU.subtract), reads=[bbs], writes=[bbs])
                P.op("dve", lambda e, i=i: e.scalar_tensor_tensor(out=bs[:, 3:4], in0=bs[:, 5:6], scalar=wt[:, i:i + 1],
                                                                   in1=bs[:, 3:4], op0=ALU.mult, op1=ALU.add),
                     reads=[bbs], writes=[bbs])
                yield
            P.op("dve", lambda e: e.tensor_tensor(out=bs[:, 6:7], in0=bs[:, 3:4], in1=wt[:, NIT:NIT + 1], op=ALU.subtract),
                 reads=[bbs], writes=[bbs])
        P.op("dve", lambda e: e.tensor_scalar(out=mask[:, 0:nkeys], in0=iscJ[:, 0:nkeys], scalar1=bs[:, 6:7], scalar2=None,
                                              op0=ALU.is_ge), reads=[biscJ, bbs], writes=[b_mask])
        for g4 in range((nk + 3) // 4):
            n4 = min(4, nk - g4 * 4)
            pi = C.tp_rr % 2
            C.tp_rr += 1
            pst = C.psum_tp[pi]
            for q4 in range(n4):
                kt = g4 * 4 + q4
                P.op("pe", lambda e, kt=kt, q4=q4, pst=pst: e.transpose(out=pst[:, q4 * 128:(q4 + 1) * 128],
                                                                        in_=mask[:, kt * 128:(kt + 1) * 128], identity=C.ident_b[:, :]),
                     reads=[b_mask, C.b_const], writes=[C.b_psum_tp[pi]])
            P.op("act", lambda e, g4=g4, n4=n4, pst=pst: e.activation(
                out=mT[:, g4 * 4:g4 * 4 + n4, :], in_=pst[:, 0:n4 * 128].rearrange("p (a b) -> p a b", a=n4), func=AF.Copy),
                reads=[C.b_psum_tp[pi]], writes=[bmT[g4]])
            yield

    def stage_C(ti, b):
        J = ti * NB + b
        nk = J + 1
        mT, bmT = maskT[J % 2], b_maskT[J % 2]
        OA, bOA, OB, bOB = C.psum[4], C.b_psum[4], C.psum[5], C.b_psum[5]
        for kt in range(nk):
            kb = b_kn[(kt * 128) // TT]
            pr = C.lp_rr % 2
            C.lp_rr += 1
            L = C.psall[:, pr * 1024:(pr + 1) * 1024]
            bL = [C.b_psum[2 * pr], C.b_psum[2 * pr + 1]]
            for hp in range(2):
                P.op("pe", lambda e, kt=kt, L=L, hp=hp: e.matmul(
                    out=L[:, hp * 512:(hp + 1) * 512], lhsT=knT2[:, kt * 128:(kt + 1) * 128],
                    rhs=qnT[:, hp, :, b * 128:(b + 1) * 128], start=True, stop=True),
                    reads=[kb, b_qz] + b_qn, writes=[bL[hp]])
            E, bE = r_E.get()
            P.op("act", lambda e, E=E, L=L: e.activation(out=E[:, :, :], in_=L.rearrange("p (a b) -> p a b", a=8), func=AF.Exp),
                 reads=bL, writes=[bE])
            Pm, bPm = r_Pm.get()
            P.op("dve", lambda e, E=E, Pm=Pm, kt=kt: e.tensor_tensor(out=Pm[:, :, :], in0=E[:, :, :],
                                                                     in1=mT[:, kt:kt + 1, :].to_broadcast([128, 8, 128]), op=ALU.mult),
                 reads=[bE, bmT[kt // 4]], writes=[bPm])
            for e8 in range(8):
                O, bO = (OA, bOA) if e8 < 4 else (OB, bOB)
                c65 = (e8 % 4) * 65
                P.op("pe", lambda e, e8=e8, Pm=Pm, kt=kt, O=O, c65=c65: e.matmul(
                    out=O[:, c65:c65 + 65], lhsT=Pm[:, e8, :], rhs=vaug[:, kt, :],
                    start=(kt == 0 and e8 % 4 == 0), stop=(kt == nk - 1), skip_group_check=True),
                    reads=[bPm, b_va[kt], b_vones], writes=[bO])
            yield
        for half, (O, bO) in enumerate(((OA, bOA), (OB, bOB))):
            Ov = O[:, 0:260].rearrange("p (a b) -> p a b", a=4)
            P.op("dve", lambda e, Ov=Ov, half=half: e.reciprocal(out=rc[:, half * 4:half * 4 + 4].rearrange("p (a b) -> p a b", b=1),
                                                                  in_=Ov[:, :, 64:65]), reads=[bO], writes=[b_rc])
            bov = botok[:, :].rearrange("p (j two d) -> p j two d", two=2, d=64)[:, :, half, :]
            P.op("dve", lambda e, Ov=Ov, half=half, bov=bov: e.tensor_tensor(
                out=bov, in0=Ov[:, :, 0:64],
                in1=rc[:, half * 4:half * 4 + 4].rearrange("p (a b) -> p a b", b=1).to_broadcast([128, 4, 64]), op=ALU.mult),
                reads=[bO, b_rc], writes=[b_botok])
        pi = C.tp_rr % 2
        C.tp_rr += 1
        pst = C.psum_tp[pi]
        for jj in range(4):
            P.op("pe", lambda e, jj=jj: e.transpose(out=pst[:, jj * 128:(jj + 1) * 128], in_=botok[:, jj * 128:(jj + 1) * 128],
                                                    identity=C.ident_b[:, :]),
                 reads=[b_botok, C.b_const], writes=[C.b_psum_tp[pi]])
        P.op("act", lambda e: e.activation(out=mix[:, 4:8, b * 128:(b + 1) * 128],
                                           in_=pst[:, 0:512].rearrange("p (a b) -> p a b", a=4), func=AF.Copy),
             reads=[C.b_psum_tp[pi]], writes=[b_mixb[b]])
        yield

    C.lp_rr = 0
    for ti in range(NT):
        P.dma("sp", lambda e, ti=ti: e.dma_start(out=xt[:, :, :], in_=srcv[ti]), s_x, reads=[b_src[ti]], writes=[b_xt])
        if boT is None:
            P.dma("sp", lambda e, ti=ti: e.dma_start(out=mix[:, 0:4, :], in_=aov[:, :, ti * TT:(ti + 1) * TT]), s_a,
                  reads=[b_ao[ti]], writes=[b_mixa])
        norm_transpose(C, xt, b_xt, hT, b_hT, ("even_norm",), NB, scr)
        tc = slice(ti * TT, (ti + 1) * TT)
        ps, bps = proj_fm(wk2, b_wk2, 0)
        qk_norm(ps, bps, 0, [(knT2[:, tc], 0, 128, b_kn[ti])])
        ps, bps = proj_fm(wik2, b_wik2, 0)
        P.op("act", lambda e, ps=ps, tc=tc: e.activation(out=kiT2[:, tc], in_=ps[:, 0:TT], func=AF.Copy), reads=[bps], writes=[b_ki[ti]])
        for jj in range(4):
            ps, bps = proj_fm(wiq, b_wiq, jj * 128)
            for hp in range(2):
                P.op("act", lambda e, ps=ps, jj=jj, hp=hp: e.activation(out=qiT[hp * 64:(hp + 1) * 64, hp, jj, :],
                                                                       in_=ps[hp * 64:(hp + 1) * 64, 0:TT], func=AF.Copy),
                     reads=[bps, b_qz], writes=[b_qi[jj]])
        for b in range(NB):
            tok_proj(ti, b)
        for jj in range(4):
            ps, bps = proj_fm(wq, b_wq, jj * 128)
            qk_norm(ps, bps, 1, [(qnT[0:64, 0, jj, :], 0, 64, b_qn[jj]), (qnT[64:128, 1, jj, :], 64, 128, b_qn[jj])])
        prep_some(C, 8)
        stage_A(ti, 0)
        for b in range(NB):
            if b + 1 < NB:
                stage_A(ti, b + 1)
            J = ti * NB + b
            gens = [(stage_B(ti, b), NIT + 4 + (J + 4) // 4)]
            if b > 0:
                gens.append((stage_C(ti, b - 1), J + 1))
            _interleave(gens)
        _interleave([(stage_C(ti, NB - 1), ti * NB + NB)])
        if boT is not None:
            P.dma("sp", lambda e, ti=ti: e.dma_start(out=boT.rearrange("h p s -> p h s")[:, :, ti * TT:(ti + 1) * TT], in_=mix[:, 4:8, :]),
                  s_bo, reads=b_mixb, writes=[b_dst[ti]])
            continue
        for b in range(NB):
            for hf in range(2):
                pi = 4 + (C.dn_rr % 2)
                C.dn_rr += 1
                ps = C.psum[pi]
                for m in range(8):
                    P.op("pe", lambda e, m=m, ps=ps, b=b, hf=hf: e.matmul(
                        out=ps[:, 0:512], lhsT=mix[:, m, b * 128:(b + 1) * 128], rhs=w_out[:, m, hf * 512:(hf + 1) * 512],
                        start=(m == 0), stop=(m == 7)),
                        reads=[b_mixa, b_mixb[b]] + b_wo, writes=[C.b_psum[pi]])
                P.op("dve", lambda e, ps=ps, b=b, hf=hf: e.tensor_tensor(
                    out=xt[:, b, hf * 512:(hf + 1) * 512], in0=ps[:, 0:512], in1=xt[:, b, hf * 512:(hf + 1) * 512], op=ALU.add),
                    reads=[C.b_psum[pi], b_xt], writes=[b_xt])
        P.dma("sp", lambda e, ti=ti: e.dma_start(out=dstv[ti], in_=xt[:, :, :]), s_o, reads=[b_xt], writes=[b_dst[ti]])


def build(phases=("ffn0",), dbg=None):
    nc = bass.Bass("TRN2", target_bir_lowering=False)
    C = Ctx()
    C.nc = nc
    C.P = P = Prog(nc)
    specs = {
        "x": [S, D], "lb_logits": [3, 512], "even_norm": [1, D], "even_w_in": [1, D, EVEN_IN], "even_w_out": [1, D, D],
        "a_out_norm": [1, 512], "b_q_norm": [1, 64], "b_k_norm": [1, 64], "odd_norm": [1, D],
        "odd_w_in": [1, D, 2 * LRU_W], "odd_conv_w": [1, 4, LRU_W], "odd_conv_b": [1, LRU_W],
        "odd_gate_a_w": [1, 10, 128, 128], "odd_gate_a_b": [1, LRU_W], "odd_gate_x_w": [1, 10, 128, 128],
        "odd_gate_x_b": [1, LRU_W], "odd_lambda": [1, LRU_W], "odd_w_out": [1, LRU_W, D],
        "ffn_norm": [2, D], "ffn_w_up": [2, D, 2 * DFF], "ffn_conv_w": [2, 3, 2 * DFF], "ffn_conv_b": [2, 2 * DFF],
        "ffn_w_down": [2, DFF, D],
        "c_ident_f": [128, 128], "c_causT": [64, 64],
    }
    C.t = {k: nc.dram_tensor(k, v, F32, kind="ExternalInput") for k, v in specs.items()}
    C.t["c_ident_b"] = nc.dram_tensor("c_ident_b", [128, 128], BF16, kind="ExternalInput")
    out = nc.dram_tensor("out", [S, D], F32, kind="ExternalOutput")

    C.persist = Arena.__new__(Arena)
    pt = nc.alloc_sbuf_tensor("persist", [128, 2048], F32)
    C.persist.t, C.persist.nwords, C.persist.off = pt, 2048, 0
    C.arena = Arena(nc, 198 * 1024)
    psall = nc.alloc_psum_tensor("psall", [128, 4096], F32)
    C.psall = psall
    C.psum = [psall[:, i * 512:(i + 1) * 512] for i in range(6)]
    C.b_psum = [Buf("ps%d" % i) for i in range(6)]
    tp = [psall[:, i * 512:(i + 1) * 512] for i in range(6, 8)]
    C.psum_tp = [a.bitcast(BF16) for a in tp]
    C.psum_tp_f = tp
    C.b_psum_tp = [Buf("pstp0"), Buf("pstp1")]
    C.tp_rr = C.mm_rr = C.u_rr = C.dn_rr = 0

    C.ident_f = C.persist.alloc([128], F32)
    C.ident_b = C.persist.alloc([128], BF16)
    C.eps_t = C.persist.alloc([1], F32)
    C.one_t = C.persist.alloc([1], F32)
    C.neghalf = C.persist.alloc([8], F32)
    C.b_const = Buf("const")
    s_c = P.new_dma_sem()
    s_c2 = P.new_dma_sem()
    P.op("dve", lambda e: e.memset(C.eps_t[:, :], EPS), writes=[C.b_const])
    P.op("dve", lambda e: e.memset(C.one_t[:, :], 1.0), writes=[C.b_const])
    P.op("dve", lambda e: e.memset(C.neghalf[:, :], -0.5), writes=[C.b_const])
    b_i1, b_i2 = Buf("i1"), Buf("i2")
    C.b_if, C.b_ib = b_i1, b_i2
    P.dma("sp", lambda e: e.dma_start(out=C.ident_f[:, :], in_=C.t["c_ident_f"][:, :]), s_c, writes=[C.b_const])
    P.dma("sp", lambda e: e.dma_start(out=C.ident_b[:, :], in_=C.t["c_ident_b"][:, :]), s_c2, writes=[C.b_const])

    load_vectors(C)

    xin = C.t["x"]
    b_xin = [Buf("xin%d" % i) for i in range(NT)]
    scratch = {}

    def dram_act(name):
        if name not in scratch:
            scratch[name] = (nc.dram_tensor(name, [S, D], F32, kind="Internal"), [Buf(name + str(i)) for i in range(NT)])
        return scratch[name]

    cur, b_cur = xin, b_xin
    plist = list(phases)
    if dbg and dbg.startswith("aoT"):
        if ":" in dbg:
            C.dbg_sel = dbg.split(":")[1]
        aoT = nc.dram_tensor("dbg", [4, 128, S], BF16, kind="ExternalOutput")
    else:
        aoT = nc.dram_tensor("aoT", [4, 128, S], BF16, kind="Internal")
    b_ao = [Buf("ao%d" % k) for k in range(NT)]
    final_bufs = None
    for i, ph in enumerate(plist):
        last = (i == len(plist) - 1)
        if last:
            dst, b_dst = out, [Buf("out%d" % k) for k in range(NT)]
        else:
            dst, b_dst = dram_act("act%d" % i)
        if ph == "ffn0":
            ffn_phase(C, 0, cur, b_cur, dst, b_dst)
        elif ph == "ffn1":
            ffn_phase(C, 1, cur, b_cur, dst, b_dst)
        elif ph == "odd":
            odd_phase(C, cur, b_cur, dst, b_dst)
        elif ph == "evenB":
            if dbg == "boT":
                boT = nc.dram_tensor("dbg", [4, 128, S], BF16, kind="ExternalOutput")
                b_dst = [Buf("bo%d" % k) for k in range(NT)]
                evenB_phase(C, cur, b_cur, aoT, b_ao, dst, b_dst, boT=boT)
            else:
                evenB_phase(C, cur, b_cur, aoT, b_ao, dst, b_dst)
        elif ph == "evenA":
            evenA_phase(C, cur, b_cur, aoT, b_ao)
            final_bufs = b_ao
            continue
        else:
            raise ValueError(ph)
        cur, b_cur = dst, b_dst
        final_bufs = b_dst
    waits = P._deps("sp", final_bufs, ())
    P.q["sp"].append((None, waits, None))
    P.barrier()
    P.emit()
    return nc


_CONSTS = None


def consts():
    global _CONSTS
    if _CONSTS is None:
        eye = np.eye(128, dtype=np.float32)
        _CONSTS = {"c_ident_f": eye, "c_ident_b": eye.astype(ml_dtypes.bfloat16),
                   "c_causT": np.triu(np.ones((64, 64), dtype=np.float32))}
    return _CONSTS


LAST = {}


def run(inputs, phases, core_ids=tuple(range(8)), trace=False, dbg=None):
    nc = build(phases, dbg)
    shared = {k: np.ascontiguousarray(v, dtype=np.float32) for k, v in inputs.items() if k != "x"}
    shared.update(consts())
    x = np.asarray(inputs["x"], dtype=np.float32)
    in_maps = []
    for c in core_ids:
        m = dict(shared)
        m["x"] = np.ascontiguousarray(x[c])
        in_maps.append(m)
    res = run_bass_kernel_spmd(nc, in_maps, core_ids=list(core_ids), **({"trace": True} if trace else {}))
    LAST["res"] = res
    if dbg:
        return np.stack([r["dbg"] for r in res.results], axis=0)
    return np.stack([r["out"] for r in res.results], axis=0)


PHASES = ("evenA", "evenB", "ffn0", "odd", "ffn1")


def kernel(**inputs):
    return run(inputs, PHASES).astype(np.float32)
```

```python
import contextlib
import numpy as np
import ml_dtypes
import concourse.bass as bass
import concourse.mybir as mybir
from concourse.bass_utils import run_bass_kernel_spmd

F32 = mybir.dt.float32
BF16 = mybir.dt.bfloat16
ALU = mybir.AluOpType
AF = mybir.ActivationFunctionType
AX = mybir.AxisListType

S = 4096
D = 1024
DFF = 3072
TT = 512
NT = S // TT
EPS = 1e-6
EVEN_IN = 3272
LRU_W = 1280

ENGS = ("pe", "act", "dve", "pool", "sp")


class Buf:
    __slots__ = ("name", "w", "r")

    def __init__(self, name=""):
        self.name = name
        self.w = None
        self.r = []


class Prog:
    def __init__(self, nc):
        self.nc = nc
        self.q = {e: [] for e in ENGS}
        self.cnt = {e: 0 for e in ENGS}
        self.known = {e: {} for e in ENGS}
        self.dma_sems = []
        self.sem_handles = {}

    def _deps(self, eng, reads, writes):
        best = {}
        for b in reads:
            if b.w is not None:
                k, v = b.w
                if v > best.get(k, 0):
                    best[k] = v
        for b in writes:
            if b.w is not None:
                k, v = b.w
                if v > best.get(k, 0):
                    best[k] = v
            for (k, v) in b.r:
                if v > best.get(k, 0):
                    best[k] = v
        waits = []
        kn = self.known[eng]
        for k, v in best.items():
            if k == "pe" and eng == "pe":
                continue
            if kn.get(k, 0) >= v:
                continue
            kn[k] = v
            waits.append((k, v))
        return waits

    def _commit(self, tok, reads, writes):
        for b in reads:
            if len(b.r) > 24:
                m = {}
                for (k, v) in b.r:
                    if v > m.get(k, 0):
                        m[k] = v
                b.r = list(m.items())
            b.r.append(tok)
        for b in writes:
            b.w = tok
            b.r = []

    def op(self, eng, fn, reads=(), writes=()):
        waits = self._deps(eng, reads, writes)
        self.cnt[eng] += 1
        tok = (eng, self.cnt[eng])
        self.q[eng].append((fn, waits, (eng, 1)))
        self._commit(tok, reads, writes)
        return tok

    def new_dma_sem(self):
        key = "d%d" % (len(self.dma_sems) + 1)
        self.dma_sems.append(key)
        self.cnt[key] = 0
        return key

    def dma(self, eng, fn, sem, reads=(), writes=()):
        waits = self._deps(eng, reads, writes)
        self.cnt[sem] += 16
        tok = (sem, self.cnt[sem])
        self.q[eng].append((fn, waits, (sem, 16)))
        self._commit(tok, reads, writes)
        return tok

    def barrier(self, skip=()):
        for e in ENGS:
            waits = []
            kn = self.known[e]
            for k in list(ENGS) + self.dma_sems:
                if k in skip:
                    continue
                v = self.cnt[k]
                if v > kn.get(k, 0):
                    kn[k] = v
                    waits.append((k, v))
            self.q[e].append((None, waits, None))

    def emit(self):
        nc = self.nc
        with contextlib.ExitStack() as st:
            for k in list(ENGS) + self.dma_sems:
                self.sem_handles[k] = st.enter_context(nc.semaphore("s_" + k))
            block = st.enter_context(nc.Block())
            sh = self.sem_handles

            def run(e, eobj):
                for (fn, waits, inc) in self.q[e]:
                    for (k, v) in waits:
                        eobj.wait_ge(sh[k], v)
                    if fn is not None:
                        fn(eobj).then_inc(sh[inc[0]], inc[1])

            @block.tensor
            def _(eobj):
                run("pe", eobj)

            @block.scalar
            def _(eobj):
                run("act", eobj)

            @block.vector
            def _(eobj):
                run("dve", eobj)

            @block.gpsimd
            def _(eobj):
                run("pool", eobj)

            @block.sync
            def _(eobj):
                run("sp", eobj)


def _dsize(dt):
    return 4 if dt == F32 else 2


class Arena:
    def __init__(self, nc, nbytes):
        self.t = nc.alloc_sbuf_tensor("arena", [128, nbytes // 4], F32)
        self.nwords = nbytes // 4
        self.off = 0

    def reset(self, off=0):
        self.off = off

    def alloc(self, free_shape, dt):
        n = 1
        for s in free_shape:
            n *= s
        nw = (n * _dsize(dt) + 3) // 4
        nw = (nw + 7) // 8 * 8
        assert self.off + nw <= self.nwords, ("SBUF arena overflow", self.off, nw, self.nwords)
        ap = self.t[:, self.off:self.off + nw]
        self.off += nw
        if dt != F32:
            ap = ap.bitcast(dt)
        ap = ap[:, 0:n]
        if len(free_shape) == 2:
            ap = ap.rearrange("p (a b) -> p a b", a=free_shape[0])
        elif len(free_shape) == 3:
            ap = ap.rearrange("p (a b c) -> p a b c", a=free_shape[0], b=free_shape[1])
        return ap


class Ctx:
    pass


def vec_rows(C):
    rows = []

    def add(key, ap1d, n):
        rows.append((key, ap1d.rearrange("(c p) -> c p", p=128), n // 128))

    t = C.t
    for l in range(2):
        add(("ffn_norm", l), t["ffn_norm"][l, :], D)
        for j in range(3):
            add(("ffn_cw", l, j), t["ffn_conv_w"][l, j, :], 2 * DFF)
        add(("ffn_cb", l), t["ffn_conv_b"][l, :], 2 * DFF)
    add(("even_norm",), t["even_norm"][0, :], D)
    add(("odd_norm",), t["odd_norm"][0, :], D)
    add(("a_out_norm",), t["a_out_norm"][0, :], 512)
    for j in range(3):
        add(("lb", j), t["lb_logits"][j, :], 512)
    for j in range(4):
        add(("odd_cw", j), t["odd_conv_w"][0, j, :], LRU_W)
    add(("odd_cb",), t["odd_conv_b"][0, :], LRU_W)
    add(("odd_ba",), t["odd_gate_a_b"][0, :], LRU_W)
    add(("odd_bx",), t["odd_gate_x_b"][0, :], LRU_W)
    add(("odd_lam",), t["odd_lambda"][0, :], LRU_W)
    return rows


def load_vectors(C):
    P, nc = C.P, C.nc
    rows = vec_rows(C)
    total = sum(r[2] for r in rows)
    C.vt = C.persist.alloc([total], F32)
    C.b_vt = Buf("vt")
    C.vcol = {}
    stg = C.arena.alloc([128], F32)
    sem = P.new_dma_sem()
    st = {"col": 0, "cur": 0, "bufs": []}
    guard = Buf("stg_guard")

    def flush():
        n = st["cur"]
        if n == 0:
            return
        c0 = st["col"]
        ps = C.psum[0]
        P.op("pe", lambda e: e.transpose(out=ps[:, 0:n], in_=stg[0:n, :], identity=C.ident_f[0:n, 0:n]),
             reads=st["bufs"] + [C.b_const], writes=[C.b_psum[0], guard])
        P.op("dve", lambda e: e.tensor_copy(out=C.vt[:, c0:c0 + n], in_=ps[:, 0:n]),
             reads=[C.b_psum[0]], writes=[C.b_vt])
        st["col"] += n
        st["cur"] = 0

    slot_bufs = {}
    for key, ap, n in rows:
        r0 = 0
        C.vcol[key] = st["col"] + st["cur"]
        while r0 < n:
            cur = st["cur"]
            if cur == 0:
                st["bufs"] = []
            m = min(n - r0, 128 - cur)
            bb = slot_bufs.setdefault((cur, m), Buf("stg"))
            st["bufs"].append(bb)
            P.dma("sp", lambda e, ap=ap, a0=r0, m=m, c0=cur: e.dma_start(out=stg[c0:c0 + m, :], in_=ap[a0:a0 + m, :]),
                  sem, reads=[guard], writes=[bb])
            st["cur"] += m
            r0 += m
            if st["cur"] == 128:
                flush()
    flush()


def V(C, key, j=0):
    c = C.vcol[key] + j
    return C.vt[:, c:c + 1]


def ensure_prep(C):
    if not getattr(C, "prep_done", False):
        C.prep_done = True
        prep_weights(C)


def prep_some(C, n):
    ensure_prep(C)
    for _ in range(min(n, len(C.prep_queue))):
        C.prep_queue.pop(0)()


def prep_weights(C):
    P, nc, t = C.P, C.nc, C.t
    C.wb = {}
    C.b_wb = {}
    C.prep_queue = []
    sem_i = [0]
    sems = [P.new_dma_sem() for _ in range(4)]
    C.prep_sems = tuple(sems)

    def cast(name, dst, src):
        b = Buf(name)
        s = sems[sem_i[0] % 4]
        sem_i[0] += 1
        C.prep_queue.append(lambda: P.dma("pool", lambda e: e.dma_start(out=dst, in_=src), s, reads=[], writes=[b]))
        return b

    for l in range(2):
        wu = nc.dram_tensor("wup_b%d" % l, [24, 128, 8, 256], BF16, kind="Internal")
        src = t["ffn_w_up"][l].rearrange("(kc p) f -> p kc f", p=128)
        bl = []
        for g in range(12):
            for gv in range(2):
                base = gv * DFF + g * 256
                bl.append(cast("wup%d_%d_%d" % (l, g, gv), wu[g * 2 + gv], src[:, :, base:base + 256]))
        C.wb[("wup", l)] = wu
        C.b_wb[("wup", l)] = bl
        wd = nc.dram_tensor("wdn_b%d" % l, [128, 24, 1024], BF16, kind="Internal")
        srcd = t["ffn_w_down"][l].rearrange("(fc p) d -> p fc d", p=128)
        bl = []
        for q in range(4):
            bl.append(cast("wdn%d_%d" % (l, q), wd[:, q * 6:(q + 1) * 6, :], srcd[:, q * 6:(q + 1) * 6, :]))
        C.wb[("wdn", l)] = wd
        C.b_wb[("wdn", l)] = bl


def norm_transpose(C, xt, b_xt, hT, b_hT, gain_key, nblk, scr):
    P = C.P
    ss, rs, xn, junk = scr["ss"], scr["rs"], scr["xn"], scr["junk"]
    b_ss, b_xn, b_junk = scr["b_ss"], scr["b_xn"], scr["b_junk"]
    for b in range(nblk):
        P.op("act", lambda e, b=b: e.activation(out=junk[:, :], in_=xt[:, b, :], func=AF.Square,
                                                 accum_out=ss[:, b:b + 1]),
             reads=[b_xt], writes=[b_junk, b_ss])
    P.op("pool", lambda e: e.tensor_scalar(out=rs[:, 0:nblk], in0=ss[:, 0:nblk], scalar1=1.0 / D, scalar2=EPS, op0=ALU.mult, op1=ALU.add),
         reads=[b_ss], writes=[b_ss])
    P.op("pool", lambda e: e.tensor_tensor(out=rs[:, 0:nblk], in0=rs[:, 0:nblk], in1=C.neghalf[:, 0:nblk], op=ALU.pow),
         reads=[b_ss, C.b_const], writes=[b_ss])
    for b in range(nblk):
        P.op("dve", lambda e, b=b: e.tensor_scalar(out=xn[:, b, :], in0=xt[:, b, :], scalar1=rs[:, b:b + 1],
                                                    scalar2=None, op0=ALU.mult),
             reads=[b_xt, b_ss], writes=[b_xn[b]])
    for kc in range(8):
        pi = C.tp_rr % 2
        C.tp_rr += 1
        ps = C.psum_tp[pi]
        for b in range(nblk):
            P.op("pe", lambda e, b=b, kc=kc, ps=ps: e.transpose(out=ps[:, b * 128:(b + 1) * 128],
                                                                  in_=xn[:, b, kc * 128:(kc + 1) * 128],
                                                                  identity=C.ident_b[:, :]),
                 reads=[b_xn[b], C.b_const], writes=[C.b_psum_tp[pi]])
        P.op("act", lambda e, kc=kc, ps=ps: e.activation(out=hT[:, kc, 0:nblk * 128], in_=ps[:, 0:nblk * 128],
                                                          func=AF.Copy, scale=V(C, gain_key, kc)),
             reads=[C.b_psum_tp[pi], C.b_vt], writes=[b_hT])


def ffn_phase(C, l, src, b_src, dst, b_dst):
    P, nc, A = C.P, C.nc, C.arena
    prep_some(C, 1000)
    P.barrier()
    A.reset()
    NB = TT // 128
    wd = A.alloc([24, 1024], BF16)
    b_wd = [Buf("wd%d" % q) for q in range(4)]
    wu = [A.alloc([2, 8, 256], BF16) for _ in range(2)]
    b_wu = [[Buf("wu"), Buf("wu")] for _ in range(2)]
    xt = [A.alloc([NB, 1024], F32) for _ in range(2)]
    b_xt = [Buf("xt0"), Buf("xt1")]
    hT = A.alloc([8, TT], BF16)
    b_hT = Buf("hT")
    gT = A.alloc([24, TT], BF16)
    b_gT = [Buf("gT%d" % i) for i in range(24)]
    u_sb = [A.alloc([TT + 4], F32) for _ in range(4)]
    b_u = [Buf("u%d" % i) for i in range(4)]
    cb = [A.alloc([TT], F32) for _ in range(4)]
    b_c = [Buf("c%d" % i) for i in range(4)]
    halo = A.alloc([48, 2], F32)
    b_halo = [Buf("halo%d" % i) for i in range(48)]
    scr = dict(ss=A.alloc([8], F32), rs=A.alloc([8], F32), xn=A.alloc([NB, 1024], BF16), junk=A.alloc([1024], BF16),
               b_ss=Buf("ss"), b_xn=[Buf("xn%d" % i) for i in range(NB)], b_junk=Buf("junk"))
    s_wd = [P.new_dma_sem() for _ in range(4)]
    s_wu = [[P.new_dma_sem(), P.new_dma_sem()] for _ in range(2)]
    s_x = [P.new_dma_sem(), P.new_dma_sem()]
    s_o = [P.new_dma_sem(), P.new_dma_sem()]

    wdd = C.wb[("wdn", l)]
    for q in range(4):
        P.dma("pool", lambda e, q=q: e.dma_start(out=wd[:, q * 6:(q + 1) * 6, :], in_=wdd[:, q * 6:(q + 1) * 6, :]),
              s_wd[q], reads=[C.b_wb[("wdn", l)][q]], writes=[b_wd[q]])
    P.op("pool", lambda e: e.memset(halo[:, :, :], 0.0), writes=b_halo)

    wud = C.wb[("wup", l)]
    srcv = src.rearrange("(t b p) d -> t p b d", b=NB, p=128)
    dstv = dst.rearrange("(t b p) d -> t p b d", b=NB, p=128)

    def load_x(ti):
        sl = ti % 2
        P.dma("sp", lambda e: e.dma_start(out=xt[sl][:, :, :], in_=srcv[ti]), s_x[sl], reads=[b_src[ti]], writes=[b_xt[sl]])

    wu_seq = [(ti, g) for ti in range(NT) for g in range(12)]

    def load_wu(i):
        ti, g = wu_seq[i]
        sl = i % 2
        for gv in range(2):
            P.dma("pool", lambda e, gv=gv: e.dma_start(out=wu[sl][:, gv, :, :], in_=wud[g * 2 + gv]),
                  s_wu[sl][gv], reads=[C.b_wb[("wup", l)][g * 2 + gv]], writes=[b_wu[sl][gv]])

    load_x(0)
    load_wu(0)
    cwk, cbk = ("ffn_cw", l), ("ffn_cb", l)
    norm_transpose(C, xt[0], b_xt[0], hT, b_hT, ("ffn_norm", l), NB, scr)
    for ti in range(NT):
        sl = ti % 2
        if ti + 1 < NT:
            load_x(ti + 1)
        for g in range(12):
            i = ti * 12 + g
            if i + 1 < len(wu_seq):
                load_wu(i + 1)
            wsl = i % 2
            for j in range(2):
                c = g * 2 + j
                res = []
                for gv in range(2):
                    fc = c + 24 * gv
                    pi = C.mm_rr % 4
                    C.mm_rr += 1
                    ps = C.psum[pi]
                    for kc in range(8):
                        P.op("pe", lambda e, kc=kc, ps=ps, gv=gv, j=j, wsl=wsl: e.matmul(
                            out=ps[:, 0:TT], lhsT=wu[wsl][:, gv, kc, j * 128:(j + 1) * 128], rhs=hT[:, kc, :],
                            start=(kc == 0), stop=(kc == 7)),
                            reads=[b_wu[wsl][gv], b_hT], writes=[C.b_psum[pi]])
                    ui = C.u_rr % 4
                    C.u_rr += 1
                    u, bu, cc, bc = u_sb[ui], b_u[ui], cb[ui], b_c[ui]
                    P.op("pool", lambda e, u=u, fc=fc: e.tensor_copy(out=u[:, 0:2], in_=halo[:, fc, :]),
                         reads=[b_halo[fc]], writes=[bu])
                    P.op("act", lambda e, u=u, ps=ps: e.activation(out=u[:, 2:TT + 2], in_=ps[:, 0:TT], func=AF.Copy),
                         reads=[C.b_psum[pi]], writes=[bu])
                    P.op("pool", lambda e, u=u, fc=fc: e.tensor_copy(out=halo[:, fc, :], in_=u[:, TT:TT + 2]),
                         reads=[bu], writes=[b_halo[fc]])
                    P.op("dve", lambda e, u=u, cc=cc, fc=fc: e.tensor_scalar(
                        out=cc[:, :], in0=u[:, 2:TT + 2], scalar1=V(C, cwk + (2,), fc), scalar2=V(C, cbk, fc),
                        op0=ALU.mult, op1=ALU.add), reads=[bu, C.b_vt], writes=[bc])
                    P.op("dve", lambda e, u=u, cc=cc, fc=fc: e.scalar_tensor_tensor(
                        out=cc[:, :], in0=u[:, 1:TT + 1], scalar=V(C, cwk + (1,), fc), in1=cc[:, :],
                        op0=ALU.mult, op1=ALU.add), reads=[bu, bc, C.b_vt], writes=[bc])
                    P.op("dve", lambda e, u=u, cc=cc, fc=fc: e.scalar_tensor_tensor(
                        out=cc[:, :], in0=u[:, 0:TT], scalar=V(C, cwk + (0,), fc), in1=cc[:, :],
                        op0=ALU.mult, op1=ALU.add), reads=[bu, bc, C.b_vt], writes=[bc])
                    res.append((cc, bc))
                (cg, bcg), (cv, bcv) = res
                P.op("act", lambda e, cg=cg: e.activation(out=cg[:, :], in_=cg[:, :], func=AF.Gelu_apprx_tanh),
                     reads=[bcg], writes=[bcg])
                P.op("dve", lambda e, cg=cg, cv=cv, c=c: e.tensor_tensor(out=gT[:, c, :], in0=cg[:, :], in1=cv[:, :], op=ALU.mult),
                     reads=[bcg, bcv], writes=[b_gT[c]])
        if ti + 1 < NT:
            norm_transpose(C, xt[1 - sl], b_xt[1 - sl], hT, b_hT, ("ffn_norm", l), NB, scr)
        for b in range(NB):
            for hf in range(2):
                pi = 4 + (C.dn_rr % 2)
                C.dn_rr += 1
                ps = C.psum[pi]
                for fc in range(24):
                    P.op("pe", lambda e, fc=fc, ps=ps, b=b, hf=hf, sl=sl: e.matmul(
                        out=ps[:, 0:512], lhsT=gT[:, fc, b * 128:(b + 1) * 128], rhs=wd[:, fc, hf * 512:(hf + 1) * 512],
                        start=(fc == 0), stop=(fc == 23)),
                        reads=[b_gT[fc], b_wd[fc // 6]], writes=[C.b_psum[pi]])
                P.op("dve", lambda e, ps=ps, b=b, hf=hf, sl=sl: e.tensor_tensor(
                    out=xt[sl][:, b, hf * 512:(hf + 1) * 512], in0=ps[:, 0:512], in1=xt[sl][:, b, hf * 512:(hf + 1) * 512],
                    op=ALU.add), reads=[C.b_psum[pi], b_xt[sl]], writes=[b_xt[sl]])
        P.dma("sp", lambda e, ti=ti, sl=sl: e.dma_start(out=dstv[ti], in_=xt[sl][:, :, :]), s_o[sl], reads=[b_xt[sl]], writes=[b_dst[ti]])


class Rot:
    def __init__(self, A, n, free_shape, dt, name="rot"):
        self.aps = [A.alloc(free_shape, dt) for _ in range(n)]
        self.bufs = [Buf("%s%d" % (name, i)) for i in range(n)]
        self.i = 0

    def get(self):
        k = self.i % len(self.aps)
        self.i += 1
        return self.aps[k], self.bufs[k]


def mm_psum(C):
    pi = C.mm_rr % 4
    C.mm_rr += 1
    return C.psum[pi], C.b_psum[pi]


def odd_phase(C, src, b_src, dst, b_dst):
    P, nc, A, t = C.P, C.nc, C.arena, C.t
    P.barrier()
    A.reset()
    NB = TT // 128
    NBK = 10
    w_in = A.alloc([8, 2 * LRU_W], BF16)
    b_win = [Buf("win%d" % i) for i in range(8)]
    w_out = A.alloc([NBK, D], BF16)
    b_wout = [Buf("wout0"), Buf("wout1")]
    wa = A.alloc([NBK, 128], BF16)
    wx = A.alloc([NBK, 128], BF16)
    b_wa, b_wx = Buf("wa"), Buf("wx")
    xt = [A.alloc([NB, 1024], F32) for _ in range(2)]
    b_xt = [Buf("xt0"), Buf("xt1")]
    hT = A.alloc([8, TT], BF16)
    b_hT = Buf("hT")
    mixT = A.alloc([NBK, TT], BF16)
    b_mix = [Buf("mix%d" % i) for i in range(NBK)]
    halo = A.alloc([NBK, 4], F32)
    b_halo = [Buf("halo%d" % i) for i in range(NBK)]
    hprev = A.alloc([NBK, 2], F32)
    b_hprev = [Buf("hprev%d" % i) for i in range(NBK)]
    Lt = A.alloc([NBK], F32)
    hb = A.alloc([2, NBK], F32)
    b_L = Buf("L")
    halfc = A.alloc([TT], F32)
    b_half = Buf("half")
    scr = dict(ss=A.alloc([8], F32), rs=A.alloc([8], F32), xn=A.alloc([NB, 1024], BF16), junk=A.alloc([1024], BF16),
               b_ss=Buf("ss"), b_xn=[Buf("xn%d" % i) for i in range(NB)], b_junk=Buf("junk"))
    r_y = Rot(A, 3, [TT], F32, "y")
    r_u = Rot(A, 2, [TT + 4], F32, "u")
    r_xc = Rot(A, 3, [TT], F32, "xc")
    r_xcb = Rot(A, 3, [TT], BF16, "xcb")
    r_r = Rot(A, 2, [TT], F32, "r")
    r_gi = Rot(A, 2, [TT], F32, "gi")
    r_a = Rot(A, 2, [TT], F32, "a")
    r_sq = Rot(A, 2, [TT], F32, "sq")
    r_hs = Rot(A, 2, [TT], F32, "hs")
    sems = [P.new_dma_sem() for _ in range(14)]
    s_x = [P.new_dma_sem(), P.new_dma_sem()]
    s_o = [P.new_dma_sem(), P.new_dma_sem()]

    src_in = t["odd_w_in"][0].rearrange("(kc p) f -> p kc f", p=128)
    for kc in range(8):
        P.dma("pool", lambda e, kc=kc: e.dma_start(out=w_in[:, kc, :], in_=src_in[:, kc, :]), sems[kc], writes=[b_win[kc]])
    P.dma("pool", lambda e: e.dma_start(out=wa[:, :, :], in_=t["odd_gate_a_w"][0].rearrange("n c d -> c n d")),
          sems[8], writes=[b_wa])
    P.dma("pool", lambda e: e.dma_start(out=wx[:, :, :], in_=t["odd_gate_x_w"][0].rearrange("n c d -> c n d")),
          sems[9], writes=[b_wx])
    src_out = t["odd_w_out"][0].rearrange("(n p) d -> p n d", p=128)
    for q in range(2):
        P.dma("pool", lambda e, q=q: e.dma_start(out=w_out[:, q * 5:(q + 1) * 5, :], in_=src_out[:, q * 5:(q + 1) * 5, :]),
              sems[10 + q], writes=[b_wout[q]])
    P.op("pool", lambda e: e.memset(halo[:, :, :], 0.0), writes=b_halo)
    P.op("pool", lambda e: e.memset(hprev[:, :, :], 0.0), writes=b_hprev)
    P.op("pool", lambda e: e.memset(halfc[:, :], 0.25), writes=[b_half])
    lamc0 = C.vcol[("odd_lam",)]
    P.op("act", lambda e, lamc0=lamc0: e.activation(out=Lt[:, :], in_=C.vt[:, lamc0:lamc0 + NBK], func=AF.Exp, scale=-1.0),
         reads=[C.b_vt], writes=[b_L])
    P.op("act", lambda e: e.activation(out=Lt[:, :], in_=Lt[:, :], func=AF.Ln, bias=C.one_t[:, 0:1]),
         reads=[b_L, C.b_const], writes=[b_L])
    P.op("dve", lambda e: e.tensor_scalar(out=Lt[:, :], in0=Lt[:, :], scalar1=-4.0, scalar2=None, op0=ALU.mult), reads=[b_L], writes=[b_L])
    ca, cx = C.vcol[("odd_ba",)], C.vcol[("odd_bx",)]
    P.op("dve", lambda e, ca=ca: e.tensor_scalar(out=hb[:, 0, :], in0=C.vt[:, ca:ca + NBK], scalar1=0.5, scalar2=None, op0=ALU.mult),
         reads=[C.b_vt], writes=[b_L])
    P.op("dve", lambda e, cx=cx: e.tensor_scalar(out=hb[:, 1, :], in0=C.vt[:, cx:cx + NBK], scalar1=0.5, scalar2=None, op0=ALU.mult),
         reads=[C.b_vt], writes=[b_L])

    srcv = src.rearrange("(t b p) d -> t p b d", b=NB, p=128)
    dstv = dst.rearrange("(t b p) d -> t p b d", b=NB, p=128)

    def load_x(ti):
        sl = ti % 2
        P.dma("sp", lambda e: e.dma_start(out=xt[sl][:, :, :], in_=srcv[ti]), s_x[sl], reads=[b_src[ti]], writes=[b_xt[sl]])

    def S1(n):
        ps, bps = mm_psum(C)
        for kc in range(8):
            P.op("pe", lambda e, kc=kc: e.matmul(out=ps[:, 0:TT], lhsT=w_in[:, kc, n * 128:(n + 1) * 128],
                                                 rhs=hT[:, kc, :], start=(kc == 0), stop=(kc == 7)),
                 reads=[b_win[kc], b_hT], writes=[bps])
        y, by = r_y.get()
        P.op("act", lambda e: e.activation(out=y[:, :], in_=ps[:, 0:TT], func=AF.Gelu_apprx_tanh), reads=[bps], writes=[by])
        ps2, bps2 = mm_psum(C)
        for kc in range(8):
            P.op("pe", lambda e, kc=kc: e.matmul(out=ps2[:, 0:TT], lhsT=w_in[:, kc, LRU_W + n * 128:LRU_W + (n + 1) * 128],
                                                 rhs=hT[:, kc, :], start=(kc == 0), stop=(kc == 7)),
                 reads=[b_win[kc], b_hT], writes=[bps2])
        u, bu = r_u.get()
        P.op("pool", lambda e: e.tensor_copy(out=u[:, 0:3], in_=halo[:, n, 0:3]), reads=[b_halo[n]], writes=[bu])
        P.op("act", lambda e: e.activation(out=u[:, 3:TT + 3], in_=ps2[:, 0:TT], func=AF.Copy), reads=[bps2], writes=[bu])
        P.op("pool", lambda e: e.tensor_copy(out=halo[:, n, 0:3], in_=u[:, TT:TT + 3]), reads=[bu], writes=[b_halo[n]])
        xc, bxc = r_xc.get()
        P.op("dve", lambda e: e.tensor_scalar(out=xc[:, :], in0=u[:, 3:TT + 3], scalar1=V(C, ("odd_cw", 3), n),
                                              scalar2=V(C, ("odd_cb",), n), op0=ALU.mult, op1=ALU.add),
             reads=[bu, C.b_vt], writes=[bxc])
        for j in range(3):
            P.op("dve", lambda e, j=j: e.scalar_tensor_tensor(out=xc[:, :], in0=u[:, j:TT + j], scalar=V(C, ("odd_cw", j), n),
                                                               in1=xc[:, :], op0=ALU.mult, op1=ALU.add),
                 reads=[bu, bxc, C.b_vt], writes=[bxc])
        xcb, bxcb = r_xcb.get()
        P.op("act", lambda e: e.activation(out=xcb[:, :], in_=xc[:, :], func=AF.Copy), reads=[bxc], writes=[bxcb])
        return dict(y=y, by=by, xc=xc, bxc=bxc, xcb=xcb, bxcb=bxcb)

    def S2(n, st):
        y, by, xc, bxc, xcb, bxcb = st["y"], st["by"], st["xc"], st["bxc"], st["xcb"], st["bxcb"]
        psa, bpsa = mm_psum(C)
        P.op("pe", lambda e: e.matmul(out=psa[:, 0:TT], lhsT=wa[:, n, :], rhs=xcb[:, :], start=True, stop=True),
             reads=[b_wa, bxcb], writes=[bpsa])
        psg, bpsg = mm_psum(C)
        P.op("pe", lambda e: e.matmul(out=psg[:, 0:TT], lhsT=wx[:, n, :], rhs=xcb[:, :], start=True, stop=True),
             reads=[b_wx, bxcb], writes=[bpsg])
        r, br = r_r.get()
        gi, bgi = r_gi.get()
        P.op("act", lambda e: e.activation(out=r[:, :], in_=psa[:, 0:TT], func=AF.Tanh, scale=0.5, bias=hb[:, 0, n:n + 1]),
             reads=[bpsa, b_L], writes=[br])
        P.op("act", lambda e: e.activation(out=gi[:, :], in_=psg[:, 0:TT], func=AF.Tanh, scale=0.5, bias=hb[:, 1, n:n + 1]),
             reads=[bpsg, b_L], writes=[bgi])
        a, ba = r_a.get()
        sq, bsq = r_sq.get()
        P.op("act", lambda e: e.activation(out=a[:, :], in_=r[:, :], func=AF.Exp, scale=Lt[:, n:n + 1], bias=Lt[:, n:n + 1]),
             reads=[br, b_L], writes=[ba])
        P.op("pool", lambda e: e.tensor_tensor(out=sq[:, :], in0=a[:, :], in1=a[:, :], op=ALU.mult), reads=[ba], writes=[bsq])
        P.op("act", lambda e: e.activation(out=sq[:, :], in_=sq[:, :], func=AF.Sqrt, scale=-0.25, bias=halfc[:, 0:1]),
             reads=[bsq, b_half], writes=[bsq])
        P.op("dve", lambda e: e.scalar_tensor_tensor(out=gi[:, :], in0=gi[:, :], scalar=1.0, in1=xc[:, :], op0=ALU.add, op1=ALU.mult),
             reads=[bgi, bxc], writes=[bgi])
        P.op("dve", lambda e: e.tensor_tensor(out=gi[:, :], in0=gi[:, :], in1=sq[:, :], op=ALU.mult), reads=[bgi, bsq], writes=[bgi])
        hs, bhs = r_hs.get()
        P.op("dve", lambda e: e.tensor_tensor_scan(out=hs[:, :], data0=a[:, :], data1=gi[:, :], initial=hprev[:, n, 0:1],
                                                   op0=ALU.mult, op1=ALU.add), reads=[ba, bgi, b_hprev[n]], writes=[bhs])
        P.op("pool", lambda e: e.tensor_copy(out=hprev[:, n, 0:1], in_=hs[:, TT - 1:TT]), reads=[bhs], writes=[b_hprev[n]])
        P.op("dve", lambda e: e.tensor_tensor(out=mixT[:, n, :], in0=hs[:, :], in1=y[:, :], op=ALU.mult),
             reads=[bhs, by], writes=[b_mix[n]])

    load_x(0)
    norm_transpose(C, xt[0], b_xt[0], hT, b_hT, ("odd_norm",), NB, scr)
    for ti in range(NT):
        sl = ti % 2
        if ti + 1 < NT:
            load_x(ti + 1)
        prev = None
        for n in range(NBK):
            st = S1(n)
            if prev is not None:
                S2(*prev)
            prev = (n, st)
        S2(*prev)
        if ti + 1 < NT:
            norm_transpose(C, xt[1 - sl], b_xt[1 - sl], hT, b_hT, ("odd_norm",), NB, scr)
        for b in range(NB):
            for hf in range(2):
                pi = 4 + (C.dn_rr % 2)
                C.dn_rr += 1
                ps = C.psum[pi]
                for n in range(NBK):
                    P.op("pe", lambda e, n=n, ps=ps, b=b, hf=hf: e.matmul(
                        out=ps[:, 0:512], lhsT=mixT[:, n, b * 128:(b + 1) * 128], rhs=w_out[:, n, hf * 512:(hf + 1) * 512],
                        start=(n == 0), stop=(n == NBK - 1)),
                        reads=[b_mix[n], b_wout[n // 5]], writes=[C.b_psum[pi]])
                P.op("dve", lambda e, ps=ps, b=b, hf=hf, sl=sl: e.tensor_tensor(
                    out=xt[sl][:, b, hf * 512:(hf + 1) * 512], in0=ps[:, 0:512], in1=xt[sl][:, b, hf * 512:(hf + 1) * 512],
                    op=ALU.add), reads=[C.b_psum[pi], b_xt[sl]], writes=[b_xt[sl]])
        P.dma("sp", lambda e, ti=ti, sl=sl: e.dma_start(out=dstv[ti], in_=xt[sl][:, :, :]), s_o[sl],
              reads=[b_xt[sl]], writes=[b_dst[ti]])


def evenA_phase(C, src, b_src, aoT, b_ao):
    P, nc, A, t = C.P, C.nc, C.arena, C.t
    P.barrier()
    A.reset()
    NB = TT // 128
    NCH = TT // 64
    w = A.alloc([8, 2048], BF16)
    b_w = [Buf("w%d" % i) for i in range(8)]
    xt = [A.alloc([NB, 1024], F32) for _ in range(2)]
    b_xt = [Buf("xt0"), Buf("xt1")]
    hT = A.alloc([8, TT], BF16)
    b_hT = Buf("hT")
    scr = dict(ss=A.alloc([8], F32), rs=A.alloc([8], F32), xn=A.alloc([NB, 1024], BF16), junk=A.alloc([1024], BF16),
               b_ss=Buf("ss"), b_xn=[Buf("xn%d" % i) for i in range(NB)], b_junk=Buf("junk"))
    lbt = A.alloc([3, 4], F32)
    lb = A.alloc([4], F32)
    oml = A.alloc([4], F32)
    b_lb = Buf("lb")
    rmask = A.alloc([TT], F32)
    b_rmask = Buf("rmask")
    ones_b = A.alloc([128], BF16)
    causT = A.alloc([64], F32)
    b_cm = Buf("cm")
    St = A.alloc([4, 128], F32)
    Sb = A.alloc([4, 128], BF16)
    b_S = [Buf("S%d" % h) for h in range(4)]
    b_Sb = [Buf("Sb%d" % h) for h in range(4)]
    vtok = A.alloc([NCH, 512], BF16)
    b_vtok = [Buf("vtok%d" % c) for c in range(NCH)]
    eb = A.alloc([4, TT], F32)
    b_eb = [Buf("eb%d" % h) for h in range(4)]
    kf = A.alloc([4, TT], F32)
    b_kf = [Buf("kf%d" % h) for h in range(4)]
    ktb = A.alloc([4, TT], BF16)
    b_ktb = [Buf("ktb%d" % h) for h in range(4)]
    qtb = A.alloc([4, TT], BF16)
    b_qtb = [Buf("qtb%d" % h) for h in range(4)]
    sg = A.alloc([4, TT], F32)
    b_sg = [Buf("sg%d" % h) for h in range(4)]
    osb = A.alloc([4, TT], F32)
    b_osb = [Buf("osb%d" % h) for h in range(4)]
    ao = [A.alloc([4, TT], BF16) for _ in range(2)]
    b_aot = [Buf("ao0"), Buf("ao1")]
    r_f = Rot(A, 2, [TT], F32, "f")
    r_g = Rot(A, 2, [TT], F32, "g")
    r_b = Rot(A, 2, [TT], F32, "b")
    r_enb = Rot(A, 2, [TT], F32, "enb")
    r_qs = Rot(A, 2, [TT], F32, "qs")
    r_khb = Rot(A, 4, [64], BF16, "khb")
    r_kht = Rot(A, 4, [128], BF16, "kht")
    r_pt = Rot(A, 4, [64], BF16, "pt")
    r_osq = Rot(A, 2, [TT], BF16, "osq")
    r_rt = Rot(A, 2, [TT], F32, "rt")
    sems = [P.new_dma_sem() for _ in range(8)]
    s_x = [P.new_dma_sem(), P.new_dma_sem()]
    s_o = [P.new_dma_sem(), P.new_dma_sem()]
    s_m = P.new_dma_sem()

    src_in = t["even_w_in"][0].rearrange("(kc p) f -> p kc f", p=128)
    for kc in range(8):
        P.dma("pool", lambda e, kc=kc: e.dma_start(out=w[:, kc, :], in_=src_in[:, kc, 0:2048]), sems[kc], writes=[b_w[kc]])
    P.dma("sp", lambda e: e.dma_start(out=causT[0:64, :], in_=t["c_causT"][:, :]), s_m, writes=[b_cm])
    lbc0 = C.vcol[("lb", 0)]
    P.op("act", lambda e, lbc0=lbc0: e.activation(out=lbt[:, :, :].rearrange("p a b -> p (a b)"), in_=C.vt[:, lbc0:lbc0 + 12], func=AF.Exp),
         reads=[C.b_vt], writes=[b_lb])
    P.op("dve", lambda e: e.tensor_tensor(out=oml[:, :], in0=lbt[:, 0, :], in1=lbt[:, 1, :], op=ALU.add), reads=[b_lb], writes=[b_lb])
    P.op("dve", lambda e: e.tensor_tensor(out=oml[:, :], in0=oml[:, :], in1=lbt[:, 2, :], op=ALU.add), reads=[b_lb], writes=[b_lb])
    P.op("dve", lambda e: e.reciprocal(out=oml[:, :], in_=oml[:, :]), reads=[b_lb], writes=[b_lb])
    P.op("dve", lambda e: e.tensor_tensor(out=lb[:, :], in0=lbt[:, 0, :], in1=oml[:, :], op=ALU.mult), reads=[b_lb], writes=[b_lb])
    P.op("dve", lambda e: e.tensor_scalar(out=oml[:, :], in0=lb[:, :], scalar1=-1.0, scalar2=1.0, op0=ALU.mult, op1=ALU.add),
         reads=[b_lb], writes=[b_lb])
    P.op("pool", lambda e: e.memset(rmask[:, :], 1.0), writes=[b_rmask])
    P.op("pool", lambda e: e.memset(rmask[:, :].rearrange("p (c t) -> p c t", t=64)[:, :, 0:1], 0.0), writes=[b_rmask])
    P.op("pool", lambda e: e.memset(ones_b[:, :], 1.0), writes=[b_rmask])
    P.op("pool", lambda e: e.memset(St[:, :, :], 0.0), writes=b_S)
    P.op("pool", lambda e: e.memset(Sb[:, :, :], 0.0), writes=b_Sb)

    srcv = src.rearrange("(t b p) d -> t p b d", b=NB, p=128)
    aov = aoT.rearrange("h p s -> p h s")

    def load_x(ti):
        sl = ti % 2
        P.dma("sp", lambda e: e.dma_start(out=xt[sl][:, :, :], in_=srcv[ti]), s_x[sl], reads=[b_src[ti]], writes=[b_xt[sl]])

    def proj(col0, M=128):
        ps, bps = mm_psum(C)
        for kc in range(8):
            P.op("pe", lambda e, kc=kc, ps=ps: e.matmul(out=ps[0:M, 0:TT], lhsT=w[:, kc, col0:col0 + M], rhs=hT[:, kc, :],
                                                        start=(kc == 0), stop=(kc == 7)),
                 reads=[b_w[kc], b_hT], writes=[bps])
        return ps, bps

    load_x(0)
    for ti in range(NT):
        sl = ti % 2
        if ti + 1 < NT:
            load_x(ti + 1)
        norm_transpose(C, xt[sl], b_xt[sl], hT, b_hT, ("even_norm",), NB, scr)
        for c in range(NCH):
            ps, bps = mm_psum(C)
            for kc in range(8):
                P.op("pe", lambda e, kc=kc, ps=ps, c=c: e.matmul(out=ps[0:64, 0:512], lhsT=hT[:, kc, c * 64:(c + 1) * 64],
                                                                  rhs=w[:, kc, 1024:1536], start=(kc == 0), stop=(kc == 7)),
                     reads=[b_w[kc], b_hT], writes=[bps])
            P.op("act", lambda e, ps=ps, c=c: e.activation(out=vtok[0:64, c, :], in_=ps[0:64, 0:512], func=AF.Copy),
                 reads=[bps], writes=[b_vtok[c]])
        for h in range(4):
            ps, bps = proj(512 + h * 128)
            f, bf = r_f.get()
            P.op("act", lambda e, f=f, ps=ps: e.activation(out=f[:, :], in_=ps[:, 0:TT], func=AF.Sigmoid), reads=[bps], writes=[bf])
            P.op("dve", lambda e, f=f, h=h: e.tensor_scalar(out=f[:, :], in0=f[:, :], scalar1=oml[:, h:h + 1], scalar2=lb[:, h:h + 1],
                                                            op0=ALU.mult, op1=ALU.add), reads=[bf, b_lb], writes=[bf])
            g, bg = r_g.get()
            P.op("act", lambda e, f=f, g=g: e.activation(out=g[:, :], in_=f[:, :], func=AF.Ln), reads=[bf], writes=[bg])
            P.op("dve", lambda e, f=f: e.tensor_scalar(out=f[:, :], in0=f[:, :], scalar1=-1.0, scalar2=1.0, op0=ALU.mult, op1=ALU.add),
                 reads=[bf], writes=[bf])
            bb, bbb = r_b.get()
            P.op("dve", lambda e, bb=bb, g=g: e.tensor_tensor_scan(out=bb[:, :], data0=rmask[:, :], data1=g[:, :], initial=0.0,
                                                                   op0=ALU.mult, op1=ALU.add), reads=[bg, b_rmask], writes=[bbb])
            P.op("act", lambda e, bb=bb, h=h: e.activation(out=eb[:, h, :], in_=bb[:, :], func=AF.Exp), reads=[bbb], writes=[b_eb[h]])
            enb, benb = r_enb.get()
            P.op("act", lambda e, bb=bb, enb=enb: e.activation(out=enb[:, :], in_=bb[:, :], func=AF.Exp, scale=-1.0),
                 reads=[bbb], writes=[benb])
            P.op("dve", lambda e, f=f, enb=enb, h=h: e.tensor_tensor(out=kf[:, h, :], in0=f[:, :], in1=enb[:, :], op=ALU.mult),
                 reads=[bf, benb], writes=[b_kf[h]])
            P.op("pool", lambda e, h=h: e.tensor_copy(out=ktb[:, h, :], in_=kf[:, h, :]), reads=[b_kf[h]], writes=[b_ktb[h]])
            ps, bps = proj(h * 128)
            qs, bqs = r_qs.get()
            P.op("act", lambda e, qs=qs, ps=ps: e.activation(out=qs[:, :], in_=ps[:, 0:TT], func=AF.Silu), reads=[bps], writes=[bqs])
            P.op("dve", lambda e, qs=qs, h=h: e.tensor_tensor(out=qtb[:, h, :], in0=qs[:, :], in1=eb[:, h, :], op=ALU.mult),
                 reads=[bqs, b_eb[h]], writes=[b_qtb[h]])
            ps, bps = proj(1536 + h * 128)
            P.op("act", lambda e, ps=ps, h=h: e.activation(out=sg[:, h, :], in_=ps[:, 0:TT], func=AF.Silu), reads=[bps], writes=[b_sg[h]])
        for c in range(NCH):
            c0, c1, last = c * 64, (c + 1) * 64, c * 64 + 63
            for h in range(4):
                khb, bkhb = r_khb.get()
                P.op("dve", lambda e, khb=khb, h=h, c0=c0, c1=c1, last=last: e.tensor_scalar(
                    out=khb[:, :], in0=kf[:, h, c0:c1], scalar1=eb[:, h, last:last + 1], scalar2=None, op0=ALU.mult),
                    reads=[b_kf[h], b_eb[h]], writes=[bkhb])
                pi = C.tp_rr % 2
                C.tp_rr += 1
                pst = C.psum_tp[pi]
                P.op("pe", lambda e, pst=pst, khb=khb: e.transpose(out=pst[0:64, 0:128], in_=khb[:, :], identity=C.ident_b[:, :]),
                     reads=[bkhb, C.b_const], writes=[C.b_psum_tp[pi]])
                kht, bkht = r_kht.get()
                P.op("act", lambda e, pst=pst, kht=kht: e.activation(out=kht[0:64, :], in_=pst[0:64, 0:128], func=AF.Copy),
                     reads=[C.b_psum_tp[pi]], writes=[bkht])
                ps, bps = mm_psum(C)
                P.op("pe", lambda e, ps=ps, h=h, c0=c0, c1=c1: e.matmul(out=ps[0:64, 0:64], lhsT=ktb[:, h, c0:c1], rhs=qtb[:, h, c0:c1],
                                                                        start=True, stop=True),
                     reads=[b_ktb[h], b_qtb[h]], writes=[bps])
                pt, bpt = r_pt.get()
                P.op("dve", lambda e, ps=ps, pt=pt: e.tensor_tensor(out=pt[0:64, :], in0=ps[0:64, 0:64], in1=causT[0:64, :], op=ALU.mult),
                     reads=[bps, b_cm], writes=[bpt])
                pso, bpso = mm_psum(C)
                P.op("pe", lambda e, pso=pso, h=h, c0=c0, c1=c1: e.matmul(out=pso[:, 0:64], lhsT=Sb[:, h, :], rhs=qtb[:, h, c0:c1],
                                                                          start=True, stop=False),
                     reads=[b_Sb[h], b_qtb[h]], writes=[bpso])
                P.op("pe", lambda e, pso=pso, h=h, c=c, pt=pt: e.matmul(out=pso[:, 0:64], lhsT=vtok[0:64, c, h * 128:(h + 1) * 128],
                                                                        rhs=pt[0:64, :], start=False, stop=True),
                     reads=[b_vtok[c], bpt], writes=[bpso])
                P.op("act", lambda e, pso=pso, h=h, c0=c0, c1=c1: e.activation(out=osb[:, h, c0:c1], in_=pso[:, 0:64], func=AF.Copy),
                     reads=[bpso], writes=[b_osb[h]])
                psu, bpsu = mm_psum(C)
                P.op("pe", lambda e, psu=psu, kht=kht, h=h, c=c: e.matmul(out=psu[:, 0:128], lhsT=kht[0:64, :],
                                                                          rhs=vtok[0:64, c, h * 128:(h + 1) * 128], start=True, stop=True),
                     reads=[bkht, b_vtok[c]], writes=[bpsu])
                P.op("dve", lambda e, psu=psu, h=h, last=last: e.scalar_tensor_tensor(
                    out=St[:, h, :], in0=St[:, h, :], scalar=eb[:, h, last:last + 1], in1=psu[:, 0:128], op0=ALU.mult, op1=ALU.add),
                    reads=[b_S[h], b_eb[h], bpsu], writes=[b_S[h]])
                P.op("pool", lambda e, h=h: e.tensor_copy(out=Sb[:, h, :], in_=St[:, h, :]), reads=[b_S[h]], writes=[b_Sb[h]])
        aot, baot = ao[sl], b_aot[sl]
        for h in range(4):
            osq, bosq = r_osq.get()
            P.op("act", lambda e, osq=osq, h=h: e.activation(out=osq[:, :], in_=osb[:, h, :], func=AF.Square), reads=[b_osb[h]], writes=[bosq])
            ps, bps = mm_psum(C)
            P.op("pe", lambda e, ps=ps, osq=osq: e.matmul(out=ps[:, 0:TT], lhsT=ones_b[:, :], rhs=osq[:, :], start=True, stop=True),
                 reads=[bosq, b_rmask], writes=[bps])
            rt, brt = r_rt.get()
            P.op("act", lambda e, ps=ps, rt=rt: e.activation(out=rt[:, :], in_=ps[:, 0:TT], func=AF.Sqrt, scale=1.0 / 128, bias=C.eps_t[:, 0:1]),
                 reads=[bps, C.b_const], writes=[brt])
            P.op("dve", lambda e, rt=rt: e.reciprocal(out=rt[:, :], in_=rt[:, :]), reads=[brt], writes=[brt])
            P.op("dve", lambda e, rt=rt, h=h: e.tensor_tensor(out=rt[:, :], in0=rt[:, :], in1=osb[:, h, :], op=ALU.mult),
                 reads=[brt, b_osb[h]], writes=[brt])
            P.op("dve", lambda e, rt=rt, h=h, aot=aot: e.scalar_tensor_tensor(
                out=aot[:, h, :], in0=rt[:, :], scalar=V(C, ("a_out_norm",), h), in1=sg[:, h, :], op0=ALU.mult, op1=ALU.mult),
                reads=[brt, b_sg[h], C.b_vt], writes=[baot])
            dsel = getattr(C, "dbg_sel", None)
            if dsel is not None:
                srcs = {"lbd": (eb, b_eb), "eb": (eb, b_eb), "osb": (osb, b_osb), "sg": (sg, b_sg), "qtb": (qtb, b_qtb), "ktb": (ktb, b_ktb)}
                sa, sb = srcs[dsel]
                P.op("dve", lambda e, h=h, aot=aot, sa=sa: e.tensor_copy(out=aot[:, h, :], in_=sa[:, h, :]),
                     reads=[sb[h], baot], writes=[baot])
        if getattr(C, "dbg_sel", None) == "lbd":
            P.op("dve", lambda e, aot=aot: e.tensor_copy(out=aot[:, 0, 0:4], in_=lb[:, :]), reads=[b_lb, baot], writes=[baot])
            P.op("dve", lambda e, aot=aot: e.tensor_copy(out=aot[:, 0, 4:8], in_=oml[:, :]), reads=[b_lb, baot], writes=[baot])
            P.op("dve", lambda e, aot=aot: e.tensor_copy(out=aot[:, 0, 8:20], in_=lbt[:, :, :].rearrange("p a b -> p (a b)")), reads=[b_lb, baot], writes=[baot])
            P.op("dve", lambda e, aot=aot: e.tensor_copy(out=aot[:, 0, 20:32], in_=C.vt[:, lbc0:lbc0 + 12]), reads=[C.b_vt, baot], writes=[baot])
        P.dma("sp", lambda e, ti=ti, aot=aot: e.dma_start(out=aov[:, :, ti * TT:(ti + 1) * TT], in_=aot[:, :, :]), s_o[sl],
              reads=[baot], writes=[b_ao[ti]])


NIT = 12
TOPK = 256
NEG = -1.0e30
MBIG = 30000.0


def _interleave(gens):
    items = []
    for g, n in gens:
        items.append([g, max(n, 1), 0, True])
    total = max(it[1] for it in items) if items else 0
    for step in range(1, total + 1):
        for it in items:
            g, n, done, alive = it
            want = (step * n + total - 1) // total
            while alive and it[2] < want:
                try:
                    next(g)
                    it[2] += 1
                except StopIteration:
                    it[3] = False
                    alive = False
    for it in items:
        if it[3]:
            for _ in it[0]:
                pass


def evenB_phase(C, src, b_src, aoT, b_ao, dst, b_dst, boT=None):
    P, nc, A, t = C.P, C.nc, C.arena, C.t
    P.barrier(skip=getattr(C, "prep_sems", ()))
    A.reset()
    NB = TT // 128
    NKT = S // 128
    wq = A.alloc([8, 512], BF16)
    wiq = A.alloc([8, 512], BF16)
    wk2 = A.alloc([8, 128], BF16)
    wik2 = A.alloc([8, 128], BF16)
    wvw = A.alloc([8, 72], BF16)
    w_out = A.alloc([8, D], BF16)
    b_wparts = [Buf("wp%d" % i) for i in range(10)]
    kiT2 = A.alloc([S], BF16)
    knT2 = A.alloc([S], BF16)
    vaug = A.alloc([NKT, 65], BF16)
    b_ki = [Buf("ki%d" % i) for i in range(NT)]
    b_kn = [Buf("kn%d" % i) for i in range(NT)]
    b_va = [Buf("va%d" % i) for i in range(NKT)]
    b_vones = Buf("vones")
    xt = A.alloc([NB, 1024], F32)
    b_xt = Buf("xt")
    hT = A.alloc([8, TT], BF16)
    b_hT = Buf("hT")
    scr = dict(ss=A.alloc([8], F32), rs=A.alloc([8], F32), xn=A.alloc([NB, 1024], BF16), junk=A.alloc([1024], BF16),
               b_ss=Buf("ss"), b_xn=[Buf("xn%d" % i) for i in range(NB)], b_junk=Buf("junk"))
    qiT = A.alloc([2, 4, TT], BF16)
    qnT = A.alloc([2, 4, TT], BF16)
    b_qi = [Buf("qi%d" % i) for i in range(4)]
    b_qn = [Buf("qn%d" % i) for i in range(4)]
    b_qz = Buf("qz")
    mix = A.alloc([8, TT], BF16)
    b_mixa = Buf("mixa")
    b_mixb = [Buf("mixb%d" % i) for i in range(NB)]
    wabs = A.alloc([NB, 8], F32)
    sgn = A.alloc([NB, 8], F32)
    b_wabs = [Buf("wabs%d" % i) for i in range(NB)]
    isc = [A.alloc([S], F32) for _ in range(2)]
    b_isc = [Buf("isc0"), Buf("isc1")]
    mask = A.alloc([S], BF16)
    b_mask = Buf("mask")
    maskT = [A.alloc([NKT, 128], BF16) for _ in range(2)]
    b_maskT = [[Buf("mT%d_%d" % (k, i)) for i in range(NKT // 4)] for k in range(2)]
    dsg = [A.alloc([8, 128], BF16) for _ in range(2)]
    b_dsg = [Buf("dsg0"), Buf("dsg1")]
    r_R = Rot(A, 3, [512], BF16, "R")
    r_E = Rot(A, 2, [8, 128], BF16, "E")
    r_Pm = Rot(A, 2, [8, 128], BF16, "Pm")
    bis = [A.alloc([8], F32) for _ in range(2)]
    b_bis = [Buf("bis0"), Buf("bis1")]
    wtab = [A.alloc([NIT + 2], F32) for _ in range(2)]
    pow2 = A.alloc([NIT + 2], F32)
    b_pow2 = Buf("pow2")
    botok = A.alloc([512], BF16)
    b_botok = Buf("botok")
    rc = A.alloc([8], F32)
    b_rc = Buf("rc")
    r_ksb = Rot(A, 1, [TT], F32, "ksb")
    r_sq = Rot(A, 1, [TT], BF16, "sq")
    r_rt = Rot(A, 1, [TT], F32, "rt")
    blk1 = A.alloc([128], BF16)
    kg2 = A.alloc([2], F32)
    b_g2 = Buf("g2")
    sems = [P.new_dma_sem() for _ in range(12)]
    s_x, s_o, s_a, s_bo = P.new_dma_sem(), P.new_dma_sem(), P.new_dma_sem(), P.new_dma_sem()

    src_in = t["even_w_in"][0].rearrange("(kc p) f -> p kc f", p=128)
    wo_src = t["even_w_out"][0].rearrange("(m p) d -> p m d", p=128)
    loads = [(wk2[:, :, 0:64], src_in[:, :, 2560:2624]), (wk2[:, :, 64:128], src_in[:, :, 2560:2624]),
             (wik2[:, :, 0:64], src_in[:, :, 3200:3264]), (wik2[:, :, 64:128], src_in[:, :, 3200:3264]),
             (wiq[:, :, :], src_in[:, :, 2688:3200]), (wq[:, :, :], src_in[:, :, 2048:2560]),
             (wvw[:, :, 0:64], src_in[:, :, 2624:2688]), (wvw[:, :, 64:72], src_in[:, :, 3264:3272]),
             (w_out[:, 0:4, :], wo_src[:, 0:4, :]), (w_out[:, 4:8, :], wo_src[:, 4:8, :])]
    for i, (o_, i_) in enumerate(loads):
        P.dma("pool", lambda e, o_=o_, i_=i_: e.dma_start(out=o_, in_=i_), sems[i], writes=[b_wparts[i]])
    b_wk2, b_wik2, b_wiq, b_wq, b_wvw, b_wo = b_wparts[0:2], b_wparts[2:4], [b_wparts[4]], [b_wparts[5]], b_wparts[6:8], b_wparts[8:10]
    ensure_prep(C)
    for i, (nm, col) in enumerate((("b_k_norm", 0), ("b_q_norm", 1))):
        for hf in range(2):
            P.dma("sp", lambda e, nm=nm, col=col, hf=hf: e.dma_start(
                out=kg2[hf * 64:(hf + 1) * 64, col:col + 1], in_=t[nm][0, :].rearrange("(p o) -> p o", o=1)),
                sems[10], writes=[b_g2])
    P.op("dve", lambda e: e.tensor_scalar(out=kg2[:, 1:2], in0=kg2[:, 1:2], scalar1=0.125, scalar2=None, op0=ALU.mult),
         reads=[b_g2], writes=[b_g2])
    P.op("pool", lambda e: e.memset(blk1[:, :], 0.0), writes=[b_pow2])
    P.op("pool", lambda e: e.memset(blk1[0:64, 0:64], 1.0), writes=[b_pow2])
    P.op("pool", lambda e: e.memset(blk1[64:128, 64:128], 1.0), writes=[b_pow2])
    for i in range(NIT + 2):
        P.op("pool", lambda e, i=i: e.memset(pow2[:, i:i + 1], 2.0 ** (-(i + 1))), writes=[b_pow2])
    P.op("pool", lambda e: e.memset(vaug[:, :, 64:65], 1.0), writes=[b_vones])
    P.op("pool", lambda e: e.memset(qiT[:, :, :, :], 0.0), writes=[b_qz])
    P.op("pool", lambda e: e.memset(qnT[:, :, :, :], 0.0), writes=[b_qz])

    srcv = src.rearrange("(t b p) d -> t p b d", b=NB, p=128)
    dstv = dst.rearrange("(t b p) d -> t p b d", b=NB, p=128)
    aov = aoT.rearrange("h p s -> p h s")

    def proj_fm(wt, bw, c0, M=128):
        ps, bps = mm_psum(C)
        for kc in range(8):
            P.op("pe", lambda e, kc=kc: e.matmul(out=ps[0:M, 0:TT], lhsT=wt[:, kc, c0:c0 + M], rhs=hT[:, kc, :],
                                                 start=(kc == 0), stop=(kc == 7)),
                 reads=list(bw) + [b_hT], writes=[bps])
        return ps, bps

    def qk_norm(ps, bps, gcol, outs):
        ksb, bksb = r_ksb.get()
        P.op("act", lambda e: e.activation(out=ksb[:, :], in_=ps[:, 0:TT], func=AF.Copy), reads=[bps], writes=[bksb])
        sq, bsq = r_sq.get()
        P.op("act", lambda e: e.activation(out=sq[:, :], in_=ksb[:, :], func=AF.Square), reads=[bksb], writes=[bsq])
        ps2, bps2 = mm_psum(C)
        P.op("pe", lambda e: e.matmul(out=ps2[:, 0:TT], lhsT=blk1[:, :], rhs=sq[:, :], start=True, stop=True),
             reads=[bsq, b_pow2], writes=[bps2])
        rt, brt = r_rt.get()
        P.op("act", lambda e: e.activation(out=rt[:, :], in_=ps2[:, 0:TT], func=AF.Sqrt, scale=1.0 / 64, bias=C.eps_t[:, 0:1]),
             reads=[bps2, C.b_const], writes=[brt])
        P.op("dve", lambda e: e.reciprocal(out=rt[:, :], in_=rt[:, :]), reads=[brt], writes=[brt])
        for (oap, p0, p1, bo) in outs:
            P.op("dve", lambda e, oap=oap, p0=p0, p1=p1: e.scalar_tensor_tensor(
                out=oap, in0=ksb[p0:p1, :], scalar=kg2[p0:p1, gcol:gcol + 1], in1=rt[p0:p1, :], op0=ALU.mult, op1=ALU.mult),
                reads=[bksb, brt, b_g2, b_qz], writes=[bo])

    def tok_proj(ti, b):
        J = ti * NB + b
        ps, bps = mm_psum(C)
        for kc in range(8):
            P.op("pe", lambda e, kc=kc: e.matmul(out=ps[:, 0:72], lhsT=hT[:, kc, b * 128:(b + 1) * 128], rhs=wvw[:, kc, :],
                                                 start=(kc == 0), stop=(kc == 7)),
                 reads=b_wvw + [b_hT], writes=[bps])
        P.op("act", lambda e: e.activation(out=vaug[:, J, 0:64], in_=ps[:, 0:64], func=AF.Copy), reads=[bps], writes=[b_va[J]])
        P.op("act", lambda e: e.activation(out=wabs[:, b, :], in_=ps[:, 64:72], func=AF.Abs, scale=0.125 * (8 ** -0.5)),
             reads=[bps], writes=[b_wabs[b]])
        P.op("act", lambda e: e.activation(out=sgn[:, b, :], in_=ps[:, 64:72], func=AF.Sign), reads=[bps], writes=[b_wabs[b]])

    def stage_A(ti, b):
        J = ti * NB + b
        nkeys = 128 * (J + 1)
        iscJ, biscJ = isc[J % 2], b_isc[J % 2]
        dg, bdg = dsg[J % 2], b_dsg[J % 2]
        for h in range(8):
            P.op("pool", lambda e, h=h: e.tensor_scalar(out=dg[:, h, :], in0=C.ident_b[:, :], scalar1=sgn[:, b, h:h + 1],
                                                        scalar2=None, op0=ALU.mult),
                 reads=[C.b_const, b_wabs[b]], writes=[bdg])
        ngrp = (nkeys + 511) // 512
        acc, bacc = C.psum[3], C.b_psum[3]
        for G in range(ngrp):
            k0 = G * 512
            wd = min(512, nkeys - k0)
            kbufs = [b_ki[i] for i in range(k0 // TT, (k0 + wd - 1) // TT + 1)]
            pend = None
            for h in range(8):
                hp, jj = h % 2, h // 2
                pi = C.mm_rr % 3
                C.mm_rr += 1
                dps, bdps = C.psum[pi], C.b_psum[pi]
                P.op("pe", lambda e, hp=hp, jj=jj, dps=dps, k0=k0, wd=wd: e.matmul(
                    out=dps[:, 0:wd], lhsT=qiT[:, hp, jj, b * 128:(b + 1) * 128], rhs=kiT2[:, k0:k0 + wd], start=True, stop=True),
                    reads=[b_qi[jj], b_qz] + kbufs, writes=[bdps])
                R, bR = r_R.get()
                P.op("act", lambda e, h=h, dps=dps, R=R, wd=wd: e.activation(out=R[:, 0:wd], in_=dps[:, 0:wd], func=AF.Relu,
                                                                             scale=wabs[:, b, h:h + 1]),
                     reads=[bdps, b_wabs[b]], writes=[bR])
                if pend is not None:
                    pend()
                pend = (lambda h=h, R=R, bR=bR, wd=wd: P.op("pe", lambda e: e.matmul(
                    out=acc[:, 0:wd], lhsT=dg[:, h, :], rhs=R[:, 0:wd], start=(h == 0), stop=(h == 7)),
                    reads=[bdg, bR], writes=[bacc]))
            pend()
            P.op("act", lambda e, k0=k0, wd=wd: e.activation(out=iscJ[:, k0:k0 + wd], in_=acc[:, 0:wd], func=AF.Copy),
                 reads=[bacc], writes=[biscJ])

    def stage_B(ti, b):
        J = ti * NB + b
        nkeys = 128 * (J + 1)
        nk = J + 1
        iscJ, biscJ = isc[J % 2], b_isc[J % 2]
        bs, bbs, wt = bis[J % 2], b_bis[J % 2], wtab[J % 2]
        mT, bmT = maskT[J % 2], b_maskT[J % 2]
        if J < 2:
            P.op("dve", lambda e: e.memset(iscJ[0:64, nkeys - 64:nkeys], NEG), writes=[biscJ])
            P.op("dve", lambda e: e.memset(bs[:, 6:7], -1.0e29), writes=[bbs])
            yield
        else:
            P.op("dve", lambda e: e.tensor_reduce(out=bs[:, 0:1], in_=iscJ[:, 0:nkeys], axis=AX.X, op=ALU.min),
                 reads=[biscJ], writes=[bbs])
            P.op("dve", lambda e: e.memset(iscJ[0:64, nkeys - 64:nkeys], NEG), reads=[], writes=[biscJ])
            yield
            P.op("dve", lambda e: e.tensor_reduce(out=bs[:, 1:2], in_=iscJ[:, 0:nkeys], axis=AX.X, op=ALU.max),
                 reads=[biscJ], writes=[bbs])
            P.op("dve", lambda e: e.tensor_tensor(out=bs[:, 2:3], in0=bs[:, 1:2], in1=bs[:, 0:1], op=ALU.subtract),
                 reads=[bbs], writes=[bbs])
            P.op("dve", lambda e: e.tensor_scalar(out=wt[:, :], in0=pow2[:, :], scalar1=bs[:, 2:3], scalar2=None, op0=ALU.mult),
                 reads=[bbs, b_pow2], writes=[bbs])
            P.op("dve", lambda e: e.tensor_tensor(out=bs[:, 3:4], in0=bs[:, 0:1], in1=wt[:, 0:1], op=ALU.add),
                 reads=[bbs], writes=[bbs])
            yield
            for i in range(NIT):
                P.op("dve", lambda e: e.tensor_scalar(out=mask[:, 0:nkeys], in0=iscJ[:, 0:nkeys], scalar1=bs[:, 3:4], scalar2=None,
                                                      op0=ALU.is_ge, op1=ALU.add, accum_out=bs[:, 4:5]),
                     reads=[biscJ, bbs], writes=[b_mask, bbs])
                P.op("dve", lambda e: e.tensor_scalar(out=bs[:, 5:6], in0=bs[:, 4:5], scalar1=TOPK - 0.5, scalar2=0.5,
                                                      op0=ALU.is_ge, op1=ALU.subtract), reads=[bbs], writes=[bbs])
                P.op("dve", lambda e, i=i: e.scalar_tensor_tensor(out=bs[:, 3:4], in0=bs[:, 5:6], scalar=wt[:, i:i + 1],
                                                                   in1=bs[:, 3:4], op0=ALU.mult, op1=ALU.add),
                     reads=[bbs], writes=[bbs])
                yield
            P.op("dve", lambda e: e.tensor_tensor(out=bs[:, 6:7], in0=bs[:, 3:4], in1=wt[:, NIT:NIT + 1], op=ALU.subtract),
                 reads=[bbs], writes=[bbs])
        P.op("dve", lambda e: e.tensor_scalar(out=mask[:, 0:nkeys], in0=iscJ[:, 0:nkeys], scalar1=bs[:, 6:7], scalar2=-1.0,
                                              op0=ALU.is_ge, op1=ALU.add), reads=[biscJ, bbs], writes=[b_mask])
        for g4 in range((nk + 3) // 4):
            n4 = min(4, nk - g4 * 4)
            pi = C.tp_rr % 2
            C.tp_rr += 1
            pst = C.psum_tp[pi]
            for q4 in range(n4):
                kt = g4 * 4 + q4
                P.op("pe", lambda e, kt=kt, q4=q4, pst=pst: e.transpose(out=pst[:, q4 * 128:(q4 + 1) * 128],
                                                                        in_=mask[:, kt * 128:(kt + 1) * 128], identity=C.ident_b[:, :]),
                     reads=[b_mask, C.b_const], writes=[C.b_psum_tp[pi]])
            P.op("act", lambda e, g4=g4, n4=n4, pst=pst: e.activation(
                out=mT[:, g4 * 4:g4 * 4 + n4, :], in_=pst[:, 0:n4 * 128].rearrange("p (a b) -> p a b", a=n4), func=AF.Copy, scale=MBIG),
                reads=[C.b_psum_tp[pi]], writes=[bmT[g4]])
            yield

    def stage_C(ti, b):
        J = ti * NB + b
        nk = J + 1
        mT, bmT = maskT[J % 2], b_maskT[J % 2]
        OA, bOA, OB, bOB = C.psum[4], C.b_psum[4], C.psum[5], C.b_psum[5]
        for kt in range(nk):
            kb = b_kn[(kt * 128) // TT]
            pr = C.lp_rr % 2
            C.lp_rr += 1
            L = C.psall[:, pr * 1024:(pr + 1) * 1024]
            bL = [C.b_psum[2 * pr], C.b_psum[2 * pr + 1]]
            for hp in range(2):
                P.op("pe", lambda e, kt=kt, L=L, hp=hp: e.matmul(
                    out=L[:, hp * 512:(hp + 1) * 512], lhsT=knT2[:, kt * 128:(kt + 1) * 128],
                    rhs=qnT[:, hp, :, b * 128:(b + 1) * 128], start=True, stop=False),
                    reads=[kb, b_qz] + b_qn, writes=[bL[hp]])
                P.op("pe", lambda e, kt=kt, L=L, hp=hp: e.matmul(
                    out=L[:, hp * 512:(hp + 1) * 512], lhsT=C.ident_b[:, :],
                    rhs=mT[:, kt:kt + 1, :].to_broadcast([128, 4, 128]), start=False, stop=True),
                    reads=[bmT[kt // 4], C.b_const], writes=[bL[hp]])
            E, bE = r_E.get()
            P.op("act", lambda e, E=E, L=L: e.activation(out=E[:, :, :], in_=L.rearrange("p (a b) -> p a b", a=8), func=AF.Exp),
                 reads=bL, writes=[bE])
            for e8 in range(8):
                O, bO = (OA, bOA) if e8 < 4 else (OB, bOB)
                c65 = (e8 % 4) * 65
                P.op("pe", lambda e, e8=e8, E=E, kt=kt, O=O, c65=c65: e.matmul(
                    out=O[:, c65:c65 + 65], lhsT=E[:, e8, :], rhs=vaug[:, kt, :],
                    start=(kt == 0 and e8 % 4 == 0), stop=(kt == nk - 1), skip_group_check=True),
                    reads=[bE, b_va[kt], b_vones], writes=[bO])
            yield
        for half, (O, bO) in enumerate(((OA, bOA), (OB, bOB))):
            Ov = O[:, 0:260].rearrange("p (a b) -> p a b", a=4)
            P.op("dve", lambda e, Ov=Ov, half=half: e.reciprocal(out=rc[:, half * 4:half * 4 + 4].rearrange("p (a b) -> p a b", b=1),
                                                                  in_=Ov[:, :, 64:65]), reads=[bO], writes=[b_rc])
            bov = botok[:, :].rearrange("p (j two d) -> p j two d", two=2, d=64)[:, :, half, :]
            P.op("dve", lambda e, Ov=Ov, half=half, bov=bov: e.tensor_tensor(
                out=bov, in0=Ov[:, :, 0:64],
                in1=rc[:, half * 4:half * 4 + 4].rearrange("p (a b) -> p a b", b=1).to_broadcast([128, 4, 64]), op=ALU.mult),
                reads=[bO, b_rc], writes=[b_botok])
        pi = C.tp_rr % 2
        C.tp_rr += 1
        pst = C.psum_tp[pi]
        for jj in range(4):
            P.op("pe", lambda e, jj=jj: e.transpose(out=pst[:, jj * 128:(jj + 1) * 128], in_=botok[:, jj * 128:(jj + 1) * 128],
                                                    identity=C.ident_b[:, :]),
                 reads=[b_botok, C.b_const], writes=[C.b_psum_tp[pi]])
        P.op("act", lambda e: e.activation(out=mix[:, 4:8, b * 128:(b + 1) * 128],
                                           in_=pst[:, 0:512].rearrange("p (a b) -> p a b", a=4), func=AF.Copy),
             reads=[C.b_psum_tp[pi]], writes=[b_mixb[b]])
        yield

    C.lp_rr = 0
    for ti in range(NT):
        P.dma("sp", lambda e, ti=ti: e.dma_start(out=xt[:, :, :], in_=srcv[ti]), s_x, reads=[b_src[ti]], writes=[b_xt])
        if boT is None:
            P.dma("sp", lambda e, ti=ti: e.dma_start(out=mix[:, 0:4, :], in_=aov[:, :, ti * TT:(ti + 1) * TT]), s_a,
                  reads=[b_ao[ti]], writes=[b_mixa])
        norm_transpose(C, xt, b_xt, hT, b_hT, ("even_norm",), NB, scr)
        tc = slice(ti * TT, (ti + 1) * TT)
        ps, bps = proj_fm(wk2, b_wk2, 0)
        qk_norm(ps, bps, 0, [(knT2[:, tc], 0, 128, b_kn[ti])])
        ps, bps = proj_fm(wik2, b_wik2, 0)
        P.op("act", lambda e, ps=ps, tc=tc: e.activation(out=kiT2[:, tc], in_=ps[:, 0:TT], func=AF.Copy), reads=[bps], writes=[b_ki[ti]])
        for jj in range(4):
            ps, bps = proj_fm(wiq, b_wiq, jj * 128)
            for hp in range(2):
                P.op("act", lambda e, ps=ps, jj=jj, hp=hp: e.activation(out=qiT[hp * 64:(hp + 1) * 64, hp, jj, :],
                                                                       in_=ps[hp * 64:(hp + 1) * 64, 0:TT], func=AF.Copy),
                     reads=[bps, b_qz], writes=[b_qi[jj]])
        for b in range(NB):
            tok_proj(ti, b)
        for jj in range(4):
            ps, bps = proj_fm(wq, b_wq, jj * 128)
            qk_norm(ps, bps, 1, [(qnT[0:64, 0, jj, :], 0, 64, b_qn[jj]), (qnT[64:128, 1, jj, :], 64, 128, b_qn[jj])])
        prep_some(C, 8)
        stage_A(ti, 0)
        for b in range(NB):
            if b + 1 < NB:
                stage_A(ti, b + 1)
            J = ti * NB + b
            gens = [(stage_B(ti, b), NIT + 4 + (J + 4) // 4)]
            if b > 0:
                gens.append((stage_C(ti, b - 1), J + 1))
            _interleave(gens)
        _interleave([(stage_C(ti, NB - 1), ti * NB + NB)])
        if boT is not None:
            P.dma("sp", lambda e, ti=ti: e.dma_start(out=boT.rearrange("h p s -> p h s")[:, :, ti * TT:(ti + 1) * TT], in_=mix[:, 4:8, :]),
                  s_bo, reads=b_mixb, writes=[b_dst[ti]])
            continue
        for b in range(NB):
            for hf in range(2):
                pi = 4 + (C.dn_rr % 2)
                C.dn_rr += 1
                ps = C.psum[pi]
                for m in range(8):
                    P.op("pe", lambda e, m=m, ps=ps, b=b, hf=hf: e.matmul(
                        out=ps[:, 0:512], lhsT=mix[:, m, b * 128:(b + 1) * 128], rhs=w_out[:, m, hf * 512:(hf + 1) * 512],
                        start=(m == 0), stop=(m == 7)),
                        reads=[b_mixa, b_mixb[b]] + b_wo, writes=[C.b_psum[pi]])
                P.op("dve", lambda e, ps=ps, b=b, hf=hf: e.tensor_tensor(
                    out=xt[:, b, hf * 512:(hf + 1) * 512], in0=ps[:, 0:512], in1=xt[:, b, hf * 512:(hf + 1) * 512], op=ALU.add),
                    reads=[C.b_psum[pi], b_xt], writes=[b_xt])
        P.dma("sp", lambda e, ti=ti: e.dma_start(out=dstv[ti], in_=xt[:, :, :]), s_o, reads=[b_xt], writes=[b_dst[ti]])


def build(phases=("ffn0",), dbg=None):
    nc = bass.Bass("TRN2", target_bir_lowering=False)
    C = Ctx()
    C.nc = nc
    C.P = P = Prog(nc)
    specs = {
        "x": [S, D], "lb_logits": [3, 512], "even_norm": [1, D], "even_w_in": [1, D, EVEN_IN], "even_w_out": [1, D, D],
        "a_out_norm": [1, 512], "b_q_norm": [1, 64], "b_k_norm": [1, 64], "odd_norm": [1, D],
        "odd_w_in": [1, D, 2 * LRU_W], "odd_conv_w": [1, 4, LRU_W], "odd_conv_b": [1, LRU_W],
        "odd_gate_a_w": [1, 10, 128, 128], "odd_gate_a_b": [1, LRU_W], "odd_gate_x_w": [1, 10, 128, 128],
        "odd_gate_x_b": [1, LRU_W], "odd_lambda": [1, LRU_W], "odd_w_out": [1, LRU_W, D],
        "ffn_norm": [2, D], "ffn_w_up": [2, D, 2 * DFF], "ffn_conv_w": [2, 3, 2 * DFF], "ffn_conv_b": [2, 2 * DFF],
        "ffn_w_down": [2, DFF, D],
        "c_ident_f": [128, 128], "c_causT": [64, 64],
    }
    C.t = {k: nc.dram_tensor(k, v, F32, kind="ExternalInput") for k, v in specs.items()}
    C.t["c_ident_b"] = nc.dram_tensor("c_ident_b", [128, 128], BF16, kind="ExternalInput")
    out = nc.dram_tensor("out", [S, D], F32, kind="ExternalOutput")

    C.persist = Arena.__new__(Arena)
    pt = nc.alloc_sbuf_tensor("persist", [128, 2048], F32)
    C.persist.t, C.persist.nwords, C.persist.off = pt, 2048, 0
    C.arena = Arena(nc, 198 * 1024)
    psall = nc.alloc_psum_tensor("psall", [128, 4096], F32)
    C.psall = psall
    C.psum = [psall[:, i * 512:(i + 1) * 512] for i in range(6)]
    C.b_psum = [Buf("ps%d" % i) for i in range(6)]
    tp = [psall[:, i * 512:(i + 1) * 512] for i in range(6, 8)]
    C.psum_tp = [a.bitcast(BF16) for a in tp]
    C.psum_tp_f = tp
    C.b_psum_tp = [Buf("pstp0"), Buf("pstp1")]
    C.tp_rr = C.mm_rr = C.u_rr = C.dn_rr = 0

    C.ident_f = C.persist.alloc([128], F32)
    C.ident_b = C.persist.alloc([128], BF16)
    C.eps_t = C.persist.alloc([1], F32)
    C.one_t = C.persist.alloc([1], F32)
    C.neghalf = C.persist.alloc([8], F32)
    C.b_const = Buf("const")
    s_c = P.new_dma_sem()
    s_c2 = P.new_dma_sem()
    P.op("dve", lambda e: e.memset(C.eps_t[:, :], EPS), writes=[C.b_const])
    P.op("dve", lambda e: e.memset(C.one_t[:, :], 1.0), writes=[C.b_const])
    P.op("dve", lambda e: e.memset(C.neghalf[:, :], -0.5), writes=[C.b_const])
    b_i1, b_i2 = Buf("i1"), Buf("i2")
    C.b_if, C.b_ib = b_i1, b_i2
    P.dma("sp", lambda e: e.dma_start(out=C.ident_f[:, :], in_=C.t["c_ident_f"][:, :]), s_c, writes=[C.b_const])
    P.dma("sp", lambda e: e.dma_start(out=C.ident_b[:, :], in_=C.t["c_ident_b"][:, :]), s_c2, writes=[C.b_const])

    load_vectors(C)

    xin = C.t["x"]
    b_xin = [Buf("xin%d" % i) for i in range(NT)]
    scratch = {}

    def dram_act(name):
        if name not in scratch:
            scratch[name] = (nc.dram_tensor(name, [S, D], F32, kind="Internal"), [Buf(name + str(i)) for i in range(NT)])
        return scratch[name]

    cur, b_cur = xin, b_xin
    plist = list(phases)
    if dbg and dbg.startswith("aoT"):
        if ":" in dbg:
            C.dbg_sel = dbg.split(":")[1]
        aoT = nc.dram_tensor("dbg", [4, 128, S], BF16, kind="ExternalOutput")
    else:
        aoT = nc.dram_tensor("aoT", [4, 128, S], BF16, kind="Internal")
    b_ao = [Buf("ao%d" % k) for k in range(NT)]
    final_bufs = None
    for i, ph in enumerate(plist):
        last = (i == len(plist) - 1)
        if last:
            dst, b_dst = out, [Buf("out%d" % k) for k in range(NT)]
        else:
            dst, b_dst = dram_act("act%d" % i)
        if ph == "ffn0":
            ffn_phase(C, 0, cur, b_cur, dst, b_dst)
        elif ph == "ffn1":
            ffn_phase(C, 1, cur, b_cur, dst, b_dst)
        elif ph == "odd":
            odd_phase(C, cur, b_cur, dst, b_dst)
        elif ph == "evenB":
            if dbg == "boT":
                boT = nc.dram_tensor("dbg", [4, 128, S], BF16, kind="ExternalOutput")
                b_dst = [Buf("bo%d" % k) for k in range(NT)]
                evenB_phase(C, cur, b_cur, aoT, b_ao, dst, b_dst, boT=boT)
            else:
                evenB_phase(C, cur, b_cur, aoT, b_ao, dst, b_dst)
        elif ph == "evenA":
            evenA_phase(C, cur, b_cur, aoT, b_ao)
            final_bufs = b_ao
            continue
        else:
            raise ValueError(ph)
        cur, b_cur = dst, b_dst
        final_bufs = b_dst
    waits = P._deps("sp", final_bufs, ())
    P.q["sp"].append((None, waits, None))
    P.barrier()
    P.emit()
    return nc


_CONSTS = None


def consts():
    global _CONSTS
    if _CONSTS is None:
        eye = np.eye(128, dtype=np.float32)
        _CONSTS = {"c_ident_f": eye, "c_ident_b": eye.astype(ml_dtypes.bfloat16),
                   "c_causT": np.triu(np.ones((64, 64), dtype=np.float32))}
    return _CONSTS


LAST = {}


def run(inputs, phases, core_ids=tuple(range(8)), trace=False, dbg=None):
    nc = build(phases, dbg)
    shared = {k: np.ascontiguousarray(v, dtype=np.float32) for k, v in inputs.items() if k != "x"}
    shared.update(consts())
    x = np.asarray(inputs["x"], dtype=np.float32)
    in_maps = []
    for c in core_ids:
        m = dict(shared)
        m["x"] = np.ascontiguousarray(x[c])
        in_maps.append(m)
    res = run_bass_kernel_spmd(nc, in_maps, core_ids=list(core_ids), **({"trace": True} if trace else {}))
    LAST["res"] = res
    if dbg:
        return np.stack([r["dbg"] for r in res.results], axis=0)
    return np.stack([r["out"] for r in res.results], axis=0)


PHASES = ("evenA", "evenB", "ffn0", "odd", "ffn1")


def kernel(**inputs):
    return run(inputs, PHASES).astype(np.float32)
```
